# Optimizing a Trainium2 kernel written in Bass

```python
import math
import jax, jax.numpy as jnp
from jax import lax
import numpy as np

D_MODEL = 1024
BATCH = 2
SEQ = 8192
DEPTH = 1

CHUNK = 64
D_MIX = D_MODEL
SB_HEADS = 8
SB_HEAD_DIM = 64
D_SB = SB_HEADS * SB_HEAD_DIM
CONV_GROUPS = 8
D_CONV = D_MIX - D_SB
CONV_WIDTH = 3
D_FF = 2816
Q_BLOCK = 128
EPS = 1e-6
D_IN = 3 * D_SB + 3 * D_CONV

kernel_name = "hybrid_stickbreaking_shortconv_macaron"


def rmsnorm(x, g):
    xf = x.astype(jnp.float32)
    y = xf * lax.rsqrt(jnp.mean(xf * xf, axis=-1, keepdims=True) + EPS)
    return (y * g.astype(jnp.float32)).astype(x.dtype)


def swiglu(h, w_gate, w_up, w_down):
    return (jax.nn.silu(h @ w_gate) * (h @ w_up)) @ w_down


def stick_breaking_attention(q, k, v):
    b, h, s, d = q.shape
    n_blk = s // Q_BLOCK
    scale = 1.0 / math.sqrt(d)
    kf = k.astype(jnp.float32)
    vf = v.astype(jnp.float32)
    key_pos = jnp.arange(s)
    q_blocks = q.reshape(b, h, n_blk, Q_BLOCK, d).transpose(2, 0, 1, 3, 4)

    def one_block(args):
        q_blk, blk = args
        z = jnp.einsum('bhqd,bhkd->bhqk', q_blk.astype(jnp.float32), kf) * scale
        q_pos = blk * Q_BLOCK + jnp.arange(Q_BLOCK)
        strict = key_pos[None, :] < q_pos[:, None]
        log_beta = jax.nn.log_sigmoid(z)
        log_keep = jnp.where(strict, log_beta - z, 0.0)
        later = lax.cumsum(log_keep, axis=3, reverse=True) - log_keep
        a = jnp.where(strict, jnp.exp(log_beta + later), 0.0)
        return jnp.einsum('bhqk,bhkd->bhqd', a, vf)

    out = lax.map(one_block, (q_blocks, jnp.arange(n_blk)))
    return out.transpose(1, 2, 0, 3, 4).reshape(b, h, s, d).astype(q.dtype)


def short_conv_mixer(gate_b, gate_c, u, conv_w, conv_b):
    s = u.shape[1]
    xc = gate_c * u
    pad = jnp.pad(xc, ((0, 0), (CONV_WIDTH - 1, 0), (0, 0)))
    y = conv_b
    for i in range(CONV_WIDTH):
        y = y + pad[:, i:i + s, :] * conv_w[i]
    return gate_b * y


def hybrid_mixer(h, w_in, conv_w, conv_b, sb_out_norm, conv_out_norm, w_out):
    b, s, _ = h.shape
    proj = h @ w_in
    q, k, v, gate_b, gate_c, u = jnp.split(
        proj, [D_SB, 2 * D_SB, 3 * D_SB, 3 * D_SB + D_CONV, 3 * D_SB + 2 * D_CONV], axis=-1)

    def heads(t):
        return t.reshape(b, s, SB_HEADS, SB_HEAD_DIM).transpose(0, 2, 1, 3)

    y_sb = stick_breaking_attention(heads(q), heads(k), heads(v))
    y_sb = y_sb.transpose(0, 2, 1, 3).reshape(b, s, D_SB)
    y_conv = short_conv_mixer(gate_b, gate_c, u, conv_w, conv_b)
    y = jnp.concatenate([rmsnorm(y_sb, sb_out_norm), rmsnorm(y_conv, conv_out_norm)], axis=-1)
    return y @ w_out


def setup_inputs(seed: int = 0) -> dict:
    key = jax.random.key(seed)
    ks = jax.random.split(key, 20)

    def w(k, shape, fan_in):
        return jax.random.normal(k, shape, jnp.float32) * fan_in ** -0.5

    def gain(k, shape):
        return 1.0 + 0.02 * jax.random.normal(k, shape, jnp.float32)

    return {
        "x": jax.random.normal(ks[0], (BATCH, SEQ, D_MODEL), jnp.float32),
        "ffn1_norm": gain(ks[1], (DEPTH, D_MODEL)),
        "ffn1_w_gate": w(ks[2], (DEPTH, D_MODEL, D_FF), D_MODEL),
        "ffn1_w_up": w(ks[3], (DEPTH, D_MODEL, D_FF), D_MODEL),
        "ffn1_w_down": w(ks[4], (DEPTH, D_FF, D_MODEL), D_FF),
        "mix_norm": gain(ks[5], (DEPTH, D_MODEL)),
        "w_in": w(ks[6], (DEPTH, D_MODEL, D_IN), D_MODEL),
        "conv_w": w(ks[7], (DEPTH, CONV_WIDTH, D_CONV), CONV_WIDTH),
        "conv_b": 0.01 * jax.random.normal(ks[8], (DEPTH, D_CONV), jnp.float32),
        "sb_out_norm": gain(ks[9], (DEPTH, D_SB)),
        "conv_out_norm": gain(ks[10], (DEPTH, D_CONV)),
        "w_out": w(ks[11], (DEPTH, D_MIX, D_MODEL), D_MIX),
        "ffn2_norm": gain(ks[12], (DEPTH, D_MODEL)),
        "ffn2_w_gate": w(ks[13], (DEPTH, D_MODEL, D_FF), D_MODEL),
        "ffn2_w_up": w(ks[14], (DEPTH, D_MODEL, D_FF), D_MODEL),
        "ffn2_w_down": w(ks[15], (DEPTH, D_FF, D_MODEL), D_FF),
        "final_norm": gain(ks[16], (D_MODEL,)),
    }


def reference(x, ffn1_norm, ffn1_w_gate, ffn1_w_up, ffn1_w_down, mix_norm, w_in, conv_w, conv_b,
              sb_out_norm, conv_out_norm, w_out, ffn2_norm, ffn2_w_gate, ffn2_w_up, ffn2_w_down,
              final_norm):
    for l in range(DEPTH):
        x = x + 0.5 * swiglu(rmsnorm(x, ffn1_norm[l]), ffn1_w_gate[l], ffn1_w_up[l], ffn1_w_down[l])
        x = x + hybrid_mixer(rmsnorm(x, mix_norm[l]), w_in[l], conv_w[l], conv_b[l],
                             sb_out_norm[l], conv_out_norm[l], w_out[l])
        x = x + 0.5 * swiglu(rmsnorm(x, ffn2_norm[l]), ffn2_w_gate[l], ffn2_w_up[l], ffn2_w_down[l])
    return rmsnorm(x, final_norm)
```

```python
import numpy as np
PAIR_SP = True
PAIR_A = True
KSKIP = set()
from contextlib import ExitStack
import concourse.bass as bass
import concourse.mybir as mybir
from concourse.bass_utils import run_bass_kernel_spmd

F32 = mybir.dt.float32
BF16 = mybir.dt.bfloat16
I32 = mybir.dt.int32
AF = mybir.ActivationFunctionType
ALU = mybir.AluOpType

D = 1024
DFF = 2816
NTOK = 2048
NT = 16
NS = 4
SEQ = 8192
EPS = 1e-6
GROUPS = [[0, 1, 2, 3], [4, 5, 6, 7]]


class Res:
    __slots__ = ("writers", "readers")

    def __init__(self):
        self.writers = []
        self.readers = []


class Sched:
    ENGS = ("pe", "act", "dve", "pool", "sp")

    def __init__(self, nc, n_dma_sems=12):
        self.nc = nc
        self.ops = []
        self.queues = {e: [] for e in self.ENGS}
        self.n_dma_sems = n_dma_sems
        self.epoch = 0
        self.pending_dma = []
        self.last = {e: None for e in self.ENGS}
        self.tag = ""

    def add(self, eng, fn, reads=(), writes=(), deps=(), kind="c", pwrites=()):
        i = len(self.ops)
        if self.tag in KSKIP:
            fn = lambda eng: eng.nop()
            kind = "c"
        dl = set()
        for d in deps:
            if d is not None:
                dl.add(d)
        for r in reads:
            dl.update(r.writers)
        for w in writes:
            dl.update(w.writers)
            dl.update(w.readers)
        for w in pwrites:
            dl.update(w.readers)
        self.ops.append([eng, fn, sorted(dl), kind, self.epoch])
        self.queues[eng].append(i)
        for r in reads:
            r.readers.append(i)
        for w in writes:
            w.writers = [i]
            w.readers = []
        for w in pwrites:
            if w.readers:
                w.writers = []
                w.readers = []
            w.writers.append(i)
        if kind != "c":
            self.pending_dma.append(i)
        else:
            self.last[eng] = i
        return i

    def barrier(self):
        deps = [v for v in self.last.values() if v is not None] + list(self.pending_dma)
        self.pending_dma = []
        for e in self.ENGS:
            self.add(e, lambda eng: eng.nop(), deps=deps)
        self.epoch += 1

    def check(self):
        done = set()
        ptr = {e: 0 for e in self.ENGS}
        total = len(self.ops)
        while len(done) < total:
            progressed = False
            for e in self.ENGS:
                q = self.queues[e]
                while ptr[e] < len(q):
                    i = q[ptr[e]]
                    if all(d in done for d in self.ops[i][2]):
                        done.add(i)
                        ptr[e] += 1
                        progressed = True
                    else:
                        break
            if not progressed:
                stuck = {e: self.queues[e][ptr[e]] for e in self.ENGS if ptr[e] < len(self.queues[e])}
                raise RuntimeError(f"schedule deadlock; stuck ops {stuck}")

    def emit(self, stack):
        nc = self.nc
        self.check()
        needed = set()
        for op in self.ops:
            needed.update(op[2])
        nep = self.epoch + 1
        csem = {(e, ep): stack.enter_context(nc.semaphore(f"c_{e}{ep}"))
                for e in self.ENGS for ep in range(nep)}
        dsem = {e: [stack.enter_context(nc.semaphore(f"d_{e}{k}")) for k in range(self.n_dma_sems)]
                for e in ("sp", "pool")}
        ncc = sum(1 for op in self.ops if op[3] == "cc")
        ccsems = [stack.enter_context(nc.semaphore(f"cc{k}")) for k in range(ncc)]
        sig = {}
        pre_wait = {}
        cccount = 0
        for e in self.ENGS:
            ccount = {}
            dcount = [0] * self.n_dma_sems
            rr = 0
            for i in self.queues[e]:
                _e, _f, _deps, kind, ep = self.ops[i]
                if kind == "d":
                    k = rr % self.n_dma_sems
                    rr += 1
                    s = dsem[e][k]
                    if dcount[k] > 0:
                        pre_wait[i] = (s, dcount[k])
                    dcount[k] += 16
                    sig[i] = (s, dcount[k])
                elif kind == "cc":
                    sig[i] = (ccsems[cccount], 1)
                    cccount += 1
                elif i in needed:
                    ccount[ep] = ccount.get(ep, 0) + 1
                    sig[i] = (csem[(e, ep)], ccount[ep])
        block = stack.enter_context(nc.Block())
        ops = self.ops
        queues = self.queues

        def run_queue(e, eng):
            waited = {}
            for i in queues[e]:
                _e, fn, deps, kind, _ep = ops[i]
                ws = []
                if i in pre_wait:
                    ws.append(pre_wait[i])
                for d in deps:
                    if e == "pe" and ops[d][0] == "pe" and ops[d][3] == "c":
                        continue
                    ws.append(sig[d])
                for (s, v) in ws:
                    key = id(s)
                    if waited.get(key, 0) >= v:
                        continue
                    waited[key] = v
                    eng.wait_ge(s, v)
                ins = fn(eng)
                if i in sig:
                    s, v = sig[i]
                    ins.then_inc(s, 16 if kind == "d" else 1)

        @block.tensor
        def _(eng):
            run_queue("pe", eng)

        @block.scalar
        def _(eng):
            run_queue("act", eng)

        @block.vector
        def _(eng):
            run_queue("dve", eng)

        @block.gpsimd
        def _(eng):
            run_queue("pool", eng)

        @block.sync
        def _(eng):
            run_queue("sp", eng)


def build_program(stage=99):
    nc = bass.Bass("TRN2", target_bir_lowering=False)

    def din(name, shape, dt=F32):
        return nc.dram_tensor(name, list(shape), dt, kind="ExternalInput").ap()

    def dint(name, shape, dt=F32):
        return nc.dram_tensor(name, list(shape), dt, kind="Internal").ap()

    x_d = din("x", [NTOK, D])
    out_d = nc.dram_tensor("out", [NTOK, D], F32, kind="ExternalOutput").ap()
    W = {}
    for f in ("ffn1", "ffn2"):
        W[f + "_norm"] = din(f + "_norm", [1, D])
        W[f + "_w_gate"] = din(f + "_w_gate", [D, DFF])
        W[f + "_w_up"] = din(f + "_w_up", [D, DFF])
        W[f + "_w_down"] = din(f + "_w_down", [DFF, D])
    mixn_d = din("mix_norm", [1, D])
    win_d = din("w_in", [D, 3072])
    convw_d = din("conv_w", [3, 512])
    convb_d = din("conv_b", [1, 512])
    sbn_d = din("sb_out_norm", [1, 512])
    cvn_d = din("conv_out_norm", [1, 512])
    wout_d = din("w_out", [D, D])
    finn_d = din("final_norm", [1, D])
    cid_d = din("cid", [1, 1], I32)
    sel_d = din("sel", [128, 4])

    xs1_d = dint("xs1", [NTOK, D])
    s1a_d = [dint(f"s1a{q}", [4 * 256, 512], BF16) for q in range(4)]
    g1a_d = [dint(f"g1a{q}", [4 * 1024, 512], BF16) for q in range(4)]
    s1v_d = [dint(f"s1v{h}", [4 * 1024, 128], BF16) for h in range(2)]
    g1v_d = [dint(f"g1v{h}", [4 * 4096, 128], BF16) for h in range(2)]
    tin_d = dint("t_in", [128, 8])
    tout_d = dint("t_out", [512, 8])
    s2_d = [dint(f"s2{q}", [128, 2048]) for q in range(4)]
    g2_d = dint("g2", [2048, 2048])

    st = ExitStack()
    with st:
        M = st.enter_context(nc.sbuf_tensor("arena", [128, 50700], F32))
        PS = st.enter_context(nc.psum_tensor("ps", [128, 4096], F32))
        cidreg = st.enter_context(nc.sync.register("cidreg"))
        S = Sched(nc)

        def f32v(off, n):
            return M[:, off:off + n]

        def bfv(off, n):
            return M[:, off:off + n // 2].bitcast(BF16)

        def bank(b):
            return PS[:, b * 512:(b + 1) * 512]

        bankres = [Res() for _ in range(8)]

        OX, OH, OA, OW, OY, OC = 0, 16400, 24592, 32784, 46096, 50192
        x_sb = f32v(OX, 16384).rearrange("p (t d) -> p t d", d=D)
        x_res = [Res() for _ in range(NT)]
        hT = bfv(OH, 16384).rearrange("p (k t) -> p k t", t=NTOK)
        hT_res = [Res() for _ in range(NT)]
        aT = bfv(OA, 6 * 2048).rearrange("p (f t) -> p f t", t=NTOK)
        wgu = [bfv(OW + i * 1024, 2048).rearrange("p (k c) -> p k c", c=256) for i in range(4)]
        wgu_res = [Res() for _ in range(4)]
        wdb = [bfv(OW + 4096 + i * 3072, 6144).rearrange("p (f c) -> p f c", c=D) for i in range(2)]
        wdb_res = [Res() for _ in range(2)]
        xn = [bfv(OW + 10240 + i * 512, 1024) for i in range(2)]
        xn_res = [Res() for _ in range(2)]
        gbc = f32v(OW + 11264, 1024)
        gbc_res = Res()
        sil = [f32v(OW + 12288 + i * 512, 512) for i in range(2)]
        sil_res = [Res() for _ in range(2)]
        ycn = bfv(OY, 8192).rearrange("p (c t) -> p c t", t=NTOK)
        ycn_res = [Res() for _ in range(NS)]
        ident = bfv(OC, 128)
        trineg = bfv(OC + 64, 128)
        onesb = bfv(OC + 128, 128)
        ssq = f32v(OC + 192, 16)
        lnv = f32v(OC + 208, 16)
        rstd = f32v(OC + 224, 16)
        epst = f32v(OC + 240, 1)
        onet = f32v(OC + 241, 1)
        cw = f32v(OC + 244, 12).rearrange("p (i c) -> p i c", c=4)
        cb = f32v(OC + 256, 4)
        gcn = f32v(OC + 260, 4)
        gsn = f32v(OC + 264, 4)
        sel = f32v(OC + 268, 4)
        tails = f32v(OC + 272, 32).rearrange("p (r e) -> p r e", e=8)
        halo = f32v(OC + 304, 8).rearrange("p (c m) -> p c m", m=2)
        const_res = Res()
        stat_res = Res()

        def mk_consts():
            S.add("pool", lambda e: e.memset(ident, 1.0), writes=[const_res])
            S.add("pool", lambda e: e.affine_select(out=ident, in_=ident, pattern=[[-1, 128]],
                                                    compare_op=ALU.is_equal, fill=0.0, base=0,
                                                    channel_multiplier=1), writes=[const_res])
            S.add("pool", lambda e: e.memset(trineg, -1.0), writes=[const_res])
            S.add("pool", lambda e: e.affine_select(out=trineg, in_=trineg, pattern=[[-1, 128]],
                                                    compare_op=ALU.is_ge, fill=0.0, base=0,
                                                    channel_multiplier=1), writes=[const_res])
            S.add("pool", lambda e: e.memset(onesb, 1.0), writes=[const_res])
            S.add("pool", lambda e: e.memset(epst, EPS), writes=[const_res])
            S.add("pool", lambda e: e.memset(onet, 1.0), writes=[const_res])
            for i in range(3):
                S.add("sp", lambda e, i=i: e.dma_start(
                    out=cw[:, i, :], in_=convw_d[i, :].rearrange("(c p) -> p c", p=128),
                    allow_slow_non_contiguous=True), pwrites=[const_res], kind="d")
            for (dst, src) in ((cb, convb_d), (gcn, cvn_d), (gsn, sbn_d)):
                S.add("sp", lambda e, dst=dst, src=src: e.dma_start(
                    out=dst, in_=src[0, :].rearrange("(c p) -> p c", p=128),
                    allow_slow_non_contiguous=True), pwrites=[const_res], kind="d")
            S.add("sp", lambda e: e.dma_start(out=sel, in_=sel_d), pwrites=[const_res], kind="d")

        def load_x(src, tts=range(NT), deps=()):
            ids = []
            for tt in tts:
                ids.append(S.add("sp", lambda e, tt=tt: e.dma_start(out=x_sb[:, tt, :], in_=src[tt * 128:(tt + 1) * 128, :]),
                                 writes=[x_res[tt]], deps=deps, kind="d"))
            return ids

        def tm_stats(junk_bufs, junk_res, tts=range(NT)):
            t0_, t1_ = tts[0], tts[-1] + 1
            for tt in tts:
                S.add("act", lambda e, tt=tt: e.activation(out=junk_bufs[tt % 2], in_=x_sb[:, tt, :], func=AF.Square,
                                                           accum_out=ssq[:, tt:tt + 1]),
                      reads=[x_res[tt]], writes=[junk_res[tt % 2], stat_res])
            S.add("act", lambda e: e.activation(out=lnv[:, t0_:t1_], in_=ssq[:, t0_:t1_], func=AF.Ln, bias=epst,
                                                scale=1.0 / D),
                  reads=[const_res], writes=[stat_res])
            S.add("act", lambda e: e.activation(out=rstd[:, t0_:t1_], in_=lnv[:, t0_:t1_], func=AF.Exp, scale=-0.5),
                  writes=[stat_res])

        def load_gbc(g_d):
            S.add("sp", lambda e: e.dma_start(out=gbc, in_=g_d.partition_broadcast(128)), writes=[gbc_res], kind="d")

        def make_hT(g_d, tts=range(NT)):
            load_gbc(g_d)
            tm_stats(xn, xn_res, tts)
            for tt in tts:
                b = tt % 2
                S.add("dve", lambda e, tt=tt, b=b: e.scalar_tensor_tensor(
                    out=xn[b], in0=x_sb[:, tt, :], scalar=rstd[:, tt:tt + 1], in1=gbc, op0=ALU.mult, op1=ALU.mult),
                    reads=[x_res[tt], stat_res, gbc_res], writes=[xn_res[b]])
                tp = bank(b).bitcast(BF16)
                for kc in range(8):
                    S.add("pe", lambda e, kc=kc, b=b, tp=tp: e.transpose(
                        out=tp[:, kc * 128:(kc + 1) * 128], in_=xn[b][:, kc * 128:(kc + 1) * 128], identity=ident),
                        reads=[xn_res[b], const_res], writes=[bankres[b]])
                S.add("act", lambda e, tt=tt, tp=tp: e.activation(
                    out=hT[:, :, tt * 128:(tt + 1) * 128], in_=tp.rearrange("p (k t) -> p k t", t=128), func=AF.Copy),
                    reads=[bankres[b]], writes=[hT_res[tt]])

        tstat = [Res() for _ in range(NT)]
        junkb = [bfv(OW + 12288, 1024), bfv(OW + 12288, 1024)]
        xn3 = [xn[0], xn[1], bfv(OW + 12288 + 512, 1024)]
        xn3_res = [xn_res[0], xn_res[1], sil_res[1]]

        def tile_stats(tt):
            j = 0
            S.add("act", lambda e: e.activation(out=junkb[j], in_=x_sb[:, tt, :], func=AF.Square,
                                                accum_out=ssq[:, tt:tt + 1]),
                  reads=[x_res[tt]], writes=[sil_res[j], tstat[tt]])
            S.add("act", lambda e: e.activation(out=lnv[:, tt:tt + 1], in_=ssq[:, tt:tt + 1], func=AF.Ln, bias=epst,
                                                scale=1.0 / D),
                  reads=[const_res], writes=[tstat[tt]])
            S.add("act", lambda e: e.activation(out=rstd[:, tt:tt + 1], in_=lnv[:, tt:tt + 1], func=AF.Exp, scale=-0.5),
                  writes=[tstat[tt]])

        def hT_p1(tt):
            tile_stats(tt)
            b3 = tt % 3
            S.add("dve", lambda e: e.scalar_tensor_tensor(
                out=xn3[b3], in0=x_sb[:, tt, :], scalar=rstd[:, tt:tt + 1], in1=gbc, op0=ALU.mult, op1=ALU.mult),
                reads=[x_res[tt], tstat[tt], gbc_res], writes=[xn3_res[b3]])

        def hT_p2(tt, extra_deps=()):
            b = tt % 2
            b3 = tt % 3
            tp = bank(b).bitcast(BF16)
            for kc in range(8):
                S.add("pe", lambda e, kc=kc: e.transpose(
                    out=tp[:, kc * 128:(kc + 1) * 128], in_=xn3[b3][:, kc * 128:(kc + 1) * 128], identity=ident),
                    reads=[xn3_res[b3], const_res], writes=[bankres[b]])
            S.add("act", lambda e: e.activation(
                out=hT[:, :, tt * 128:(tt + 1) * 128], in_=tp.rearrange("p (k t) -> p k t", t=128), func=AF.Copy),
                reads=[bankres[b]], writes=[hT_res[tt]], deps=extra_deps)

        class Lagged:
            def __init__(self, extra_deps=()):
                self.q = []
                self.extra = extra_deps

            def __call__(self, tt):
                if len(self.q) == 2:
                    hT_p2(self.q.pop(0), self.extra)
                hT_p1(tt)
                self.q.append(tt)

            def flush(self):
                while self.q:
                    hT_p2(self.q.pop(0), self.extra)

        ost = [f32v(OY + i * 1024, 1024) for i in range(2)]
        ost_res = [Res(), Res()]
        out_res = Res()

        def final_tile(tt):
            tile_stats(tt)
            b = tt % 2
            S.add("dve", lambda e: e.scalar_tensor_tensor(
                out=ost[b], in0=x_sb[:, tt, :], scalar=rstd[:, tt:tt + 1], in1=gbc, op0=ALU.mult, op1=ALU.mult),
                reads=[x_res[tt], tstat[tt], gbc_res], writes=[ost_res[b]])
            S.add("sp", lambda e: e.dma_start(out=out_d[tt * 128:(tt + 1) * 128, :], in_=ost[b]),
                  reads=[ost_res[b]], pwrites=[out_res], kind="d")

        def ffn(pref, tts=range(NT), hook=None, do_make=True, next_gain=None, post_tile=None):
            wg_d, wu_d, wd_d = W[pref + "_w_gate"], W[pref + "_w_up"], W[pref + "_w_down"]
            if do_make:
                make_hT(W[pref + "_norm"], tts)
            if next_gain is not None:
                load_gbc(next_gain)
            tss = range(tts[0] // 4, tts[-1] // 4 + 1)
            aT_res = [[Res() for _ in range(NS)] for _ in range(6)]
            quarters = [(0, 6), (6, 6), (12, 6), (18, 4)]
            blk = 0
            mmi = 0
            for qi, (f0, nf) in enumerate(quarters):
                wq = qi % 2
                S.add("pool", lambda e, wq=wq, f0=f0, nf=nf: e.dma_start(
                    out=wdb[wq][:, 0:nf, :], in_=wd_d[f0 * 128:(f0 + nf) * 128, :].rearrange("(f p) c -> p f c", p=128)),
                    writes=[wdb_res[wq]], kind="d")
                for pi in range(nf // 2):
                    c0 = (f0 + 2 * pi) * 128
                    bg, bu = (2 * blk) % 4, (2 * blk + 1) % 4
                    blk += 1
                    for (bb, wsrc) in ((bg, wg_d), (bu, wu_d)):
                        S.add("pool", lambda e, bb=bb, wsrc=wsrc, c0=c0: e.dma_start(
                            out=wgu[bb], in_=wsrc[:, c0:c0 + 256].rearrange("(k p) c -> p k c", p=128)),
                            writes=[wgu_res[bb]], kind="d")
                    if hook and (blk % 2 == 0) and (blk // 2 - 1) < len(hook) and blk >= 2:
                        hook[blk // 2 - 1]()
                    for fi in range(2):
                        fl = 2 * pi + fi
                        for ts in tss:
                            gb, ub = 2 + (mmi % 2), 4 + (mmi % 2)
                            sb_ = mmi % 2
                            mmi += 1
                            for (pb, wb) in ((gb, bg), (ub, bu)):
                                for kc in range(8):
                                    S.add("pe", lambda e, pb=pb, wb=wb, kc=kc, fi=fi, ts=ts: e.matmul(
                                        bank(pb), lhsT=wgu[wb][:, kc, fi * 128:(fi + 1) * 128],
                                        rhs=hT[:, kc, ts * 512:(ts + 1) * 512], start=(kc == 0), stop=(kc == 7)),
                                        reads=[wgu_res[wb]] + hT_res[4 * ts:4 * ts + 4], writes=[bankres[pb]])
                            S.add("act", lambda e, gb=gb, sb_=sb_: e.activation(out=sil[sb_], in_=bank(gb), func=AF.Silu),
                                  reads=[bankres[gb]], writes=[sil_res[sb_]])
                            S.add("dve", lambda e, ub=ub, sb_=sb_, fl=fl, ts=ts: e.tensor_tensor(
                                out=aT[:, fl, ts * 512:(ts + 1) * 512], in0=bank(ub), in1=sil[sb_], op=ALU.mult),
                                reads=[bankres[ub], sil_res[sb_]], writes=[aT_res[fl][ts]])
                for tt in tts:
                    for nh in range(2):
                        db = 6 + ((tt * 2 + nh) % 2)
                        for fl in range(nf):
                            S.add("pe", lambda e, db=db, fl=fl, tt=tt, nh=nh, wq=wq, nf=nf: e.matmul(
                                bank(db), lhsT=aT[:, fl, tt * 128:(tt + 1) * 128],
                                rhs=wdb[wq][:, fl, nh * 512:(nh + 1) * 512], start=(fl == 0), stop=(fl == nf - 1)),
                                reads=[aT_res[fl][tt // 4], wdb_res[wq]], writes=[bankres[db]])
                        S.add("dve", lambda e, db=db, tt=tt, nh=nh: e.scalar_tensor_tensor(
                            out=x_sb[:, tt, nh * 512:(nh + 1) * 512], in0=bank(db), scalar=0.5,
                            in1=x_sb[:, tt, nh * 512:(nh + 1) * 512], op0=ALU.mult, op1=ALU.add),
                            reads=[bankres[db]], writes=[x_res[tt]])
                    if post_tile is not None and qi == len(quarters) - 1:
                        post_tile(tt)
                if post_tile is not None and qi == len(quarters) - 1 and hasattr(post_tile, "flush"):
                    post_tile.flush()

        vals = {}

        def rl(e):
            ins = e.reg_load(cidreg, cid_d[0:1, 0:1])
            vals["v"] = e.snap(cidreg)
            return ins
        S.add("sp", rl)
        mk_consts()
        HALVES = [range(0, 8), range(8, 16)]
        ida_ = load_x(x_d, HALVES[0])
        load_x(x_d, HALVES[1], deps=ida_)

        Bsb = f32v(OX, 8192).rearrange("p (c t) -> p c t", t=NTOK)
        xc = f32v(OX + 8192, 8200).rearrange("p (c t) -> p c t", t=NTOK + 2)
        bx_res = Res()
        stq = [bfv(OY + i * 1024, 2048) for i in range(2)]
        stq_res = [Res(), Res()]
        stv = [bfv(OY + 2048 + i * 128, 256) for i in range(2)]
        stv_res = [Res(), Res()]
        s1a_res, s1v_res, tin_res = Res(), Res(), Res()
        g1a_res, g1v_res, tout_res = Res(), Res(), Res()
        xs1_res = Res()

        def allgather(src, dst, rres, wres):
            S.add("pool", lambda e: e.collective_compute("AllGather", ALU.bypass, replica_groups=GROUPS,
                                                         ins=[src], outs=[dst]),
                  reads=[rres], pwrites=[wres], kind="cc")
        pj = {"mmi": 0, "sqi": 0, "svi": 0, "wbi": 0}

        def proj_blocks(blks, tts, hook=None, hook_after=2, drain=None):
            tss = range(tts[0] // 4, tts[-1] // 4 + 1)
            for bi_, blkI in enumerate(blks):
                if drain and bi_ > 0:
                    for _ in range(4):
                        if drain:
                            drain.pop(0)()
                wb = pj["wbi"] % 4
                pj["wbi"] += 1
                S.add("pool", lambda e, wb=wb, blkI=blkI: e.dma_start(
                    out=wgu[wb], in_=win_d[:, blkI * 256:(blkI + 1) * 256].rearrange("(k p) c -> p k c", p=128)),
                    writes=[wgu_res[wb]], kind="d")
                if hook and bi_ == 3:
                    hook[0]()
                if hook and bi_ == len(blks) - 1:
                    for hk_ in hook[1:]:
                        hk_()
                kindb = blkI // 2
                if kindb == 2:
                    for tt in tts:
                        pb = 2 + (pj["mmi"] % 4)
                        pj["mmi"] += 1
                        for kc in range(8):
                            S.add("pe", lambda e, pb=pb, wb=wb, kc=kc, tt=tt: e.matmul(
                                bank(pb)[:, 0:256], lhsT=hT[:, kc, tt * 128:(tt + 1) * 128], rhs=wgu[wb][:, kc, :],
                                start=(kc == 0), stop=(kc == 7)),
                                reads=[wgu_res[wb], hT_res[tt]], writes=[bankres[pb]])
                        sv = pj["svi"] % 2
                        pj["svi"] += 1
                        S.add("act", lambda e, pb=pb, sv=sv: e.activation(out=stv[sv], in_=bank(pb)[:, 0:256], func=AF.Copy),
                              reads=[bankres[pb]], writes=[stv_res[sv]])
                        for hh in range(2):
                            hp = (blkI - 4) * 2 + hh
                            S.add("sp", lambda e, sv=sv, hh=hh, hp=hp, tt=tt: e.dma_start(
                                out=s1v_d[tt // 8][hp * 1024:(hp + 1) * 1024, :].rearrange("(p b) f -> p b f", b=8)[:, tt % 8, :],
                                in_=stv[sv][:, hh * 128:(hh + 1) * 128]),
                                reads=[stv_res[sv]], pwrites=[s1v_res], kind="d")
                    continue
                for ci in range(2):
                    chunk = (blkI % 2) * 2 + ci
                    sq = pj["sqi"] % 2
                    if kindb < 2:
                        pj["sqi"] += 1
                    for ts in tss:
                        pb = 2 + (pj["mmi"] % 4)
                        pj["mmi"] += 1
                        for kc in range(8):
                            S.add("pe", lambda e, pb=pb, wb=wb, kc=kc, ci=ci, ts=ts: e.matmul(
                                bank(pb), lhsT=wgu[wb][:, kc, ci * 128:(ci + 1) * 128],
                                rhs=hT[:, kc, ts * 512:(ts + 1) * 512], start=(kc == 0), stop=(kc == 7)),
                                reads=[wgu_res[wb]] + hT_res[4 * ts:4 * ts + 4], writes=[bankres[pb]])
                        tsl = slice(ts * 512, (ts + 1) * 512)
                        if kindb == 0:
                            S.add("act", lambda e, pb=pb, sq=sq, tsl=tsl: e.activation(
                                out=stq[sq][:, tsl], in_=bank(pb), func=AF.Copy, scale=0.125),
                                reads=[bankres[pb]], writes=[stq_res[sq]])
                        elif kindb == 1:
                            S.add("act", lambda e, pb=pb, sq=sq, tsl=tsl: e.activation(
                                out=stq[sq][:, tsl], in_=bank(pb), func=AF.Copy),
                                reads=[bankres[pb]], writes=[stq_res[sq]])
                        elif kindb == 3:
                            S.add("act", lambda e, pb=pb, chunk=chunk, tsl=tsl: e.activation(
                                out=Bsb[:, chunk, tsl], in_=bank(pb), func=AF.Copy),
                                reads=[bankres[pb], xs1_res], pwrites=[bx_res])
                        elif kindb == 4:
                            S.add("act", lambda e, pb=pb, chunk=chunk, ts=ts: e.activation(
                                out=xc[:, chunk, 2 + ts * 512: 2 + (ts + 1) * 512], in_=bank(pb), func=AF.Copy),
                                reads=[bankres[pb], xs1_res], pwrites=[bx_res])
                        else:
                            S.add("dve", lambda e, pb=pb, chunk=chunk, ts=ts: e.tensor_tensor(
                                out=xc[:, chunk, 2 + ts * 512: 2 + (ts + 1) * 512], in0=bank(pb),
                                in1=xc[:, chunk, 2 + ts * 512: 2 + (ts + 1) * 512], op=ALU.mult),
                                reads=[bankres[pb], bx_res], pwrites=[bx_res])
                    if kindb < 2:
                        r0 = chunk * 256 + kindb * 128
                        for q4 in tss:
                            S.add("sp", lambda e, sq=sq, r0=r0, q4=q4: e.dma_start(
                                out=s1a_d[q4][r0:r0 + 128, :], in_=stq[sq][:, q4 * 512:(q4 + 1) * 512]),
                                reads=[stq_res[sq]], pwrites=[s1a_res], kind="d")

        def collectives_for(hi):
            return [lambda: allgather(s1a_d[2 * hi], g1a_d[2 * hi], s1a_res, g1a_res),
                    lambda: allgather(s1v_d[hi], g1v_d[hi], s1v_res, g1v_res),
                    lambda: allgather(s1a_d[2 * hi + 1], g1a_d[2 * hi + 1], s1a_res, g1a_res)]

        pend = [None]
        deferred = []
        for hi, tts in enumerate(HALVES):
            hk = pend[0]
            pend[0] = None
            ffn("ffn1", tts, hook=hk, do_make=(hi == 0), next_gain=mixn_d,
                post_tile=Lagged() if stage >= 2 else None)
            if stage >= 2:
                S.tag = 'proj'
                for tt in tts:
                    S.add("sp", lambda e, tt=tt: e.dma_start(out=xs1_d[tt * 128:(tt + 1) * 128, :], in_=x_sb[:, tt, :]),
                          reads=[x_res[tt]], pwrites=[xs1_res], kind="d")
                if hi == 0:
                    load_gbc(W["ffn1_norm"])
                    tb_ = list(HALVES[1])
                    deferred.append(lambda: hT_p1(tb_[0]))
                    deferred.append(lambda: hT_p1(tb_[1]))
                    for k_ in range(2, len(tb_)):
                        deferred.append(lambda k_=k_: hT_p2(tb_[k_ - 2]))
                        deferred.append(lambda k_=k_: hT_p1(tb_[k_]))
                    deferred.append(lambda: hT_p2(tb_[-2]))
                    deferred.append(lambda: hT_p2(tb_[-1]))
                proj_blocks(range(0, 6), tts, drain=deferred)
                while deferred:
                    deferred.pop(0)()
                pend[0] = collectives_for(hi)
                S.tag = ''
            elif hi == 0:
                make_hT(W["ffn1_norm"], HALVES[1])

        if stage >= 2:
            S.tag = 'proj'
            proj_blocks(range(6, 12), range(NT), hook=pend[0], hook_after=2)
            S.add("sp", lambda e: e.dma_start(out=tin_d.rearrange("p (c m) -> p c m", m=2), in_=xc[:, :, NTOK:NTOK + 2]),
                  reads=[bx_res], writes=[tin_res], kind="d")
            allgather(tin_d, tout_d, tin_res, tout_res)

            S.tag = 'attload'
            KQ = bfv(OH, 16384).rearrange("p (w c) -> p w c", w=2)
            Qp = KQ[:, 0, :]
            KT = KQ[:, 1, :]
            Vs = bfv(OA, 8192).rearrange("p (b f) -> p b f", f=128)
            QA = bfv(OA + 4096, 8192)
            QB = bfv(OA + 8192, 8192)
            att_in = Res()
            kq_res = [Res() for _ in range(4)]
            ht_dead = sorted(set(o for r_ in hT_res for o in (r_.readers + r_.writers)))
            S.add("pool", lambda e: e.memset(QA[64:128, :], 0.0), pwrites=[att_in])
            S.add("pool", lambda e: e.memset(QB[0:64, :], 0.0), pwrites=[att_in])
            for i in range(4):
                cs = slice(i * 2048, (i + 1) * 2048)
                for q4 in range(4):
                    S.add("sp", lambda e, i=i, q4=q4: e.dma_start(
                        out=KQ[:, :, i * 2048 + q4 * 512: i * 2048 + (q4 + 1) * 512],
                        in_=g1a_d[q4][bass.ds(vals["v"] * 256 + i * 1024, 256), :].rearrange("(w p) c -> p w c", p=128)),
                        reads=[g1a_res], pwrites=[kq_res[i]], deps=ht_dead, kind="d")
                for h2 in range(2):
                    S.add("sp", lambda e, i=i, h2=h2: e.dma_start(
                        out=Vs[:, i * 16 + h2 * 8: i * 16 + h2 * 8 + 8, :],
                        in_=g1v_d[h2][bass.ds(vals["v"] * 1024 + i * 4096, 1024), :].rearrange("(p b) f -> p b f", b=8)),
                        reads=[g1v_res], pwrites=[att_in], kind="d")
                S.add("sp", lambda e, cs=cs: e.dma_start(out=QA[0:64, cs], in_=Qp[0:64, cs]),
                      reads=[kq_res[i]], pwrites=[att_in], kind="d")
                S.add("sp", lambda e, cs=cs: e.dma_start(out=QB[64:128, cs], in_=Qp[64:128, cs]),
                      reads=[kq_res[i]], pwrites=[att_in], kind="d")
            tails_res = Res()
            S.add("sp", lambda e: e.dma_start(out=tails, in_=tout_d.rearrange("(r p) e -> p r e", p=128)),
                  reads=[tout_res], writes=[tails_res], kind="d")

            S.tag = 'conv'
            accb = [f32v(OW + 4096 + i * 512, 512) for i in range(2)]
            acc_res = [Res(), Res()]
            sqb = [bfv(OW + 4096 + 1024 + i * 256, 512) for i in range(2)]
            sqb_res = [Res(), Res()]
            lnb = f32v(OW + 4096 + 1536, 512)
            rsb = f32v(OW + 4096 + 2048, 512)
            lnb_res, rsb_res = Res(), Res()
            stage_dmas = sorted(set(s1a_res.writers + s1v_res.writers))
            haloflat = f32v(OC + 304, 8)
            halo_res = Res()

            def halo_ops():
                S.add("dve", lambda e: e.tensor_scalar(out=haloflat, in0=tails[:, 0, :], scalar1=sel[:, 0:1], scalar2=None,
                                                       op0=ALU.mult), reads=[const_res, tails_res], writes=[stat_res])
                for r in range(1, 4):
                    S.add("dve", lambda e, r=r: e.scalar_tensor_tensor(
                        out=haloflat, in0=tails[:, r, :], scalar=sel[:, r:r + 1], in1=haloflat, op0=ALU.mult, op1=ALU.add),
                        reads=[const_res, tails_res], writes=[stat_res])
                S.add("dve", lambda e: e.tensor_copy(xc[:, :, 0:2], halo), reads=[stat_res], writes=[halo_res])
            yc_res = [[Res() for _ in range(NS)] for _ in range(4)]
            ci_ = 0
            for ts in (1, 2, 3, 0):
                if ts == 0:
                    halo_ops()
                for cc in range(4):
                    a = ci_ % 2
                    ci_ += 1
                    t0 = ts * 512
                    S.add("dve", lambda e, a=a, cc=cc, t0=t0: e.tensor_scalar(
                        out=accb[a], in0=xc[:, cc, t0 + 2:t0 + 514], scalar1=cw[:, 2, cc:cc + 1],
                        scalar2=cb[:, cc:cc + 1], op0=ALU.mult, op1=ALU.add),
                        reads=[bx_res, const_res], writes=[acc_res[a]])
                    S.add("dve", lambda e, a=a, cc=cc, t0=t0: e.scalar_tensor_tensor(
                        out=accb[a], in0=xc[:, cc, t0 + 1:t0 + 513], scalar=cw[:, 1, cc:cc + 1], in1=accb[a],
                        op0=ALU.mult, op1=ALU.add), reads=[bx_res, halo_res], writes=[acc_res[a]])
                    S.add("dve", lambda e, a=a, cc=cc, t0=t0: e.scalar_tensor_tensor(
                        out=accb[a], in0=xc[:, cc, t0:t0 + 512], scalar=cw[:, 0, cc:cc + 1], in1=accb[a],
                        op0=ALU.mult, op1=ALU.add), reads=[bx_res, halo_res], writes=[acc_res[a]])
                    S.add("pool", lambda e, a=a, cc=cc, t0=t0: e.tensor_tensor(
                        out=Bsb[:, cc, t0:t0 + 512], in0=Bsb[:, cc, t0:t0 + 512], in1=accb[a], op=ALU.mult),
                        reads=[acc_res[a], bx_res], writes=[yc_res[cc][ts]])

            def fm_norm(src, src_res, gain, dst, dst_res, nb):
                k = 0
                for ts in range(NS):
                    tsl = slice(ts * 512, (ts + 1) * 512)
                    for cc in range(4):
                        q = k % 2
                        k += 1
                        S.add("act", lambda e, q=q, cc=cc, tsl=tsl: e.activation(out=sqb[q], in_=src[:, cc, tsl], func=AF.Square),
                              reads=[src_res[cc][ts]], writes=[sqb_res[q]])
                        S.add("pe", lambda e, q=q, cc=cc: e.matmul(bank(nb), lhsT=onesb, rhs=sqb[q],
                                                                   start=(cc == 0), stop=(cc == 3)),
                              reads=[sqb_res[q], const_res], writes=[bankres[nb]])
                    S.add("act", lambda e: e.activation(out=lnb, in_=bank(nb), func=AF.Ln, bias=epst, scale=1.0 / 512),
                          reads=[bankres[nb], const_res], writes=[lnb_res])
                    S.add("act", lambda e: e.activation(out=rsb, in_=lnb, func=AF.Exp, scale=-0.5),
                          reads=[lnb_res], writes=[rsb_res])
                    for cc in range(4):
                        S.add("dve", lambda e, cc=cc, tsl=tsl: e.scalar_tensor_tensor(
                            out=dst[:, cc, tsl], in0=src[:, cc, tsl], scalar=gain[:, cc:cc + 1], in1=rsb,
                            op0=ALU.mult, op1=ALU.mult),
                            reads=[src_res[cc][ts], rsb_res, const_res], writes=[dst_res[ts]])

            S.add("dve", lambda e: e.nop(), deps=stage_dmas)
            fm_norm(Bsb, yc_res, gcn, ycn, ycn_res, 6)
            S.barrier()

        if stage >= 3:
            S.tag = 'att'
            load_x(xs1_d)
            OB = OW + 4096
            Ebuf = [f32v(OB + i * 1024, 1024).rearrange("p (h t) -> p h t", h=2) for i in range(2)]
            Tbuf = [f32v(OB + 2048 + i * 1024, 1024).rearrange("p (h t) -> p h t", h=2) for i in range(3)]
            SPb = [bfv(OB + 5120 + i * 512, 1024).rearrange("p (h t) -> p h t", h=2) for i in range(2)]
            Abf = [bfv(OB + 6144 + i * 512, 1024).rearrange("p (h t) -> p h t", h=2) for i in range(2)]
            carry = [f32v(OB + 7168 + i * 512, 512) for i in range(2)]
            ybuf = [f32v(OB + 8192 + i * 512, 512) for i in range(2)]
            E_res = [Res(), Res()]
            T_res = [Res(), Res(), Res()]
            SP_res = [Res(), Res()]
            A_res = [Res(), Res()]
            carry_res = [Res(), Res()]
            ybuf_res = [Res(), Res()]
            s2_res = [Res() for _ in range(4)]
            g2_res = Res()
            Qh = [QA, QB]
            NQS = SEQ // 512
            pairs = []
            for qs in range(NQS):
                nkb = 4 * qs + 4
                for r in range(nkb):
                    pairs.append((qs, r, nkb))
            NP = len(pairs)

            def need_mask(qs, kb):
                return kb >= 4 * qs

            def mask_pair(buf, res, qs, kb):
                base = 512 * qs - 128 * kb
                for h in range(2):
                    S.add("pool", lambda e, h=h: e.affine_select(out=buf[:, h, :], in_=buf[:, h, :], pattern=[[1, 512]],
                                                                 compare_op=ALU.is_gt, fill=0.0, base=base,
                                                                 channel_multiplier=-1),
                          writes=[res])

            def st_Z(p):
                qs, r, nkb = pairs[p]
                kb = nkb - 1 - r
                for h in range(2):
                    S.add("pe", lambda e, h=h: e.matmul(bank(h), lhsT=KT[:, kb * 128:(kb + 1) * 128],
                                                        rhs=Qh[h][:, qs * 512:(qs + 1) * 512], start=True, stop=True),
                          reads=[att_in], writes=[bankres[h]])

            def st_E(p):
                pp = p % 2
                for h in range(2):
                    S.add("act", lambda e, h=h: e.activation(out=Ebuf[pp][:, h, :], in_=bank(h), func=AF.Exp),
                          reads=[bankres[h]], pwrites=[E_res[pp]])

            def st_SP(p):
                qs, r, nkb = pairs[p]
                kb = nkb - 1 - r
                pp = p % 2
                if PAIR_SP:
                    S.add("act", lambda e: e.activation(out=SPb[pp], in_=Ebuf[pp], func=AF.Ln, bias=onet),
                          reads=[E_res[pp], const_res], writes=[SP_res[pp]])
                else:
                    for h in range(2):
                        S.add("act", lambda e, h=h: e.activation(out=SPb[pp][:, h, :], in_=Ebuf[pp][:, h, :], func=AF.Ln, bias=onet),
                              reads=[E_res[pp], const_res], writes=[SP_res[pp]])
                if need_mask(qs, kb):
                    mask_pair(SPb[pp], SP_res[pp], qs, kb)

            def st_LC(p):
                qs, r, nkb = pairs[p]
                kb = nkb - 1 - r
                pp = p % 2
                for h in range(2):
                    lb, cbk = 2 + h, 4 + h
                    S.add("pe", lambda e, h=h, lb=lb: e.matmul(bank(lb), lhsT=KT[:, kb * 128:(kb + 1) * 128],
                                                               rhs=Qh[h][:, qs * 512:(qs + 1) * 512], start=True, stop=False),
                          reads=[att_in], writes=[bankres[lb]])
                    S.add("pe", lambda e, h=h, lb=lb: e.matmul(bank(lb), lhsT=trineg, rhs=SPb[pp][:, h, :],
                                                               start=False, stop=True),
                          reads=[SP_res[pp], const_res], writes=[bankres[lb]])
                    if r < nkb - 1:
                        S.add("pe", lambda e, h=h, cbk=cbk: e.matmul(bank(cbk), lhsT=onesb, rhs=SPb[pp][:, h, :],
                                                                     start=True, stop=True),
                              reads=[SP_res[pp], const_res], writes=[bankres[cbk]])

            def st_T(p):
                qs, r, nkb = pairs[p]
                pt = p % 3
                for h in range(2):
                    lb, cbk = 2 + h, 4 + h
                    if r == 0:
                        S.add("dve", lambda e, h=h, lb=lb: e.tensor_copy(Tbuf[pt][:, h, :], bank(lb)),
                              reads=[bankres[lb]], pwrites=[T_res[pt]])
                        if r < nkb - 1:
                            S.add("dve", lambda e, h=h, cbk=cbk: e.tensor_copy(carry[h], bank(cbk)),
                                  reads=[bankres[cbk]], writes=[carry_res[h]])
                    else:
                        S.add("dve", lambda e, h=h, lb=lb: e.tensor_tensor(out=Tbuf[pt][:, h, :], in0=bank(lb),
                                                                           in1=carry[h], op=ALU.subtract),
                              reads=[bankres[lb], carry_res[h]], pwrites=[T_res[pt]])
                        if r < nkb - 1:
                            S.add("dve", lambda e, h=h, cbk=cbk: e.tensor_tensor(out=carry[h], in0=bank(cbk),
                                                                               in1=carry[h], op=ALU.add),
                                  reads=[bankres[cbk]], writes=[carry_res[h]])

            def st_A(p):
                qs, r, nkb = pairs[p]
                kb = nkb - 1 - r
                pt = p % 3
                pp = p % 2
                if PAIR_A:
                    S.add("act", lambda e: e.activation(out=Abf[pp], in_=Tbuf[pt], func=AF.Exp),
                          reads=[T_res[pt]], writes=[A_res[pp]])
                else:
                    for h in range(2):
                        S.add("act", lambda e, h=h: e.activation(out=Abf[pp][:, h, :], in_=Tbuf[pt][:, h, :], func=AF.Exp),
                              reads=[T_res[pt]], writes=[A_res[pp]])
                if need_mask(qs, kb):
                    mask_pair(Abf[pp], A_res[pp], qs, kb)

            def st_AV(p):
                qs, r, nkb = pairs[p]
                kb = nkb - 1 - r
                pp = p % 2
                for h in range(2):
                    ob = 6 + h
                    S.add("pe", lambda e, h=h, ob=ob: e.matmul(bank(ob), lhsT=Vs[:, kb, :], rhs=Abf[pp][:, h, :],
                                                               start=(r == 0), stop=(r == nkb - 1)),
                          reads=[A_res[pp], att_in], writes=[bankres[ob]])
                    if r == nkb - 1:
                        yb = qs % 2
                        ps_ = slice(64 * h, 64 * h + 64)
                        S.add("dve", lambda e, ob=ob, ps_=ps_, yb=yb: e.tensor_copy(ybuf[yb][ps_, :], bank(ob)[ps_, :]),
                              reads=[bankres[ob]], writes=[ybuf_res[yb]])
                        if h == 1:
                            dst = qs // 4
                            c0 = (qs % 4) * 512
                            S.add("sp", lambda e, dst=dst, c0=c0, yb=yb: e.dma_start(out=s2_d[dst][:, c0:c0 + 512], in_=ybuf[yb]),
                                  reads=[ybuf_res[yb]], pwrites=[s2_res[dst]], kind="d")
                            if qs % 4 == 3:
                                allgather(s2_d[dst], g2_d[dst * 512:(dst + 1) * 512, :], s2_res[dst], g2_res)

            st_Z(0)
            for step in range(NP + 4):
                if step < NP:
                    st_E(step)
                if 0 <= step - 1 < NP:
                    st_LC(step - 1)
                if step + 1 < NP:
                    st_Z(step + 1)
                if 0 <= step - 3 < NP:
                    st_AV(step - 3)
                if 0 <= step - 2 < NP:
                    st_A(step - 2)
                if step < NP:
                    st_SP(step)
                if 0 <= step - 1 < NP:
                    st_T(step - 1)
            S.barrier()

        if stage >= 4:
            S.tag = 'outproj'
            ysb = f32v(OH, 8192).rearrange("p (c t) -> p c t", t=NTOK)
            ysb_res = [[Res() for _ in range(NS)] for _ in range(4)]
            S.add("sp", lambda e: e.dma_start(
                out=ysb, in_=g2_d[bass.ds(vals["v"] * 512, 512), :].rearrange("(i p) c -> p i c", p=128)),
                reads=[g2_res], writes=[r_ for i in range(4) for r_ in ysb_res[i]], kind="d")
            ysn = bfv(OA, 8192).rearrange("p (c t) -> p c t", t=NTOK)
            ysn_res = [Res() for _ in range(NS)]
            wo = bfv(OA + 4096, 8192).rearrange("p (k c) -> p k c", c=D)
            wo_res = Res()
            S.add("pool", lambda e: e.dma_start(out=wo, in_=wout_d.rearrange("(k p) c -> p k c", p=128)),
                  writes=[wo_res], kind="d")
            fm_norm(ysb, ysb_res, gsn, ysn, ysn_res, 6)
            ysb_readers = sorted(set(o for row in ysb_res for r_ in row for o in (r_.readers + r_.writers)))
            load_gbc(W["ffn2_norm"])
            lag4 = Lagged(ysb_readers)
            mmi = 0
            for tt in range(NT):
                for nh in range(2):
                    pb = 2 + (mmi % 4)
                    mmi += 1
                    for kc in range(8):
                        src, sres = (ysn, ysn_res) if kc < 4 else (ycn, ycn_res)
                        S.add("pe", lambda e, pb=pb, kc=kc, tt=tt, nh=nh, src=src: e.matmul(
                            bank(pb), lhsT=src[:, kc % 4, tt * 128:(tt + 1) * 128], rhs=wo[:, kc, nh * 512:(nh + 1) * 512],
                            start=(kc == 0), stop=(kc == 7)),
                            reads=[sres[tt // 4], wo_res], writes=[bankres[pb]])
                    S.add("dve", lambda e, pb=pb, tt=tt, nh=nh: e.tensor_tensor(
                        out=x_sb[:, tt, nh * 512:(nh + 1) * 512], in0=bank(pb), in1=x_sb[:, tt, nh * 512:(nh + 1) * 512],
                        op=ALU.add), reads=[bankres[pb]], writes=[x_res[tt]])
                if stage >= 5:
                    lag4(tt)
            lag4.flush()
            S.barrier()

        if stage >= 5:
            S.tag = 'ffn2'
            ffn("ffn2", do_make=False, next_gain=finn_d, post_tile=final_tile)
            S.barrier()

        S.tag = 'final'
        if stage < 5:
            if stage == 3:
                load_x(xs1_d)
            for tt in range(NT):
                S.add("sp", lambda e, tt=tt: e.dma_start(out=out_d[tt * 128:(tt + 1) * 128, :], in_=x_sb[:, tt, :]),
                      reads=[x_res[tt]], pwrites=[out_res], kind="d")
        S.barrier()
        S.emit(st)
    return nc


def _prep_inputs(inputs):
    f = lambda a: np.ascontiguousarray(np.asarray(a, dtype=np.float32))
    x = f(inputs["x"])
    shared = {}
    for k in ("ffn1_w_gate", "ffn1_w_up", "ffn1_w_down", "ffn2_w_gate", "ffn2_w_up", "ffn2_w_down", "w_in", "w_out",
              "conv_w"):
        shared[k] = f(inputs[k])[0]
    for k in ("ffn1_norm", "ffn2_norm", "mix_norm", "conv_b", "sb_out_norm", "conv_out_norm"):
        shared[k] = f(inputs[k]).reshape(1, -1)
    shared["final_norm"] = f(inputs["final_norm"]).reshape(1, -1)
    in_maps = []
    for c in range(8):
        b, j = c // 4, c % 4
        m = dict(shared)
        m["x"] = np.ascontiguousarray(x[b, j * NTOK:(j + 1) * NTOK, :])
        m["cid"] = np.array([[j]], dtype=np.int32)
        sel = np.zeros((128, 4), dtype=np.float32)
        if j > 0:
            sel[:, j - 1] = 1.0
        m["sel"] = sel
        in_maps.append(m)
    return in_maps


_NC_CACHE = {}


def kernel(**inputs):
    stage = 99
    if stage not in _NC_CACHE:
        _NC_CACHE[stage] = build_program(stage)
    nc = _NC_CACHE[stage]
    in_maps = _prep_inputs(inputs)
    res = run_bass_kernel_spmd(nc, in_maps, core_ids=list(range(8)))
    out = np.empty((2, SEQ, D), dtype=np.float32)
    for c in range(8):
        b, j = c // 4, c % 4
        out[b, j * NTOK:(j + 1) * NTOK, :] = res.results[c]["out"]
    return out
```

```python
import numpy as np
PAIR_SP = True
PAIR_A = True
KSKIP = set()
from contextlib import ExitStack
import concourse.bass as bass
import concourse.mybir as mybir
from concourse.bass_utils import run_bass_kernel_spmd

F32 = mybir.dt.float32
BF16 = mybir.dt.bfloat16
I32 = mybir.dt.int32
AF = mybir.ActivationFunctionType
ALU = mybir.AluOpType

D = 1024
DFF = 2816
NTOK = 2048
NT = 16
NS = 4
SEQ = 8192
EPS = 1e-6
GROUPS = [[0, 1, 2, 3], [4, 5, 6, 7]]


class Res:
    __slots__ = ("writers", "readers")

    def __init__(self):
        self.writers = []
        self.readers = []


class Sched:
    ENGS = ("pe", "act", "dve", "pool", "sp")

    def __init__(self, nc, n_dma_sems=20):
        self.nc = nc
        self.ops = []
        self.queues = {e: [] for e in self.ENGS}
        self.n_dma_sems = n_dma_sems
        self.epoch = 0
        self.pending_dma = []
        self.last = {e: None for e in self.ENGS}
        self.tag = ""

    def add(self, eng, fn, reads=(), writes=(), deps=(), kind="c", pwrites=()):
        i = len(self.ops)
        if self.tag in KSKIP:
            fn = lambda eng: eng.nop()
            kind = "c"
        dl = set()
        for d in deps:
            if d is not None:
                dl.add(d)
        for r in reads:
            dl.update(r.writers)
        for w in writes:
            dl.update(w.writers)
            dl.update(w.readers)
        for w in pwrites:
            dl.update(w.readers)
        self.ops.append([eng, fn, sorted(dl), kind, self.epoch])
        self.queues[eng].append(i)
        for r in reads:
            r.readers.append(i)
        for w in writes:
            w.writers = [i]
            w.readers = []
        for w in pwrites:
            if w.readers:
                w.writers = []
                w.readers = []
            w.writers.append(i)
        if kind != "c":
            self.pending_dma.append(i)
        else:
            self.last[eng] = i
        return i

    def barrier(self):
        deps = [v for v in self.last.values() if v is not None] + list(self.pending_dma)
        self.pending_dma = []
        for e in self.ENGS:
            self.add(e, lambda eng: eng.nop(), deps=deps)
        self.epoch += 1

    def check(self):
        done = set()
        ptr = {e: 0 for e in self.ENGS}
        total = len(self.ops)
        while len(done) < total:
            progressed = False
            for e in self.ENGS:
                q = self.queues[e]
                while ptr[e] < len(q):
                    i = q[ptr[e]]
                    if all(d in done for d in self.ops[i][2]):
                        done.add(i)
                        ptr[e] += 1
                        progressed = True
                    else:
                        break
            if not progressed:
                stuck = {e: self.queues[e][ptr[e]] for e in self.ENGS if ptr[e] < len(self.queues[e])}
                raise RuntimeError(f"schedule deadlock; stuck ops {stuck}")

    def emit(self, stack):
        nc = self.nc
        self.check()
        needed = set()
        for op in self.ops:
            needed.update(op[2])
        nep = self.epoch + 1
        csem = {(e, ep): stack.enter_context(nc.semaphore(f"c_{e}{ep}"))
                for e in self.ENGS for ep in range(nep)}
        dsem = {e: [stack.enter_context(nc.semaphore(f"d_{e}{k}")) for k in range(self.n_dma_sems)]
                for e in ("sp", "pool")}
        ncc = sum(1 for op in self.ops if op[3] == "cc")
        ccsems = [stack.enter_context(nc.semaphore(f"cc{k}")) for k in range(ncc)]
        sig = {}
        pre_wait = {}
        cccount = 0
        for e in self.ENGS:
            ccount = {}
            dcount = [0] * self.n_dma_sems
            rr = 0
            for i in self.queues[e]:
                _e, _f, _deps, kind, ep = self.ops[i]
                if kind == "d":
                    k = rr % self.n_dma_sems
                    rr += 1
                    s = dsem[e][k]
                    if dcount[k] > 0:
                        pre_wait[i] = (s, dcount[k])
                    dcount[k] += 16
                    sig[i] = (s, dcount[k])
                elif kind == "cc":
                    sig[i] = (ccsems[cccount], 1)
                    cccount += 1
                elif i in needed:
                    ccount[ep] = ccount.get(ep, 0) + 1
                    sig[i] = (csem[(e, ep)], ccount[ep])
        block = stack.enter_context(nc.Block())
        ops = self.ops
        queues = self.queues

        def run_queue(e, eng):
            waited = {}
            for i in queues[e]:
                _e, fn, deps, kind, _ep = ops[i]
                ws = []
                if i in pre_wait:
                    ws.append(pre_wait[i])
                for d in deps:
                    if e == "pe" and ops[d][0] == "pe" and ops[d][3] == "c":
                        continue
                    ws.append(sig[d])
                for (s, v) in ws:
                    key = id(s)
                    if waited.get(key, 0) >= v:
                        continue
                    waited[key] = v
                    eng.wait_ge(s, v)
                ins = fn(eng)
                if i in sig:
                    s, v = sig[i]
                    ins.then_inc(s, 16 if kind == "d" else 1)

        @block.tensor
        def _(eng):
            run_queue("pe", eng)

        @block.scalar
        def _(eng):
            run_queue("act", eng)

        @block.vector
        def _(eng):
            run_queue("dve", eng)

        @block.gpsimd
        def _(eng):
            run_queue("pool", eng)

        @block.sync
        def _(eng):
            run_queue("sp", eng)


def build_program(stage=99):
    nc = bass.Bass("TRN2", target_bir_lowering=False)

    def din(name, shape, dt=F32):
        return nc.dram_tensor(name, list(shape), dt, kind="ExternalInput").ap()

    def dint(name, shape, dt=F32):
        return nc.dram_tensor(name, list(shape), dt, kind="Internal").ap()

    x_d = din("x", [NTOK, D])
    out_d = nc.dram_tensor("out", [NTOK, D], F32, kind="ExternalOutput").ap()
    W = {}
    for f in ("ffn1", "ffn2"):
        W[f + "_norm"] = din(f + "_norm", [1, D])
        W[f + "_w_gate"] = din(f + "_w_gate", [D, DFF])
        W[f + "_w_up"] = din(f + "_w_up", [D, DFF])
        W[f + "_w_down"] = din(f + "_w_down", [DFF, D])
    mixn_d = din("mix_norm", [1, D])
    win_d = din("w_in", [D, 3072])
    convw_d = din("conv_w", [3, 512])
    convb_d = din("conv_b", [1, 512])
    sbn_d = din("sb_out_norm", [1, 512])
    cvn_d = din("conv_out_norm", [1, 512])
    wout_d = din("w_out", [D, D])
    finn_d = din("final_norm", [1, D])
    cid_d = din("cid", [1, 1], I32)
    sel_d = din("sel", [128, 4])

    xs1_d = dint("xs1", [NTOK, D])
    s1a_d = [dint(f"s1a{q}", [4 * 256, 512], BF16) for q in range(4)]
    g1a_d = [dint(f"g1a{q}", [4 * 1024, 512], BF16) for q in range(4)]
    s1v_d = [dint(f"s1v{h}", [4 * 1024, 128], BF16) for h in range(2)]
    g1v_d = [dint(f"g1v{h}", [4 * 4096, 128], BF16) for h in range(2)]
    tin_d = dint("t_in", [128, 8])
    tout_d = dint("t_out", [512, 8])
    s2_d = [dint(f"s2{q}", [128, 2048]) for q in range(4)]
    g2_d = dint("g2", [2048, 2048])

    st = ExitStack()
    with st:
        M = st.enter_context(nc.sbuf_tensor("arena", [128, 50700], F32))
        PS = st.enter_context(nc.psum_tensor("ps", [128, 4096], F32))
        cidreg = st.enter_context(nc.sync.register("cidreg"))
        S = Sched(nc)

        def f32v(off, n):
            return M[:, off:off + n]

        def bfv(off, n):
            return M[:, off:off + n // 2].bitcast(BF16)

        def bank(b):
            return PS[:, b * 512:(b + 1) * 512]

        bankres = [Res() for _ in range(8)]

        OX, OH, OA, OW, OY, OC = 0, 16400, 24592, 32784, 46096, 50192
        x_sb = f32v(OX, 16384).rearrange("p (t d) -> p t d", d=D)
        x_res = [Res() for _ in range(NT)]
        hT = bfv(OH, 16384).rearrange("p (k t) -> p k t", t=NTOK)
        hT_res = [Res() for _ in range(NT)]
        aT = bfv(OA, 6 * 2048).rearrange("p (f t) -> p f t", t=NTOK)
        wgu = [bfv(OW + i * 1024, 2048).rearrange("p (k c) -> p k c", c=256) for i in range(4)]
        wgu_res = [Res() for _ in range(4)]
        wdb = [bfv(OW + 4096 + i * 3072, 6144).rearrange("p (f c) -> p f c", c=D) for i in range(2)]
        wdb_res = [Res() for _ in range(2)]
        xn = [bfv(OW + 10240 + i * 512, 1024) for i in range(2)]
        xn_res = [Res() for _ in range(2)]
        gbc = f32v(OW + 11264, 1024)
        gbc_res = Res()
        sil = [f32v(OW + 12288 + i * 512, 512) for i in range(2)]
        sil_res = [Res() for _ in range(2)]
        ycn = bfv(OY, 8192).rearrange("p (c t) -> p c t", t=NTOK)
        ycn_res = [Res() for _ in range(NS)]
        ident = bfv(OC, 128)
        trineg = bfv(OC + 64, 128)
        onesb = bfv(OC + 128, 128)
        ssq = f32v(OC + 192, 16)
        lnv = f32v(OC + 208, 16)
        rstd = f32v(OC + 224, 16)
        epst = f32v(OC + 240, 1)
        onet = f32v(OC + 241, 1)
        cw = f32v(OC + 244, 12).rearrange("p (i c) -> p i c", c=4)
        cb = f32v(OC + 256, 4)
        gcn = f32v(OC + 260, 4)
        gsn = f32v(OC + 264, 4)
        sel = f32v(OC + 268, 4)
        tails = f32v(OC + 272, 32).rearrange("p (r e) -> p r e", e=8)
        halo = f32v(OC + 304, 8).rearrange("p (c m) -> p c m", m=2)
        const_res = Res()
        stat_res = Res()

        def mk_consts():
            S.add("pool", lambda e: e.memset(ident, 1.0), writes=[const_res])
            S.add("pool", lambda e: e.affine_select(out=ident, in_=ident, pattern=[[-1, 128]],
                                                    compare_op=ALU.is_equal, fill=0.0, base=0,
                                                    channel_multiplier=1), writes=[const_res])
            S.add("pool", lambda e: e.memset(trineg, -1.0), writes=[const_res])
            S.add("pool", lambda e: e.affine_select(out=trineg, in_=trineg, pattern=[[-1, 128]],
                                                    compare_op=ALU.is_ge, fill=0.0, base=0,
                                                    channel_multiplier=1), writes=[const_res])
            S.add("pool", lambda e: e.memset(onesb, 1.0), writes=[const_res])
            S.add("pool", lambda e: e.memset(epst, EPS), writes=[const_res])
            S.add("pool", lambda e: e.memset(onet, 1.0), writes=[const_res])
            for i in range(3):
                S.add("sp", lambda e, i=i: e.dma_start(
                    out=cw[:, i, :], in_=convw_d[i, :].rearrange("(c p) -> p c", p=128),
                    allow_slow_non_contiguous=True), pwrites=[const_res], kind="d")
            for (dst, src) in ((cb, convb_d), (gcn, cvn_d), (gsn, sbn_d)):
                S.add("sp", lambda e, dst=dst, src=src: e.dma_start(
                    out=dst, in_=src[0, :].rearrange("(c p) -> p c", p=128),
                    allow_slow_non_contiguous=True), pwrites=[const_res], kind="d")
            S.add("sp", lambda e: e.dma_start(out=sel, in_=sel_d), pwrites=[const_res], kind="d")

        def load_x(src, tts=range(NT)):
            for tt in tts:
                S.add("sp", lambda e, tt=tt: e.dma_start(out=x_sb[:, tt, :], in_=src[tt * 128:(tt + 1) * 128, :]),
                      writes=[x_res[tt]], kind="d")

        def tm_stats(junk_bufs, junk_res, tts=range(NT)):
            t0_, t1_ = tts[0], tts[-1] + 1
            for tt in tts:
                S.add("act", lambda e, tt=tt: e.activation(out=junk_bufs[tt % 2], in_=x_sb[:, tt, :], func=AF.Square,
                                                           accum_out=ssq[:, tt:tt + 1]),
                      reads=[x_res[tt]], writes=[junk_res[tt % 2], stat_res])
            S.add("act", lambda e: e.activation(out=lnv[:, t0_:t1_], in_=ssq[:, t0_:t1_], func=AF.Ln, bias=epst,
                                                scale=1.0 / D),
                  reads=[const_res], writes=[stat_res])
            S.add("act", lambda e: e.activation(out=rstd[:, t0_:t1_], in_=lnv[:, t0_:t1_], func=AF.Exp, scale=-0.5),
                  writes=[stat_res])

        def load_gbc(g_d):
            S.add("sp", lambda e: e.dma_start(out=gbc, in_=g_d.partition_broadcast(128)), writes=[gbc_res], kind="d")

        def make_hT(g_d, tts=range(NT)):
            load_gbc(g_d)
            tm_stats(xn, xn_res, tts)
            for tt in tts:
                b = tt % 2
                S.add("dve", lambda e, tt=tt, b=b: e.scalar_tensor_tensor(
                    out=xn[b], in0=x_sb[:, tt, :], scalar=rstd[:, tt:tt + 1], in1=gbc, op0=ALU.mult, op1=ALU.mult),
                    reads=[x_res[tt], stat_res, gbc_res], writes=[xn_res[b]])
                tp = bank(b).bitcast(BF16)
                for kc in range(8):
                    S.add("pe", lambda e, kc=kc, b=b, tp=tp: e.transpose(
                        out=tp[:, kc * 128:(kc + 1) * 128], in_=xn[b][:, kc * 128:(kc + 1) * 128], identity=ident),
                        reads=[xn_res[b], const_res], writes=[bankres[b]])
                S.add("act", lambda e, tt=tt, tp=tp: e.activation(
                    out=hT[:, :, tt * 128:(tt + 1) * 128], in_=tp.rearrange("p (k t) -> p k t", t=128), func=AF.Copy),
                    reads=[bankres[b]], writes=[hT_res[tt]])

        tstat = [Res() for _ in range(NT)]
        junkb = [bfv(OW + 12288, 1024), bfv(OW + 12288, 1024)]
        xn3 = [xn[0], xn[1], bfv(OW + 12288 + 512, 1024)]
        xn3_res = [xn_res[0], xn_res[1], sil_res[1]]

        def tile_stats(tt):
            j = 0
            S.add("act", lambda e: e.activation(out=junkb[j], in_=x_sb[:, tt, :], func=AF.Square,
                                                accum_out=ssq[:, tt:tt + 1]),
                  reads=[x_res[tt]], writes=[sil_res[j], tstat[tt]])
            S.add("act", lambda e: e.activation(out=lnv[:, tt:tt + 1], in_=ssq[:, tt:tt + 1], func=AF.Ln, bias=epst,
                                                scale=1.0 / D),
                  reads=[const_res], writes=[tstat[tt]])
            S.add("act", lambda e: e.activation(out=rstd[:, tt:tt + 1], in_=lnv[:, tt:tt + 1], func=AF.Exp, scale=-0.5),
                  writes=[tstat[tt]])

        def hT_p1(tt):
            tile_stats(tt)
            b3 = tt % 3
            S.add("dve", lambda e: e.scalar_tensor_tensor(
                out=xn3[b3], in0=x_sb[:, tt, :], scalar=rstd[:, tt:tt + 1], in1=gbc, op0=ALU.mult, op1=ALU.mult),
                reads=[x_res[tt], tstat[tt], gbc_res], writes=[xn3_res[b3]])

        def hT_p2(tt, extra_deps=()):
            b = tt % 2
            b3 = tt % 3
            tp = bank(b).bitcast(BF16)
            for kc in range(8):
                S.add("pe", lambda e, kc=kc: e.transpose(
                    out=tp[:, kc * 128:(kc + 1) * 128], in_=xn3[b3][:, kc * 128:(kc + 1) * 128], identity=ident),
                    reads=[xn3_res[b3], const_res], writes=[bankres[b]])
            S.add("act", lambda e: e.activation(
                out=hT[:, :, tt * 128:(tt + 1) * 128], in_=tp.rearrange("p (k t) -> p k t", t=128), func=AF.Copy),
                reads=[bankres[b]], writes=[hT_res[tt]], deps=extra_deps)

        class Lagged:
            def __init__(self, extra_deps=()):
                self.q = []
                self.extra = extra_deps

            def __call__(self, tt):
                if len(self.q) == 2:
                    hT_p2(self.q.pop(0), self.extra)
                hT_p1(tt)
                self.q.append(tt)

            def flush(self):
                while self.q:
                    hT_p2(self.q.pop(0), self.extra)

        ost = [f32v(OY + i * 1024, 1024) for i in range(2)]
        ost_res = [Res(), Res()]
        out_res = Res()

        def final_tile(tt):
            tile_stats(tt)
            b = tt % 2
            S.add("dve", lambda e: e.scalar_tensor_tensor(
                out=ost[b], in0=x_sb[:, tt, :], scalar=rstd[:, tt:tt + 1], in1=gbc, op0=ALU.mult, op1=ALU.mult),
                reads=[x_res[tt], tstat[tt], gbc_res], writes=[ost_res[b]])
            S.add("sp", lambda e: e.dma_start(out=out_d[tt * 128:(tt + 1) * 128, :], in_=ost[b]),
                  reads=[ost_res[b]], pwrites=[out_res], kind="d")

        def ffn(pref, tts=range(NT), hook=None, do_make=True, next_gain=None, post_tile=None):
            wg_d, wu_d, wd_d = W[pref + "_w_gate"], W[pref + "_w_up"], W[pref + "_w_down"]
            if do_make:
                make_hT(W[pref + "_norm"], tts)
            if next_gain is not None:
                load_gbc(next_gain)
            tss = range(tts[0] // 4, tts[-1] // 4 + 1)
            aT_res = [[Res() for _ in range(NS)] for _ in range(6)]
            quarters = [(0, 6), (6, 6), (12, 6), (18, 4)]
            blk = 0
            mmi = 0
            for qi, (f0, nf) in enumerate(quarters):
                wq = qi % 2
                S.add("pool", lambda e, wq=wq, f0=f0, nf=nf: e.dma_start(
                    out=wdb[wq][:, 0:nf, :], in_=wd_d[f0 * 128:(f0 + nf) * 128, :].rearrange("(f p) c -> p f c", p=128)),
                    writes=[wdb_res[wq]], kind="d")
                for pi in range(nf // 2):
                    c0 = (f0 + 2 * pi) * 128
                    bg, bu = (2 * blk) % 4, (2 * blk + 1) % 4
                    blk += 1
                    for (bb, wsrc) in ((bg, wg_d), (bu, wu_d)):
                        S.add("pool", lambda e, bb=bb, wsrc=wsrc, c0=c0: e.dma_start(
                            out=wgu[bb], in_=wsrc[:, c0:c0 + 256].rearrange("(k p) c -> p k c", p=128)),
                            writes=[wgu_res[bb]], kind="d")
                    if hook and (blk % 2 == 0) and (blk // 2 - 1) < len(hook) and blk >= 2:
                        hook[blk // 2 - 1]()
                    for fi in range(2):
                        fl = 2 * pi + fi
                        for ts in tss:
                            gb, ub = 2 + (mmi % 2), 4 + (mmi % 2)
                            sb_ = mmi % 2
                            mmi += 1
                            for (pb, wb) in ((gb, bg), (ub, bu)):
                                for kc in range(8):
                                    S.add("pe", lambda e, pb=pb, wb=wb, kc=kc, fi=fi, ts=ts: e.matmul(
                                        bank(pb), lhsT=wgu[wb][:, kc, fi * 128:(fi + 1) * 128],
                                        rhs=hT[:, kc, ts * 512:(ts + 1) * 512], start=(kc == 0), stop=(kc == 7)),
                                        reads=[wgu_res[wb]] + hT_res[4 * ts:4 * ts + 4], writes=[bankres[pb]])
                            S.add("act", lambda e, gb=gb, sb_=sb_: e.activation(out=sil[sb_], in_=bank(gb), func=AF.Silu),
                                  reads=[bankres[gb]], writes=[sil_res[sb_]])
                            S.add("dve", lambda e, ub=ub, sb_=sb_, fl=fl, ts=ts: e.tensor_tensor(
                                out=aT[:, fl, ts * 512:(ts + 1) * 512], in0=bank(ub), in1=sil[sb_], op=ALU.mult),
                                reads=[bankres[ub], sil_res[sb_]], writes=[aT_res[fl][ts]])
                for tt in tts:
                    for nh in range(2):
                        db = 6 + ((tt * 2 + nh) % 2)
                        for fl in range(nf):
                            S.add("pe", lambda e, db=db, fl=fl, tt=tt, nh=nh, wq=wq, nf=nf: e.matmul(
                                bank(db), lhsT=aT[:, fl, tt * 128:(tt + 1) * 128],
                                rhs=wdb[wq][:, fl, nh * 512:(nh + 1) * 512], start=(fl == 0), stop=(fl == nf - 1)),
                                reads=[aT_res[fl][tt // 4], wdb_res[wq]], writes=[bankres[db]])
                        S.add("dve", lambda e, db=db, tt=tt, nh=nh: e.scalar_tensor_tensor(
                            out=x_sb[:, tt, nh * 512:(nh + 1) * 512], in0=bank(db), scalar=0.5,
                            in1=x_sb[:, tt, nh * 512:(nh + 1) * 512], op0=ALU.mult, op1=ALU.add),
                            reads=[bankres[db]], writes=[x_res[tt]])
                    if post_tile is not None and qi == len(quarters) - 1:
                        post_tile(tt)
                if post_tile is not None and qi == len(quarters) - 1 and hasattr(post_tile, "flush"):
                    post_tile.flush()

        vals = {}

        def rl(e):
            ins = e.reg_load(cidreg, cid_d[0:1, 0:1])
            vals["v"] = e.snap(cidreg)
            return ins
        S.add("sp", rl)
        mk_consts()
        HALVES = [range(0, 8), range(8, 16)]
        load_x(x_d, HALVES[0])
        load_x(x_d, HALVES[1])

        Bsb = f32v(OX, 8192).rearrange("p (c t) -> p c t", t=NTOK)
        xc = f32v(OX + 8192, 8200).rearrange("p (c t) -> p c t", t=NTOK + 2)
        bx_res = Res()
        stq = [bfv(OY + i * 1024, 2048) for i in range(2)]
        stq_res = [Res(), Res()]
        stv = [bfv(OY + 2048 + i * 128, 256) for i in range(2)]
        stv_res = [Res(), Res()]
        s1a_res, s1v_res, tin_res = Res(), Res(), Res()
        g1a_res, g1v_res, tout_res = Res(), Res(), Res()
        xs1_res = Res()

        def allgather(src, dst, rres, wres):
            S.add("pool", lambda e: e.collective_compute("AllGather", ALU.bypass, replica_groups=GROUPS,
                                                         ins=[src], outs=[dst]),
                  reads=[rres], pwrites=[wres], kind="cc")
        pj = {"mmi": 0, "sqi": 0, "svi": 0, "wbi": 0}

        def proj_blocks(blks, tts, hook=None, hook_after=2, drain=None):
            tss = range(tts[0] // 4, tts[-1] // 4 + 1)
            for bi_, blkI in enumerate(blks):
                if drain and bi_ > 0:
                    for _ in range(4):
                        if drain:
                            drain.pop(0)()
                wb = pj["wbi"] % 4
                pj["wbi"] += 1
                S.add("pool", lambda e, wb=wb, blkI=blkI: e.dma_start(
                    out=wgu[wb], in_=win_d[:, blkI * 256:(blkI + 1) * 256].rearrange("(k p) c -> p k c", p=128)),
                    writes=[wgu_res[wb]], kind="d")
                if hook and bi_ == 3:
                    hook[0]()
                if hook and bi_ == len(blks) - 1:
                    for hk_ in hook[1:]:
                        hk_()
                kindb = blkI // 2
                if kindb == 2:
                    for tt in tts:
                        pb = 2 + (pj["mmi"] % 4)
                        pj["mmi"] += 1
                        for kc in range(8):
                            S.add("pe", lambda e, pb=pb, wb=wb, kc=kc, tt=tt: e.matmul(
                                bank(pb)[:, 0:256], lhsT=hT[:, kc, tt * 128:(tt + 1) * 128], rhs=wgu[wb][:, kc, :],
                                start=(kc == 0), stop=(kc == 7)),
                                reads=[wgu_res[wb], hT_res[tt]], writes=[bankres[pb]])
                        sv = pj["svi"] % 2
                        pj["svi"] += 1
                        S.add("act", lambda e, pb=pb, sv=sv: e.activation(out=stv[sv], in_=bank(pb)[:, 0:256], func=AF.Copy),
                              reads=[bankres[pb]], writes=[stv_res[sv]])
                        for hh in range(2):
                            hp = (blkI - 4) * 2 + hh
                            S.add("sp", lambda e, sv=sv, hh=hh, hp=hp, tt=tt: e.dma_start(
                                out=s1v_d[tt // 8][hp * 1024:(hp + 1) * 1024, :].rearrange("(p b) f -> p b f", b=8)[:, tt % 8, :],
                                in_=stv[sv][:, hh * 128:(hh + 1) * 128]),
                                reads=[stv_res[sv]], pwrites=[s1v_res], kind="d")
                    continue
                for ci in range(2):
                    chunk = (blkI % 2) * 2 + ci
                    sq = pj["sqi"] % 2
                    if kindb < 2:
                        pj["sqi"] += 1
                    for ts in tss:
                        pb = 2 + (pj["mmi"] % 4)
                        pj["mmi"] += 1
                        for kc in range(8):
                            S.add("pe", lambda e, pb=pb, wb=wb, kc=kc, ci=ci, ts=ts: e.matmul(
                                bank(pb), lhsT=wgu[wb][:, kc, ci * 128:(ci + 1) * 128],
                                rhs=hT[:, kc, ts * 512:(ts + 1) * 512], start=(kc == 0), stop=(kc == 7)),
                                reads=[wgu_res[wb]] + hT_res[4 * ts:4 * ts + 4], writes=[bankres[pb]])
                        tsl = slice(ts * 512, (ts + 1) * 512)
                        if kindb == 0:
                            S.add("act", lambda e, pb=pb, sq=sq, tsl=tsl: e.activation(
                                out=stq[sq][:, tsl], in_=bank(pb), func=AF.Copy, scale=0.125),
                                reads=[bankres[pb]], writes=[stq_res[sq]])
                        elif kindb == 1:
                            S.add("act", lambda e, pb=pb, sq=sq, tsl=tsl: e.activation(
                                out=stq[sq][:, tsl], in_=bank(pb), func=AF.Copy),
                                reads=[bankres[pb]], writes=[stq_res[sq]])
                        elif kindb == 3:
                            S.add("act", lambda e, pb=pb, chunk=chunk, tsl=tsl: e.activation(
                                out=Bsb[:, chunk, tsl], in_=bank(pb), func=AF.Copy),
                                reads=[bankres[pb], xs1_res], pwrites=[bx_res])
                        elif kindb == 4:
                            S.add("act", lambda e, pb=pb, chunk=chunk, ts=ts: e.activation(
                                out=xc[:, chunk, 2 + ts * 512: 2 + (ts + 1) * 512], in_=bank(pb), func=AF.Copy),
                                reads=[bankres[pb], xs1_res], pwrites=[bx_res])
                        else:
                            S.add("dve", lambda e, pb=pb, chunk=chunk, ts=ts: e.tensor_tensor(
                                out=xc[:, chunk, 2 + ts * 512: 2 + (ts + 1) * 512], in0=bank(pb),
                                in1=xc[:, chunk, 2 + ts * 512: 2 + (ts + 1) * 512], op=ALU.mult),
                                reads=[bankres[pb], bx_res], pwrites=[bx_res])
                    if kindb < 2:
                        r0 = chunk * 256 + kindb * 128
                        for q4 in tss:
                            S.add("sp", lambda e, sq=sq, r0=r0, q4=q4: e.dma_start(
                                out=s1a_d[q4][r0:r0 + 128, :], in_=stq[sq][:, q4 * 512:(q4 + 1) * 512]),
                                reads=[stq_res[sq]], pwrites=[s1a_res], kind="d")

        def collectives_for(hi):
            return [lambda: allgather(s1a_d[2 * hi], g1a_d[2 * hi], s1a_res, g1a_res),
                    lambda: allgather(s1v_d[hi], g1v_d[hi], s1v_res, g1v_res),
                    lambda: allgather(s1a_d[2 * hi + 1], g1a_d[2 * hi + 1], s1a_res, g1a_res)]

        pend = [None]
        deferred = []
        for hi, tts in enumerate(HALVES):
            hk = pend[0]
            pend[0] = None
            ffn("ffn1", tts, hook=hk, do_make=(hi == 0), next_gain=mixn_d,
                post_tile=Lagged() if stage >= 2 else None)
            if stage >= 2:
                S.tag = 'proj'
                for tt in tts:
                    S.add("sp", lambda e, tt=tt: e.dma_start(out=xs1_d[tt * 128:(tt + 1) * 128, :], in_=x_sb[:, tt, :]),
                          reads=[x_res[tt]], pwrites=[xs1_res], kind="d")
                if hi == 0:
                    load_gbc(W["ffn1_norm"])
                    tb_ = list(HALVES[1])
                    deferred.append(lambda: hT_p1(tb_[0]))
                    deferred.append(lambda: hT_p1(tb_[1]))
                    for k_ in range(2, len(tb_)):
                        deferred.append(lambda k_=k_: hT_p2(tb_[k_ - 2]))
                        deferred.append(lambda k_=k_: hT_p1(tb_[k_]))
                    deferred.append(lambda: hT_p2(tb_[-2]))
                    deferred.append(lambda: hT_p2(tb_[-1]))
                proj_blocks(range(0, 6), tts, drain=deferred)
                while deferred:
                    deferred.pop(0)()
                pend[0] = collectives_for(hi)
                S.tag = ''
            elif hi == 0:
                make_hT(W["ffn1_norm"], HALVES[1])

        if stage >= 2:
            S.tag = 'proj'
            proj_blocks(range(6, 12), range(NT), hook=pend[0], hook_after=2)
            S.add("sp", lambda e: e.dma_start(out=tin_d.rearrange("p (c m) -> p c m", m=2), in_=xc[:, :, NTOK:NTOK + 2]),
                  reads=[bx_res], writes=[tin_res], kind="d")
            allgather(tin_d, tout_d, tin_res, tout_res)

            S.tag = 'attload'
            KQ = bfv(OH, 16384).rearrange("p (w c) -> p w c", w=2)
            Qp = KQ[:, 0, :]
            KT = KQ[:, 1, :]
            Vs = bfv(OA, 8192).rearrange("p (b f) -> p b f", f=128)
            QA = bfv(OA + 4096, 8192)
            QB = bfv(OA + 8192, 8192)
            att_in = Res()
            kq_res = [Res() for _ in range(4)]
            ht_dead = sorted(set(o for r_ in hT_res for o in (r_.readers + r_.writers)))
            S.add("pool", lambda e: e.memset(QA[64:128, :], 0.0), pwrites=[att_in])
            S.add("pool", lambda e: e.memset(QB[0:64, :], 0.0), pwrites=[att_in])
            for i in range(4):
                cs = slice(i * 2048, (i + 1) * 2048)
                for q4 in range(4):
                    S.add("sp", lambda e, i=i, q4=q4: e.dma_start(
                        out=KQ[:, :, i * 2048 + q4 * 512: i * 2048 + (q4 + 1) * 512],
                        in_=g1a_d[q4][bass.ds(vals["v"] * 256 + i * 1024, 256), :].rearrange("(w p) c -> p w c", p=128)),
                        reads=[g1a_res], pwrites=[kq_res[i]], deps=ht_dead, kind="d")
                for h2 in range(2):
                    S.add("sp", lambda e, i=i, h2=h2: e.dma_start(
                        out=Vs[:, i * 16 + h2 * 8: i * 16 + h2 * 8 + 8, :],
                        in_=g1v_d[h2][bass.ds(vals["v"] * 1024 + i * 4096, 1024), :].rearrange("(p b) f -> p b f", b=8)),
                        reads=[g1v_res], pwrites=[att_in], kind="d")
                S.add("sp", lambda e, cs=cs: e.dma_start(out=QA[0:64, cs], in_=Qp[0:64, cs]),
                      reads=[kq_res[i]], pwrites=[att_in], kind="d")
                S.add("sp", lambda e, cs=cs: e.dma_start(out=QB[64:128, cs], in_=Qp[64:128, cs]),
                      reads=[kq_res[i]], pwrites=[att_in], kind="d")
            tails_res = Res()
            S.add("sp", lambda e: e.dma_start(out=tails, in_=tout_d.rearrange("(r p) e -> p r e", p=128)),
                  reads=[tout_res], writes=[tails_res], kind="d")

            S.tag = 'conv'
            accb = [f32v(OW + 4096 + i * 512, 512) for i in range(2)]
            acc_res = [Res(), Res()]
            sqb = [bfv(OW + 4096 + 1024 + i * 256, 512) for i in range(2)]
            sqb_res = [Res(), Res()]
            lnb = f32v(OW + 4096 + 1536, 512)
            rsb = f32v(OW + 4096 + 2048, 512)
            lnb_res, rsb_res = Res(), Res()
            stage_dmas = sorted(set(s1a_res.writers + s1v_res.writers))
            haloflat = f32v(OC + 304, 8)
            halo_res = Res()

            def halo_ops():
                S.add("dve", lambda e: e.tensor_scalar(out=haloflat, in0=tails[:, 0, :], scalar1=sel[:, 0:1], scalar2=None,
                                                       op0=ALU.mult), reads=[const_res, tails_res], writes=[stat_res])
                for r in range(1, 4):
                    S.add("dve", lambda e, r=r: e.scalar_tensor_tensor(
                        out=haloflat, in0=tails[:, r, :], scalar=sel[:, r:r + 1], in1=haloflat, op0=ALU.mult, op1=ALU.add),
                        reads=[const_res, tails_res], writes=[stat_res])
                S.add("dve", lambda e: e.tensor_copy(xc[:, :, 0:2], halo), reads=[stat_res], writes=[halo_res])
            yc_res = [[Res() for _ in range(NS)] for _ in range(4)]
            ci_ = 0
            for ts in (1, 2, 3, 0):
                if ts == 0:
                    halo_ops()
                for cc in range(4):
                    a = ci_ % 2
                    ci_ += 1
                    t0 = ts * 512
                    S.add("dve", lambda e, a=a, cc=cc, t0=t0: e.tensor_scalar(
                        out=accb[a], in0=xc[:, cc, t0 + 2:t0 + 514], scalar1=cw[:, 2, cc:cc + 1],
                        scalar2=cb[:, cc:cc + 1], op0=ALU.mult, op1=ALU.add),
                        reads=[bx_res, const_res], writes=[acc_res[a]])
                    S.add("dve", lambda e, a=a, cc=cc, t0=t0: e.scalar_tensor_tensor(
                        out=accb[a], in0=xc[:, cc, t0 + 1:t0 + 513], scalar=cw[:, 1, cc:cc + 1], in1=accb[a],
                        op0=ALU.mult, op1=ALU.add), reads=[bx_res, halo_res], writes=[acc_res[a]])
                    S.add("dve", lambda e, a=a, cc=cc, t0=t0: e.scalar_tensor_tensor(
                        out=accb[a], in0=xc[:, cc, t0:t0 + 512], scalar=cw[:, 0, cc:cc + 1], in1=accb[a],
                        op0=ALU.mult, op1=ALU.add), reads=[bx_res, halo_res], writes=[acc_res[a]])
                    S.add("pool", lambda e, a=a, cc=cc, t0=t0: e.tensor_tensor(
                        out=Bsb[:, cc, t0:t0 + 512], in0=Bsb[:, cc, t0:t0 + 512], in1=accb[a], op=ALU.mult),
                        reads=[acc_res[a], bx_res], writes=[yc_res[cc][ts]])

            def fm_norm(src, src_res, gain, dst, dst_res, nb):
                k = 0
                for ts in range(NS):
                    tsl = slice(ts * 512, (ts + 1) * 512)
                    for cc in range(4):
                        q = k % 2
                        k += 1
                        S.add("act", lambda e, q=q, cc=cc, tsl=tsl: e.activation(out=sqb[q], in_=src[:, cc, tsl], func=AF.Square),
                              reads=[src_res[cc][ts]], writes=[sqb_res[q]])
                        S.add("pe", lambda e, q=q, cc=cc: e.matmul(bank(nb), lhsT=onesb, rhs=sqb[q],
                                                                   start=(cc == 0), stop=(cc == 3)),
                              reads=[sqb_res[q], const_res], writes=[bankres[nb]])
                    S.add("act", lambda e: e.activation(out=lnb, in_=bank(nb), func=AF.Ln, bias=epst, scale=1.0 / 512),
                          reads=[bankres[nb], const_res], writes=[lnb_res])
                    S.add("act", lambda e: e.activation(out=rsb, in_=lnb, func=AF.Exp, scale=-0.5),
                          reads=[lnb_res], writes=[rsb_res])
                    for cc in range(4):
                        S.add("dve", lambda e, cc=cc, tsl=tsl: e.scalar_tensor_tensor(
                            out=dst[:, cc, tsl], in0=src[:, cc, tsl], scalar=gain[:, cc:cc + 1], in1=rsb,
                            op0=ALU.mult, op1=ALU.mult),
                            reads=[src_res[cc][ts], rsb_res, const_res], writes=[dst_res[ts]])

            S.add("dve", lambda e: e.nop(), deps=stage_dmas)
            fm_norm(Bsb, yc_res, gcn, ycn, ycn_res, 6)
            S.barrier()

        if stage >= 3:
            S.tag = 'att'
            load_x(xs1_d)
            OB = OW + 4096
            Ebuf = [f32v(OB + i * 1024, 1024).rearrange("p (h t) -> p h t", h=2) for i in range(2)]
            Tbuf = [f32v(OB + 2048 + i * 1024, 1024).rearrange("p (h t) -> p h t", h=2) for i in range(3)]
            SPb = [bfv(OB + 5120 + i * 512, 1024).rearrange("p (h t) -> p h t", h=2) for i in range(2)]
            Abf = [bfv(OB + 6144 + i * 512, 1024).rearrange("p (h t) -> p h t", h=2) for i in range(2)]
            carry = [f32v(OB + 7168 + i * 512, 512) for i in range(2)]
            ybuf = [f32v(OB + 8192 + i * 512, 512) for i in range(2)]
            E_res = [Res(), Res()]
            T_res = [Res(), Res(), Res()]
            SP_res = [Res(), Res()]
            A_res = [Res(), Res()]
            carry_res = [Res(), Res()]
            ybuf_res = [Res(), Res()]
            s2_res = [Res() for _ in range(4)]
            g2_res = Res()
            Qh = [QA, QB]
            NQS = SEQ // 512
            pairs = []
            for qs in range(NQS):
                nkb = 4 * qs + 4
                for r in range(nkb):
                    pairs.append((qs, r, nkb))
            NP = len(pairs)

            def need_mask(qs, kb):
                return kb >= 4 * qs

            def mask_pair(buf, res, qs, kb):
                base = 512 * qs - 128 * kb
                for h in range(2):
                    S.add("pool", lambda e, h=h: e.affine_select(out=buf[:, h, :], in_=buf[:, h, :], pattern=[[1, 512]],
                                                                 compare_op=ALU.is_gt, fill=0.0, base=base,
                                                                 channel_multiplier=-1),
                          writes=[res])

            def st_Z(p):
                qs, r, nkb = pairs[p]
                kb = nkb - 1 - r
                for h in range(2):
                    S.add("pe", lambda e, h=h: e.matmul(bank(h), lhsT=KT[:, kb * 128:(kb + 1) * 128],
                                                        rhs=Qh[h][:, qs * 512:(qs + 1) * 512], start=True, stop=True),
                          reads=[att_in], writes=[bankres[h]])

            def st_E(p):
                pp = p % 2
                for h in range(2):
                    S.add("act", lambda e, h=h: e.activation(out=Ebuf[pp][:, h, :], in_=bank(h), func=AF.Exp),
                          reads=[bankres[h]], pwrites=[E_res[pp]])

            def st_SP(p):
                qs, r, nkb = pairs[p]
                kb = nkb - 1 - r
                pp = p % 2
                if PAIR_SP:
                    S.add("act", lambda e: e.activation(out=SPb[pp], in_=Ebuf[pp], func=AF.Ln, bias=onet),
                          reads=[E_res[pp], const_res], writes=[SP_res[pp]])
                else:
                    for h in range(2):
                        S.add("act", lambda e, h=h: e.activation(out=SPb[pp][:, h, :], in_=Ebuf[pp][:, h, :], func=AF.Ln, bias=onet),
                              reads=[E_res[pp], const_res], writes=[SP_res[pp]])
                if need_mask(qs, kb):
                    mask_pair(SPb[pp], SP_res[pp], qs, kb)

            def st_LC(p):
                qs, r, nkb = pairs[p]
                kb = nkb - 1 - r
                pp = p % 2
                for h in range(2):
                    lb, cbk = 2 + h, 4 + h
                    S.add("pe", lambda e, h=h, lb=lb: e.matmul(bank(lb), lhsT=KT[:, kb * 128:(kb + 1) * 128],
                                                               rhs=Qh[h][:, qs * 512:(qs + 1) * 512], start=True, stop=False),
                          reads=[att_in], writes=[bankres[lb]])
                    S.add("pe", lambda e, h=h, lb=lb: e.matmul(bank(lb), lhsT=trineg, rhs=SPb[pp][:, h, :],
                                                               start=False, stop=True),
                          reads=[SP_res[pp], const_res], writes=[bankres[lb]])
                    if r < nkb - 1:
                        S.add("pe", lambda e, h=h, cbk=cbk: e.matmul(bank(cbk), lhsT=onesb, rhs=SPb[pp][:, h, :],
                                                                     start=True, stop=True),
                              reads=[SP_res[pp], const_res], writes=[bankres[cbk]])

            def st_T(p):
                qs, r, nkb = pairs[p]
                pt = p % 3
                for h in range(2):
                    lb, cbk = 2 + h, 4 + h
                    if r == 0:
                        S.add("dve", lambda e, h=h, lb=lb: e.tensor_copy(Tbuf[pt][:, h, :], bank(lb)),
                              reads=[bankres[lb]], pwrites=[T_res[pt]])
                        if r < nkb - 1:
                            S.add("dve", lambda e, h=h, cbk=cbk: e.tensor_copy(carry[h], bank(cbk)),
                                  reads=[bankres[cbk]], writes=[carry_res[h]])
                    else:
                        S.add("dve", lambda e, h=h, lb=lb: e.tensor_tensor(out=Tbuf[pt][:, h, :], in0=bank(lb),
                                                                           in1=carry[h], op=ALU.subtract),
                              reads=[bankres[lb], carry_res[h]], pwrites=[T_res[pt]])
                        if r < nkb - 1:
                            S.add("dve", lambda e, h=h, cbk=cbk: e.tensor_tensor(out=carry[h], in0=bank(cbk),
                                                                               in1=carry[h], op=ALU.add),
                                  reads=[bankres[cbk]], writes=[carry_res[h]])

            def st_A(p):
                qs, r, nkb = pairs[p]
                kb = nkb - 1 - r
                pt = p % 3
                pp = p % 2
                if PAIR_A:
                    S.add("act", lambda e: e.activation(out=Abf[pp], in_=Tbuf[pt], func=AF.Exp),
                          reads=[T_res[pt]], writes=[A_res[pp]])
                else:
                    for h in range(2):
                        S.add("act", lambda e, h=h: e.activation(out=Abf[pp][:, h, :], in_=Tbuf[pt][:, h, :], func=AF.Exp),
                              reads=[T_res[pt]], writes=[A_res[pp]])
                if need_mask(qs, kb):
                    mask_pair(Abf[pp], A_res[pp], qs, kb)

            def st_AV(p):
                qs, r, nkb = pairs[p]
                kb = nkb - 1 - r
                pp = p % 2
                for h in range(2):
                    ob = 6 + h
                    S.add("pe", lambda e, h=h, ob=ob: e.matmul(bank(ob), lhsT=Vs[:, kb, :], rhs=Abf[pp][:, h, :],
                                                               start=(r == 0), stop=(r == nkb - 1)),
                          reads=[A_res[pp], att_in], writes=[bankres[ob]])
                    if r == nkb - 1:
                        yb = qs % 2
                        ps_ = slice(64 * h, 64 * h + 64)
                        S.add("dve", lambda e, ob=ob, ps_=ps_, yb=yb: e.tensor_copy(ybuf[yb][ps_, :], bank(ob)[ps_, :]),
                              reads=[bankres[ob]], writes=[ybuf_res[yb]])
                        if h == 1:
                            dst = qs // 4
                            c0 = (qs % 4) * 512
                            S.add("sp", lambda e, dst=dst, c0=c0, yb=yb: e.dma_start(out=s2_d[dst][:, c0:c0 + 512], in_=ybuf[yb]),
                                  reads=[ybuf_res[yb]], pwrites=[s2_res[dst]], kind="d")
                            if qs % 4 == 3:
                                allgather(s2_d[dst], g2_d[dst * 512:(dst + 1) * 512, :], s2_res[dst], g2_res)

            st_Z(0)
            for step in range(NP + 4):
                if step < NP:
                    st_E(step)
                if 0 <= step - 1 < NP:
                    st_LC(step - 1)
                if step + 1 < NP:
                    st_Z(step + 1)
                if 0 <= step - 3 < NP:
                    st_AV(step - 3)
                if 0 <= step - 2 < NP:
                    st_A(step - 2)
                if step < NP:
                    st_SP(step)
                if 0 <= step - 1 < NP:
                    st_T(step - 1)
            S.barrier()

        if stage >= 4:
            S.tag = 'outproj'
            ysb = f32v(OH, 8192).rearrange("p (c t) -> p c t", t=NTOK)
            ysb_res = [[Res() for _ in range(NS)] for _ in range(4)]
            S.add("sp", lambda e: e.dma_start(
                out=ysb, in_=g2_d[bass.ds(vals["v"] * 512, 512), :].rearrange("(i p) c -> p i c", p=128)),
                reads=[g2_res], writes=[r_ for i in range(4) for r_ in ysb_res[i]], kind="d")
            ysn = bfv(OA, 8192).rearrange("p (c t) -> p c t", t=NTOK)
            ysn_res = [Res() for _ in range(NS)]
            wo = bfv(OA + 4096, 8192).rearrange("p (k c) -> p k c", c=D)
            wo_res = Res()
            S.add("pool", lambda e: e.dma_start(out=wo, in_=wout_d.rearrange("(k p) c -> p k c", p=128)),
                  writes=[wo_res], kind="d")
            fm_norm(ysb, ysb_res, gsn, ysn, ysn_res, 6)
            ysb_readers = sorted(set(o for row in ysb_res for r_ in row for o in (r_.readers + r_.writers)))
            load_gbc(W["ffn2_norm"])
            lag4 = Lagged(ysb_readers)
            mmi = 0
            for tt in range(NT):
                for nh in range(2):
                    pb = 2 + (mmi % 4)
                    mmi += 1
                    for kc in range(8):
                        src, sres = (ysn, ysn_res) if kc < 4 else (ycn, ycn_res)
                        S.add("pe", lambda e, pb=pb, kc=kc, tt=tt, nh=nh, src=src: e.matmul(
                            bank(pb), lhsT=src[:, kc % 4, tt * 128:(tt + 1) * 128], rhs=wo[:, kc, nh * 512:(nh + 1) * 512],
                            start=(kc == 0), stop=(kc == 7)),
                            reads=[sres[tt // 4], wo_res], writes=[bankres[pb]])
                    S.add("dve", lambda e, pb=pb, tt=tt, nh=nh: e.tensor_tensor(
                        out=x_sb[:, tt, nh * 512:(nh + 1) * 512], in0=bank(pb), in1=x_sb[:, tt, nh * 512:(nh + 1) * 512],
                        op=ALU.add), reads=[bankres[pb]], writes=[x_res[tt]])
                if stage >= 5:
                    lag4(tt)
            lag4.flush()
            S.barrier()

        if stage >= 5:
            S.tag = 'ffn2'
            ffn("ffn2", do_make=False, next_gain=finn_d, post_tile=final_tile)
            S.barrier()

        S.tag = 'final'
        if stage < 5:
            if stage == 3:
                load_x(xs1_d)
            for tt in range(NT):
                S.add("sp", lambda e, tt=tt: e.dma_start(out=out_d[tt * 128:(tt + 1) * 128, :], in_=x_sb[:, tt, :]),
                      reads=[x_res[tt]], pwrites=[out_res], kind="d")
        S.barrier()
        S.emit(st)
    return nc


def _prep_inputs(inputs):
    f = lambda a: np.ascontiguousarray(np.asarray(a, dtype=np.float32))
    x = f(inputs["x"])
    shared = {}
    for k in ("ffn1_w_gate", "ffn1_w_up", "ffn1_w_down", "ffn2_w_gate", "ffn2_w_up", "ffn2_w_down", "w_in", "w_out",
              "conv_w"):
        shared[k] = f(inputs[k])[0]
    for k in ("ffn1_norm", "ffn2_norm", "mix_norm", "conv_b", "sb_out_norm", "conv_out_norm"):
        shared[k] = f(inputs[k]).reshape(1, -1)
    shared["final_norm"] = f(inputs["final_norm"]).reshape(1, -1)
    in_maps = []
    for c in range(8):
        b, j = c // 4, c % 4
        m = dict(shared)
        m["x"] = np.ascontiguousarray(x[b, j * NTOK:(j + 1) * NTOK, :])
        m["cid"] = np.array([[j]], dtype=np.int32)
        sel = np.zeros((128, 4), dtype=np.float32)
        if j > 0:
            sel[:, j - 1] = 1.0
        m["sel"] = sel
        in_maps.append(m)
    return in_maps


_NC_CACHE = {}


def kernel(**inputs):
    stage = 99
    if stage not in _NC_CACHE:
        _NC_CACHE[stage] = build_program(stage)
    nc = _NC_CACHE[stage]
    in_maps = _prep_inputs(inputs)
    res = run_bass_kernel_spmd(nc, in_maps, core_ids=list(range(8)))
    out = np.empty((2, SEQ, D), dtype=np.float32)
    for c in range(8):
        b, j = c // 4, c % 4
        out[b, j * NTOK:(j + 1) * NTOK, :] = res.results[c]["out"]
    return out
```

```python
import numpy as np
PAIR_SP = True
PAIR_A = True
KSKIP = set()
from contextlib import ExitStack
import concourse.bass as bass
import concourse.mybir as mybir
from concourse.bass_utils import run_bass_kernel_spmd

F32 = mybir.dt.float32
BF16 = mybir.dt.bfloat16
I32 = mybir.dt.int32
AF = mybir.ActivationFunctionType
ALU = mybir.AluOpType

D = 1024
DFF = 2816
NTOK = 2048
NT = 16
NS = 4
SEQ = 8192
EPS = 1e-6
GROUPS = [[0, 1, 2, 3], [4, 5, 6, 7]]


class Res:
    __slots__ = ("writers", "readers")

    def __init__(self):
        self.writers = []
        self.readers = []


class Sched:
    ENGS = ("pe", "act", "dve", "pool", "sp")

    def __init__(self, nc, n_dma_sems=12):
        self.nc = nc
        self.ops = []
        self.queues = {e: [] for e in self.ENGS}
        self.n_dma_sems = n_dma_sems
        self.epoch = 0
        self.pending_dma = []
        self.last = {e: None for e in self.ENGS}
        self.tag = ""

    def add(self, eng, fn, reads=(), writes=(), deps=(), kind="c", pwrites=()):
        i = len(self.ops)
        if self.tag in KSKIP:
            fn = lambda eng: eng.nop()
            kind = "c"
        dl = set()
        for d in deps:
            if d is not None:
                dl.add(d)
        for r in reads:
            dl.update(r.writers)
        for w in writes:
            dl.update(w.writers)
            dl.update(w.readers)
        for w in pwrites:
            dl.update(w.readers)
        self.ops.append([eng, fn, sorted(dl), kind, self.epoch])
        self.queues[eng].append(i)
        for r in reads:
            r.readers.append(i)
        for w in writes:
            w.writers = [i]
            w.readers = []
        for w in pwrites:
            if w.readers:
                w.writers = []
                w.readers = []
            w.writers.append(i)
        if kind != "c":
            self.pending_dma.append(i)
        else:
            self.last[eng] = i
        return i

    def barrier(self):
        deps = [v for v in self.last.values() if v is not None] + list(self.pending_dma)
        self.pending_dma = []
        for e in self.ENGS:
            self.add(e, lambda eng: eng.nop(), deps=deps)
        self.epoch += 1

    def check(self):
        done = set()
        ptr = {e: 0 for e in self.ENGS}
        total = len(self.ops)
        while len(done) < total:
            progressed = False
            for e in self.ENGS:
                q = self.queues[e]
                while ptr[e] < len(q):
                    i = q[ptr[e]]
                    if all(d in done for d in self.ops[i][2]):
                        done.add(i)
                        ptr[e] += 1
                        progressed = True
                    else:
                        break
            if not progressed:
                stuck = {e: self.queues[e][ptr[e]] for e in self.ENGS if ptr[e] < len(self.queues[e])}
                raise RuntimeError(f"schedule deadlock; stuck ops {stuck}")

    def emit(self, stack):
        nc = self.nc
        self.check()
        needed = set()
        for op in self.ops:
            needed.update(op[2])
        nep = self.epoch + 1
        csem = {(e, ep): stack.enter_context(nc.semaphore(f"c_{e}{ep}"))
                for e in self.ENGS for ep in range(nep)}
        dsem = {e: [stack.enter_context(nc.semaphore(f"d_{e}{k}")) for k in range(self.n_dma_sems)]
                for e in ("sp", "pool")}
        ncc = sum(1 for op in self.ops if op[3] == "cc")
        ccsems = [stack.enter_context(nc.semaphore(f"cc{k}")) for k in range(ncc)]
        sig = {}
        pre_wait = {}
        cccount = 0
        for e in self.ENGS:
            ccount = {}
            dcount = [0] * self.n_dma_sems
            rr = 0
            for i in self.queues[e]:
                _e, _f, _deps, kind, ep = self.ops[i]
                if kind == "d":
                    k = rr % self.n_dma_sems
                    rr += 1
                    s = dsem[e][k]
                    if dcount[k] > 0:
                        pre_wait[i] = (s, dcount[k])
                    dcount[k] += 16
                    sig[i] = (s, dcount[k])
                elif kind == "cc":
                    sig[i] = (ccsems[cccount], 1)
                    cccount += 1
                elif i in needed:
                    ccount[ep] = ccount.get(ep, 0) + 1
                    sig[i] = (csem[(e, ep)], ccount[ep])
        block = stack.enter_context(nc.Block())
        ops = self.ops
        queues = self.queues

        def run_queue(e, eng):
            waited = {}
            for i in queues[e]:
                _e, fn, deps, kind, _ep = ops[i]
                ws = []
                if i in pre_wait:
                    ws.append(pre_wait[i])
                for d in deps:
                    if e == "pe" and ops[d][0] == "pe" and ops[d][3] == "c":
                        continue
                    ws.append(sig[d])
                for (s, v) in ws:
                    key = id(s)
                    if waited.get(key, 0) >= v:
                        continue
                    waited[key] = v
                    eng.wait_ge(s, v)
                ins = fn(eng)
                if i in sig:
                    s, v = sig[i]
                    ins.then_inc(s, 16 if kind == "d" else 1)

        @block.tensor
        def _(eng):
            run_queue("pe", eng)

        @block.scalar
        def _(eng):
            run_queue("act", eng)

        @block.vector
        def _(eng):
            run_queue("dve", eng)

        @block.gpsimd
        def _(eng):
            run_queue("pool", eng)

        @block.sync
        def _(eng):
            run_queue("sp", eng)


def build_program(stage=99):
    nc = bass.Bass("TRN2", target_bir_lowering=False)

    def din(name, shape, dt=F32):
        return nc.dram_tensor(name, list(shape), dt, kind="ExternalInput").ap()

    def dint(name, shape, dt=F32):
        return nc.dram_tensor(name, list(shape), dt, kind="Internal").ap()

    x_d = din("x", [NTOK, D])
    out_d = nc.dram_tensor("out", [NTOK, D], F32, kind="ExternalOutput").ap()
    W = {}
    for f in ("ffn1", "ffn2"):
        W[f + "_norm"] = din(f + "_norm", [1, D])
        W[f + "_w_gate"] = din(f + "_w_gate", [D, DFF])
        W[f + "_w_up"] = din(f + "_w_up", [D, DFF])
        W[f + "_w_down"] = din(f + "_w_down", [DFF, D])
    mixn_d = din("mix_norm", [1, D])
    win_d = din("w_in", [D, 3072])
    convw_d = din("conv_w", [3, 512])
    convb_d = din("conv_b", [1, 512])
    sbn_d = din("sb_out_norm", [1, 512])
    cvn_d = din("conv_out_norm", [1, 512])
    wout_d = din("w_out", [D, D])
    finn_d = din("final_norm", [1, D])
    cid_d = din("cid", [1, 1], I32)
    sel_d = din("sel", [128, 4])

    xs1_d = dint("xs1", [NTOK, D])
    s1a_d = [dint(f"s1a{q}", [4 * 256, 512], BF16) for q in range(4)]
    g1a_d = [dint(f"g1a{q}", [4 * 1024, 512], BF16) for q in range(4)]
    s1v_d = [dint(f"s1v{h}", [4 * 1024, 128], BF16) for h in range(2)]
    g1v_d = [dint(f"g1v{h}", [4 * 4096, 128], BF16) for h in range(2)]
    tin_d = dint("t_in", [128, 8])
    tout_d = dint("t_out", [512, 8])
    s2_d = [dint(f"s2{q}", [128, 2048]) for q in range(4)]
    g2_d = dint("g2", [2048, 2048])

    st = ExitStack()
    with st:
        M = st.enter_context(nc.sbuf_tensor("arena", [128, 50700], F32))
        PS = st.enter_context(nc.psum_tensor("ps", [128, 4096], F32))
        cidreg = st.enter_context(nc.sync.register("cidreg"))
        S = Sched(nc)

        def f32v(off, n):
            return M[:, off:off + n]

        def bfv(off, n):
            return M[:, off:off + n // 2].bitcast(BF16)

        def bank(b):
            return PS[:, b * 512:(b + 1) * 512]

        bankres = [Res() for _ in range(8)]

        OX, OH, OA, OW, OY, OC = 0, 16400, 24592, 32784, 46096, 50192
        x_sb = f32v(OX, 16384).rearrange("p (t d) -> p t d", d=D)
        x_res = [Res() for _ in range(NT)]
        hT = bfv(OH, 16384).rearrange("p (k t) -> p k t", t=NTOK)
        hT_res = [Res() for _ in range(NT)]
        aT = bfv(OA, 6 * 2048).rearrange("p (f t) -> p f t", t=NTOK)
        wgu = [bfv(OW + i * 1024, 2048).rearrange("p (k c) -> p k c", c=256) for i in range(4)]
        wgu_res = [Res() for _ in range(4)]
        wdb = [bfv(OW + 4096 + i * 3072, 6144).rearrange("p (f c) -> p f c", c=D) for i in range(2)]
        wdb_res = [Res() for _ in range(2)]
        xn = [bfv(OW + 10240 + i * 512, 1024) for i in range(2)]
        xn_res = [Res() for _ in range(2)]
        gbc = f32v(OW + 11264, 1024)
        gbc_res = Res()
        sil = [f32v(OW + 12288 + i * 512, 512) for i in range(2)]
        sil_res = [Res() for _ in range(2)]
        ycn = bfv(OY, 8192).rearrange("p (c t) -> p c t", t=NTOK)
        ycn_res = [Res() for _ in range(NS)]
        ident = bfv(OC, 128)
        trineg = bfv(OC + 64, 128)
        onesb = bfv(OC + 128, 128)
        ssq = f32v(OC + 192, 16)
        lnv = f32v(OC + 208, 16)
        rstd = f32v(OC + 224, 16)
        epst = f32v(OC + 240, 1)
        onet = f32v(OC + 241, 1)
        cw = f32v(OC + 244, 12).rearrange("p (i c) -> p i c", c=4)
        cb = f32v(OC + 256, 4)
        gcn = f32v(OC + 260, 4)
        gsn = f32v(OC + 264, 4)
        sel = f32v(OC + 268, 4)
        tails = f32v(OC + 272, 32).rearrange("p (r e) -> p r e", e=8)
        halo = f32v(OC + 304, 8).rearrange("p (c m) -> p c m", m=2)
        const_res = Res()
        stat_res = Res()

        def mk_consts():
            S.add("pool", lambda e: e.memset(ident, 1.0), writes=[const_res])
            S.add("pool", lambda e: e.affine_select(out=ident, in_=ident, pattern=[[-1, 128]],
                                                    compare_op=ALU.is_equal, fill=0.0, base=0,
                                                    channel_multiplier=1), writes=[const_res])
            S.add("pool", lambda e: e.memset(trineg, -1.0), writes=[const_res])
            S.add("pool", lambda e: e.affine_select(out=trineg, in_=trineg, pattern=[[-1, 128]],
                                                    compare_op=ALU.is_ge, fill=0.0, base=0,
                                                    channel_multiplier=1), writes=[const_res])
            S.add("pool", lambda e: e.memset(onesb, 1.0), writes=[const_res])
            S.add("pool", lambda e: e.memset(epst, EPS), writes=[const_res])
            S.add("pool", lambda e: e.memset(onet, 1.0), writes=[const_res])
            for i in range(3):
                S.add("sp", lambda e, i=i: e.dma_start(
                    out=cw[:, i, :], in_=convw_d[i, :].rearrange("(c p) -> p c", p=128),
                    allow_slow_non_contiguous=True), pwrites=[const_res], kind="d")
            for (dst, src) in ((cb, convb_d), (gcn, cvn_d), (gsn, sbn_d)):
                S.add("sp", lambda e, dst=dst, src=src: e.dma_start(
                    out=dst, in_=src[0, :].rearrange("(c p) -> p c", p=128),
                    allow_slow_non_contiguous=True), pwrites=[const_res], kind="d")
            S.add("sp", lambda e: e.dma_start(out=sel, in_=sel_d), pwrites=[const_res], kind="d")

        def load_x(src, tts=range(NT)):
            for tt in tts:
                S.add("sp", lambda e, tt=tt: e.dma_start(out=x_sb[:, tt, :], in_=src[tt * 128:(tt + 1) * 128, :]),
                      writes=[x_res[tt]], kind="d")

        def tm_stats(junk_bufs, junk_res, tts=range(NT)):
            t0_, t1_ = tts[0], tts[-1] + 1
            for tt in tts:
                S.add("act", lambda e, tt=tt: e.activation(out=junk_bufs[tt % 2], in_=x_sb[:, tt, :], func=AF.Square,
                                                           accum_out=ssq[:, tt:tt + 1]),
                      reads=[x_res[tt]], writes=[junk_res[tt % 2], stat_res])
            S.add("act", lambda e: e.activation(out=lnv[:, t0_:t1_], in_=ssq[:, t0_:t1_], func=AF.Ln, bias=epst,
                                                scale=1.0 / D),
                  reads=[const_res], writes=[stat_res])
            S.add("act", lambda e: e.activation(out=rstd[:, t0_:t1_], in_=lnv[:, t0_:t1_], func=AF.Exp, scale=-0.5),
                  writes=[stat_res])

        def load_gbc(g_d):
            S.add("sp", lambda e: e.dma_start(out=gbc, in_=g_d.partition_broadcast(128)), writes=[gbc_res], kind="d")

        def make_hT(g_d, tts=range(NT)):
            load_gbc(g_d)
            tm_stats(xn, xn_res, tts)
            for tt in tts:
                b = tt % 2
                S.add("dve", lambda e, tt=tt, b=b: e.scalar_tensor_tensor(
                    out=xn[b], in0=x_sb[:, tt, :], scalar=rstd[:, tt:tt + 1], in1=gbc, op0=ALU.mult, op1=ALU.mult),
                    reads=[x_res[tt], stat_res, gbc_res], writes=[xn_res[b]])
                tp = bank(b).bitcast(BF16)
                for kc in range(8):
                    S.add("pe", lambda e, kc=kc, b=b, tp=tp: e.transpose(
                        out=tp[:, kc * 128:(kc + 1) * 128], in_=xn[b][:, kc * 128:(kc + 1) * 128], identity=ident),
                        reads=[xn_res[b], const_res], writes=[bankres[b]])
                S.add("act", lambda e, tt=tt, tp=tp: e.activation(
                    out=hT[:, :, tt * 128:(tt + 1) * 128], in_=tp.rearrange("p (k t) -> p k t", t=128), func=AF.Copy),
                    reads=[bankres[b]], writes=[hT_res[tt]])

        tstat = [Res() for _ in range(NT)]
        junkb = [bfv(OW + 12288, 1024), bfv(OW + 12288, 1024)]
        xn3 = [xn[0], xn[1], bfv(OW + 12288 + 512, 1024)]
        xn3_res = [xn_res[0], xn_res[1], sil_res[1]]

        def tile_stats(tt):
            j = 0
            S.add("act", lambda e: e.activation(out=junkb[j], in_=x_sb[:, tt, :], func=AF.Square,
                                                accum_out=ssq[:, tt:tt + 1]),
                  reads=[x_res[tt]], writes=[sil_res[j], tstat[tt]])
            S.add("act", lambda e: e.activation(out=lnv[:, tt:tt + 1], in_=ssq[:, tt:tt + 1], func=AF.Ln, bias=epst,
                                                scale=1.0 / D),
                  reads=[const_res], writes=[tstat[tt]])
            S.add("act", lambda e: e.activation(out=rstd[:, tt:tt + 1], in_=lnv[:, tt:tt + 1], func=AF.Exp, scale=-0.5),
                  writes=[tstat[tt]])

        def hT_p1(tt):
            tile_stats(tt)
            b3 = tt % 3
            S.add("dve", lambda e: e.scalar_tensor_tensor(
                out=xn3[b3], in0=x_sb[:, tt, :], scalar=rstd[:, tt:tt + 1], in1=gbc, op0=ALU.mult, op1=ALU.mult),
                reads=[x_res[tt], tstat[tt], gbc_res], writes=[xn3_res[b3]])

        def hT_p2(tt, extra_deps=()):
            b = tt % 2
            b3 = tt % 3
            tp = bank(b).bitcast(BF16)
            for kc in range(8):
                S.add("pe", lambda e, kc=kc: e.transpose(
                    out=tp[:, kc * 128:(kc + 1) * 128], in_=xn3[b3][:, kc * 128:(kc + 1) * 128], identity=ident),
                    reads=[xn3_res[b3], const_res], writes=[bankres[b]])
            S.add("act", lambda e: e.activation(
                out=hT[:, :, tt * 128:(tt + 1) * 128], in_=tp.rearrange("p (k t) -> p k t", t=128), func=AF.Copy),
                reads=[bankres[b]], writes=[hT_res[tt]], deps=extra_deps)

        class Lagged:
            def __init__(self, extra_deps=()):
                self.q = []
                self.extra = extra_deps

            def __call__(self, tt):
                if len(self.q) == 2:
                    hT_p2(self.q.pop(0), self.extra)
                hT_p1(tt)
                self.q.append(tt)

            def flush(self):
                while self.q:
                    hT_p2(self.q.pop(0), self.extra)

        ost = [f32v(OY + i * 1024, 1024) for i in range(2)]
        ost_res = [Res(), Res()]
        out_res = Res()

        def final_tile(tt):
            tile_stats(tt)
            b = tt % 2
            S.add("dve", lambda e: e.scalar_tensor_tensor(
                out=ost[b], in0=x_sb[:, tt, :], scalar=rstd[:, tt:tt + 1], in1=gbc, op0=ALU.mult, op1=ALU.mult),
                reads=[x_res[tt], tstat[tt], gbc_res], writes=[ost_res[b]])
            S.add("sp", lambda e: e.dma_start(out=out_d[tt * 128:(tt + 1) * 128, :], in_=ost[b]),
                  reads=[ost_res[b]], pwrites=[out_res], kind="d")

        def ffn(pref, tts=range(NT), hook=None, do_make=True, next_gain=None, post_tile=None):
            wg_d, wu_d, wd_d = W[pref + "_w_gate"], W[pref + "_w_up"], W[pref + "_w_down"]
            if do_make:
                make_hT(W[pref + "_norm"], tts)
            if next_gain is not None:
                load_gbc(next_gain)
            tss = range(tts[0] // 4, tts[-1] // 4 + 1)
            aT_res = [[Res() for _ in range(NS)] for _ in range(6)]
            quarters = [(0, 6), (6, 6), (12, 6), (18, 4)]
            blk = 0
            mmi = 0
            for qi, (f0, nf) in enumerate(quarters):
                wq = qi % 2
                S.add("pool", lambda e, wq=wq, f0=f0, nf=nf: e.dma_start(
                    out=wdb[wq][:, 0:nf, :], in_=wd_d[f0 * 128:(f0 + nf) * 128, :].rearrange("(f p) c -> p f c", p=128)),
                    writes=[wdb_res[wq]], kind="d")
                for pi in range(nf // 2):
                    c0 = (f0 + 2 * pi) * 128
                    bg, bu = (2 * blk) % 4, (2 * blk + 1) % 4
                    blk += 1
                    for (bb, wsrc) in ((bg, wg_d), (bu, wu_d)):
                        S.add("pool", lambda e, bb=bb, wsrc=wsrc, c0=c0: e.dma_start(
                            out=wgu[bb], in_=wsrc[:, c0:c0 + 256].rearrange("(k p) c -> p k c", p=128)),
                            writes=[wgu_res[bb]], kind="d")
                    if hook and (blk % 2 == 0) and (blk // 2 - 1) < len(hook) and blk >= 2:
                        hook[blk // 2 - 1]()
                    for fi in range(2):
                        fl = 2 * pi + fi
                        for ts in tss:
                            gb, ub = 2 + (mmi % 2), 4 + (mmi % 2)
                            sb_ = mmi % 2
                            mmi += 1
                            for (pb, wb) in ((gb, bg), (ub, bu)):
                                for kc in range(8):
                                    S.add("pe", lambda e, pb=pb, wb=wb, kc=kc, fi=fi, ts=ts: e.matmul(
                                        bank(pb), lhsT=wgu[wb][:, kc, fi * 128:(fi + 1) * 128],
                                        rhs=hT[:, kc, ts * 512:(ts + 1) * 512], start=(kc == 0), stop=(kc == 7)),
                                        reads=[wgu_res[wb]] + hT_res[4 * ts:4 * ts + 4], writes=[bankres[pb]])
                            S.add("act", lambda e, gb=gb, sb_=sb_: e.activation(out=sil[sb_], in_=bank(gb), func=AF.Silu),
                                  reads=[bankres[gb]], writes=[sil_res[sb_]])
                            S.add("dve", lambda e, ub=ub, sb_=sb_, fl=fl, ts=ts: e.tensor_tensor(
                                out=aT[:, fl, ts * 512:(ts + 1) * 512], in0=bank(ub), in1=sil[sb_], op=ALU.mult),
                                reads=[bankres[ub], sil_res[sb_]], writes=[aT_res[fl][ts]])
                for tt in tts:
                    for nh in range(2):
                        db = 6 + ((tt * 2 + nh) % 2)
                        for fl in range(nf):
                            S.add("pe", lambda e, db=db, fl=fl, tt=tt, nh=nh, wq=wq, nf=nf: e.matmul(
                                bank(db), lhsT=aT[:, fl, tt * 128:(tt + 1) * 128],
                                rhs=wdb[wq][:, fl, nh * 512:(nh + 1) * 512], start=(fl == 0), stop=(fl == nf - 1)),
                                reads=[aT_res[fl][tt // 4], wdb_res[wq]], writes=[bankres[db]])
                        S.add("dve", lambda e, db=db, tt=tt, nh=nh: e.scalar_tensor_tensor(
                            out=x_sb[:, tt, nh * 512:(nh + 1) * 512], in0=bank(db), scalar=0.5,
                            in1=x_sb[:, tt, nh * 512:(nh + 1) * 512], op0=ALU.mult, op1=ALU.add),
                            reads=[bankres[db]], writes=[x_res[tt]])
                    if post_tile is not None and qi == len(quarters) - 1:
                        post_tile(tt)
                if post_tile is not None and qi == len(quarters) - 1 and hasattr(post_tile, "flush"):
                    post_tile.flush()

        vals = {}

        def rl(e):
            ins = e.reg_load(cidreg, cid_d[0:1, 0:1])
            vals["v"] = e.snap(cidreg)
            return ins
        S.add("sp", rl)
        mk_consts()
        HALVES = [range(0, 8), range(8, 16)]
        load_x(x_d, HALVES[0])
        load_x(x_d, HALVES[1])

        Bsb = f32v(OX, 8192).rearrange("p (c t) -> p c t", t=NTOK)
        xc = f32v(OX + 8192, 8200).rearrange("p (c t) -> p c t", t=NTOK + 2)
        bx_res = Res()
        stq = [bfv(OY + i * 1024, 2048) for i in range(2)]
        stq_res = [Res(), Res()]
        stv = [bfv(OY + 2048 + i * 128, 256) for i in range(2)]
        stv_res = [Res(), Res()]
        s1a_res, s1v_res, tin_res = Res(), Res(), Res()
        g1a_res, g1v_res, tout_res = Res(), Res(), Res()
        xs1_res = Res()

        def allgather(src, dst, rres, wres):
            S.add("pool", lambda e: e.collective_compute("AllGather", ALU.bypass, replica_groups=GROUPS,
                                                         ins=[src], outs=[dst]),
                  reads=[rres], pwrites=[wres], kind="cc")
        pj = {"mmi": 0, "sqi": 0, "svi": 0, "wbi": 0}

        def proj_blocks(blks, tts, hook=None, hook_after=2, drain=None):
            tss = range(tts[0] // 4, tts[-1] // 4 + 1)
            for bi_, blkI in enumerate(blks):
                if drain and bi_ > 0:
                    for _ in range(4):
                        if drain:
                            drain.pop(0)()
                wb = pj["wbi"] % 4
                pj["wbi"] += 1
                S.add("pool", lambda e, wb=wb, blkI=blkI: e.dma_start(
                    out=wgu[wb], in_=win_d[:, blkI * 256:(blkI + 1) * 256].rearrange("(k p) c -> p k c", p=128)),
                    writes=[wgu_res[wb]], kind="d")
                if hook and bi_ == 3:
                    hook[0]()
                if hook and bi_ == len(blks) - 1:
                    for hk_ in hook[1:]:
                        hk_()
                kindb = blkI // 2
                if kindb == 2:
                    for tt in tts:
                        pb = 2 + (pj["mmi"] % 4)
                        pj["mmi"] += 1
                        for kc in range(8):
                            S.add("pe", lambda e, pb=pb, wb=wb, kc=kc, tt=tt: e.matmul(
                                bank(pb)[:, 0:256], lhsT=hT[:, kc, tt * 128:(tt + 1) * 128], rhs=wgu[wb][:, kc, :],
                                start=(kc == 0), stop=(kc == 7)),
                                reads=[wgu_res[wb], hT_res[tt]], writes=[bankres[pb]])
                        sv = pj["svi"] % 2
                        pj["svi"] += 1
                        S.add("act", lambda e, pb=pb, sv=sv: e.activation(out=stv[sv], in_=bank(pb)[:, 0:256], func=AF.Copy),
                              reads=[bankres[pb]], writes=[stv_res[sv]])
                        for hh in range(2):
                            hp = (blkI - 4) * 2 + hh
                            S.add("sp", lambda e, sv=sv, hh=hh, hp=hp, tt=tt: e.dma_start(
                                out=s1v_d[tt // 8][hp * 1024:(hp + 1) * 1024, :].rearrange("(p b) f -> p b f", b=8)[:, tt % 8, :],
                                in_=stv[sv][:, hh * 128:(hh + 1) * 128]),
                                reads=[stv_res[sv]], pwrites=[s1v_res], kind="d")
                    continue
                for ci in range(2):
                    chunk = (blkI % 2) * 2 + ci
                    sq = pj["sqi"] % 2
                    if kindb < 2:
                        pj["sqi"] += 1
                    for ts in tss:
                        pb = 2 + (pj["mmi"] % 4)
                        pj["mmi"] += 1
                        for kc in range(8):
                            S.add("pe", lambda e, pb=pb, wb=wb, kc=kc, ci=ci, ts=ts: e.matmul(
                                bank(pb), lhsT=wgu[wb][:, kc, ci * 128:(ci + 1) * 128],
                                rhs=hT[:, kc, ts * 512:(ts + 1) * 512], start=(kc == 0), stop=(kc == 7)),
                                reads=[wgu_res[wb]] + hT_res[4 * ts:4 * ts + 4], writes=[bankres[pb]])
                        tsl = slice(ts * 512, (ts + 1) * 512)
                        if kindb == 0:
                            S.add("act", lambda e, pb=pb, sq=sq, tsl=tsl: e.activation(
                                out=stq[sq][:, tsl], in_=bank(pb), func=AF.Copy, scale=0.125),
                                reads=[bankres[pb]], writes=[stq_res[sq]])
                        elif kindb == 1:
                            S.add("act", lambda e, pb=pb, sq=sq, tsl=tsl: e.activation(
                                out=stq[sq][:, tsl], in_=bank(pb), func=AF.Copy),
                                reads=[bankres[pb]], writes=[stq_res[sq]])
                        elif kindb == 3:
                            S.add("act", lambda e, pb=pb, chunk=chunk, tsl=tsl: e.activation(
                                out=Bsb[:, chunk, tsl], in_=bank(pb), func=AF.Copy),
                                reads=[bankres[pb], xs1_res], pwrites=[bx_res])
                        elif kindb == 4:
                            S.add("act", lambda e, pb=pb, chunk=chunk, ts=ts: e.activation(
                                out=xc[:, chunk, 2 + ts * 512: 2 + (ts + 1) * 512], in_=bank(pb), func=AF.Copy),
                                reads=[bankres[pb], xs1_res], pwrites=[bx_res])
                        else:
                            S.add("dve", lambda e, pb=pb, chunk=chunk, ts=ts: e.tensor_tensor(
                                out=xc[:, chunk, 2 + ts * 512: 2 + (ts + 1) * 512], in0=bank(pb),
                                in1=xc[:, chunk, 2 + ts * 512: 2 + (ts + 1) * 512], op=ALU.mult),
                                reads=[bankres[pb], bx_res], pwrites=[bx_res])
                    if kindb < 2:
                        r0 = chunk * 256 + kindb * 128
                        for q4 in tss:
                            S.add("sp", lambda e, sq=sq, r0=r0, q4=q4: e.dma_start(
                                out=s1a_d[q4][r0:r0 + 128, :], in_=stq[sq][:, q4 * 512:(q4 + 1) * 512]),
                                reads=[stq_res[sq]], pwrites=[s1a_res], kind="d")

        def collectives_for(hi):
            return [lambda: allgather(s1a_d[2 * hi], g1a_d[2 * hi], s1a_res, g1a_res),
                    lambda: allgather(s1v_d[hi], g1v_d[hi], s1v_res, g1v_res),
                    lambda: allgather(s1a_d[2 * hi + 1], g1a_d[2 * hi + 1], s1a_res, g1a_res)]

        pend = [None]
        deferred = []
        for hi, tts in enumerate(HALVES):
            hk = pend[0]
            pend[0] = None
            ffn("ffn1", tts, hook=hk, do_make=(hi == 0), next_gain=mixn_d,
                post_tile=Lagged() if stage >= 2 else None)
            if stage >= 2:
                S.tag = 'proj'
                for tt in tts:
                    S.add("sp", lambda e, tt=tt: e.dma_start(out=xs1_d[tt * 128:(tt + 1) * 128, :], in_=x_sb[:, tt, :]),
                          reads=[x_res[tt]], pwrites=[xs1_res], kind="d")
                if hi == 0:
                    load_gbc(W["ffn1_norm"])
                    tb_ = list(HALVES[1])
                    deferred.append(lambda: hT_p1(tb_[0]))
                    deferred.append(lambda: hT_p1(tb_[1]))
                    for k_ in range(2, len(tb_)):
                        deferred.append(lambda k_=k_: hT_p2(tb_[k_ - 2]))
                        deferred.append(lambda k_=k_: hT_p1(tb_[k_]))
                    deferred.append(lambda: hT_p2(tb_[-2]))
                    deferred.append(lambda: hT_p2(tb_[-1]))
                proj_blocks(range(0, 6), tts, drain=deferred)
                while deferred:
                    deferred.pop(0)()
                pend[0] = collectives_for(hi)
                S.tag = ''
            elif hi == 0:
                make_hT(W["ffn1_norm"], HALVES[1])

        if stage >= 2:
            S.tag = 'proj'
            proj_blocks(range(6, 12), range(NT), hook=pend[0], hook_after=2)
            S.add("sp", lambda e: e.dma_start(out=tin_d.rearrange("p (c m) -> p c m", m=2), in_=xc[:, :, NTOK:NTOK + 2]),
                  reads=[bx_res], writes=[tin_res], kind="d")
            allgather(tin_d, tout_d, tin_res, tout_res)

            S.tag = 'attload'
            KQ = bfv(OH, 16384).rearrange("p (w c) -> p w c", w=2)
            Qp = KQ[:, 0, :]
            KT = KQ[:, 1, :]
            Vs = bfv(OA, 8192).rearrange("p (b f) -> p b f", f=128)
            QA = bfv(OA + 4096, 8192)
            QB = bfv(OA + 8192, 8192)
            att_in = Res()
            kq_res = [Res() for _ in range(4)]
            ht_dead = sorted(set(o for r_ in hT_res for o in (r_.readers + r_.writers)))
            S.add("pool", lambda e: e.memset(QA[64:128, :], 0.0), pwrites=[att_in])
            S.add("pool", lambda e: e.memset(QB[0:64, :], 0.0), pwrites=[att_in])
            for i in range(4):
                cs = slice(i * 2048, (i + 1) * 2048)
                for q4 in range(4):
                    S.add("sp", lambda e, i=i, q4=q4: e.dma_start(
                        out=KQ[:, :, i * 2048 + q4 * 512: i * 2048 + (q4 + 1) * 512],
                        in_=g1a_d[q4][bass.ds(vals["v"] * 256 + i * 1024, 256), :].rearrange("(w p) c -> p w c", p=128)),
                        reads=[g1a_res], pwrites=[kq_res[i]], deps=ht_dead, kind="d")
                for h2 in range(2):
                    S.add("sp", lambda e, i=i, h2=h2: e.dma_start(
                        out=Vs[:, i * 16 + h2 * 8: i * 16 + h2 * 8 + 8, :],
                        in_=g1v_d[h2][bass.ds(vals["v"] * 1024 + i * 4096, 1024), :].rearrange("(p b) f -> p b f", b=8)),
                        reads=[g1v_res], pwrites=[att_in], kind="d")
                S.add("sp", lambda e, cs=cs: e.dma_start(out=QA[0:64, cs], in_=Qp[0:64, cs]),
                      reads=[kq_res[i]], pwrites=[att_in], kind="d")
                S.add("sp", lambda e, cs=cs: e.dma_start(out=QB[64:128, cs], in_=Qp[64:128, cs]),
                      reads=[kq_res[i]], pwrites=[att_in], kind="d")
            tails_res = Res()
            S.add("sp", lambda e: e.dma_start(out=tails, in_=tout_d.rearrange("(r p) e -> p r e", p=128)),
                  reads=[tout_res], writes=[tails_res], kind="d")

            S.tag = 'conv'
            accb = [f32v(OW + 4096 + i * 512, 512) for i in range(2)]
            acc_res = [Res(), Res()]
            sqb = [bfv(OW + 4096 + 1024 + i * 256, 512) for i in range(2)]
            sqb_res = [Res(), Res()]
            lnb = f32v(OW + 4096 + 1536, 512)
            rsb = f32v(OW + 4096 + 2048, 512)
            lnb_res, rsb_res = Res(), Res()
            stage_dmas = sorted(set(s1a_res.writers + s1v_res.writers))
            haloflat = f32v(OC + 304, 8)
            halo_res = Res()

            def halo_ops():
                S.add("dve", lambda e: e.tensor_scalar(out=haloflat, in0=tails[:, 0, :], scalar1=sel[:, 0:1], scalar2=None,
                                                       op0=ALU.mult), reads=[const_res, tails_res], writes=[stat_res])
                for r in range(1, 4):
                    S.add("dve", lambda e, r=r: e.scalar_tensor_tensor(
                        out=haloflat, in0=tails[:, r, :], scalar=sel[:, r:r + 1], in1=haloflat, op0=ALU.mult, op1=ALU.add),
                        reads=[const_res, tails_res], writes=[stat_res])
                S.add("dve", lambda e: e.tensor_copy(xc[:, :, 0:2], halo), reads=[stat_res], writes=[halo_res])
            yc_res = [[Res() for _ in range(NS)] for _ in range(4)]
            ci_ = 0
            for ts in (1, 2, 3, 0):
                if ts == 0:
                    halo_ops()
                for cc in range(4):
                    a = ci_ % 2
                    ci_ += 1
                    t0 = ts * 512
                    S.add("dve", lambda e, a=a, cc=cc, t0=t0: e.tensor_scalar(
                        out=accb[a], in0=xc[:, cc, t0 + 2:t0 + 514], scalar1=cw[:, 2, cc:cc + 1],
                        scalar2=cb[:, cc:cc + 1], op0=ALU.mult, op1=ALU.add),
                        reads=[bx_res, const_res], writes=[acc_res[a]])
                    S.add("dve", lambda e, a=a, cc=cc, t0=t0: e.scalar_tensor_tensor(
                        out=accb[a], in0=xc[:, cc, t0 + 1:t0 + 513], scalar=cw[:, 1, cc:cc + 1], in1=accb[a],
                        op0=ALU.mult, op1=ALU.add), reads=[bx_res, halo_res], writes=[acc_res[a]])
                    S.add("dve", lambda e, a=a, cc=cc, t0=t0: e.scalar_tensor_tensor(
                        out=accb[a], in0=xc[:, cc, t0:t0 + 512], scalar=cw[:, 0, cc:cc + 1], in1=accb[a],
                        op0=ALU.mult, op1=ALU.add), reads=[bx_res, halo_res], writes=[acc_res[a]])
                    S.add("pool", lambda e, a=a, cc=cc, t0=t0: e.tensor_tensor(
                        out=Bsb[:, cc, t0:t0 + 512], in0=Bsb[:, cc, t0:t0 + 512], in1=accb[a], op=ALU.mult),
                        reads=[acc_res[a], bx_res], writes=[yc_res[cc][ts]])

            def fm_norm(src, src_res, gain, dst, dst_res, nb):
                k = 0
                for ts in range(NS):
                    tsl = slice(ts * 512, (ts + 1) * 512)
                    for cc in range(4):
                        q = k % 2
                        k += 1
                        S.add("act", lambda e, q=q, cc=cc, tsl=tsl: e.activation(out=sqb[q], in_=src[:, cc, tsl], func=AF.Square),
                              reads=[src_res[cc][ts]], writes=[sqb_res[q]])
                        S.add("pe", lambda e, q=q, cc=cc: e.matmul(bank(nb), lhsT=onesb, rhs=sqb[q],
                                                                   start=(cc == 0), stop=(cc == 3)),
                              reads=[sqb_res[q], const_res], writes=[bankres[nb]])
                    S.add("act", lambda e: e.activation(out=lnb, in_=bank(nb), func=AF.Ln, bias=epst, scale=1.0 / 512),
                          reads=[bankres[nb], const_res], writes=[lnb_res])
                    S.add("act", lambda e: e.activation(out=rsb, in_=lnb, func=AF.Exp, scale=-0.5),
                          reads=[lnb_res], writes=[rsb_res])
                    for cc in range(4):
                        S.add("dve", lambda e, cc=cc, tsl=tsl: e.scalar_tensor_tensor(
                            out=dst[:, cc, tsl], in0=src[:, cc, tsl], scalar=gain[:, cc:cc + 1], in1=rsb,
                            op0=ALU.mult, op1=ALU.mult),
                            reads=[src_res[cc][ts], rsb_res, const_res], writes=[dst_res[ts]])

            S.add("dve", lambda e: e.nop(), deps=stage_dmas)
            fm_norm(Bsb, yc_res, gcn, ycn, ycn_res, 6)
            S.barrier()

        if stage >= 3:
            S.tag = 'att'
            load_x(xs1_d)
            OB = OW + 4096
            Ebuf = [f32v(OB + i * 1024, 1024).rearrange("p (h t) -> p h t", h=2) for i in range(2)]
            Tbuf = [f32v(OB + 2048 + i * 1024, 1024).rearrange("p (h t) -> p h t", h=2) for i in range(3)]
            SPb = [bfv(OB + 5120 + i * 512, 1024).rearrange("p (h t) -> p h t", h=2) for i in range(2)]
            Abf = [bfv(OB + 6144 + i * 512, 1024).rearrange("p (h t) -> p h t", h=2) for i in range(2)]
            carry = [f32v(OB + 7168 + i * 512, 512) for i in range(2)]
            ybuf = [f32v(OB + 8192 + i * 512, 512) for i in range(2)]
            E_res = [Res(), Res()]
            T_res = [Res(), Res(), Res()]
            SP_res = [Res(), Res()]
            A_res = [Res(), Res()]
            carry_res = [Res(), Res()]
            ybuf_res = [Res(), Res()]
            s2_res = [Res() for _ in range(4)]
            g2_res = Res()
            Qh = [QA, QB]
            NQS = SEQ // 512
            pairs = []
            for qs in range(NQS):
                nkb = 4 * qs + 4
                for r in range(nkb):
                    pairs.append((qs, r, nkb))
            NP = len(pairs)

            def need_mask(qs, kb):
                return kb >= 4 * qs

            def mask_pair(buf, res, qs, kb):
                base = 512 * qs - 128 * kb
                for h in range(2):
                    S.add("pool", lambda e, h=h: e.affine_select(out=buf[:, h, :], in_=buf[:, h, :], pattern=[[1, 512]],
                                                                 compare_op=ALU.is_gt, fill=0.0, base=base,
                                                                 channel_multiplier=-1),
                          writes=[res])

            def st_Z(p):
                qs, r, nkb = pairs[p]
                kb = nkb - 1 - r
                for h in range(2):
                    S.add("pe", lambda e, h=h: e.matmul(bank(h), lhsT=KT[:, kb * 128:(kb + 1) * 128],
                                                        rhs=Qh[h][:, qs * 512:(qs + 1) * 512], start=True, stop=True),
                          reads=[att_in], writes=[bankres[h]])

            def st_E(p):
                pp = p % 2
                for h in range(2):
                    S.add("act", lambda e, h=h: e.activation(out=Ebuf[pp][:, h, :], in_=bank(h), func=AF.Exp),
                          reads=[bankres[h]], pwrites=[E_res[pp]])

            def st_SP(p):
                qs, r, nkb = pairs[p]
                kb = nkb - 1 - r
                pp = p % 2
                if PAIR_SP:
                    S.add("act", lambda e: e.activation(out=SPb[pp], in_=Ebuf[pp], func=AF.Ln, bias=1.0),
                          reads=[E_res[pp], const_res], writes=[SP_res[pp]])
                else:
                    for h in range(2):
                        S.add("act", lambda e, h=h: e.activation(out=SPb[pp][:, h, :], in_=Ebuf[pp][:, h, :], func=AF.Ln, bias=onet),
                              reads=[E_res[pp], const_res], writes=[SP_res[pp]])
                if need_mask(qs, kb):
                    mask_pair(SPb[pp], SP_res[pp], qs, kb)

            def st_LC(p):
                qs, r, nkb = pairs[p]
                kb = nkb - 1 - r
                pp = p % 2
                for h in range(2):
                    lb, cbk = 2 + h, 4 + h
                    S.add("pe", lambda e, h=h, lb=lb: e.matmul(bank(lb), lhsT=KT[:, kb * 128:(kb + 1) * 128],
                                                               rhs=Qh[h][:, qs * 512:(qs + 1) * 512], start=True, stop=False),
                          reads=[att_in], writes=[bankres[lb]])
                    S.add("pe", lambda e, h=h, lb=lb: e.matmul(bank(lb), lhsT=trineg, rhs=SPb[pp][:, h, :],
                                                               start=False, stop=True),
                          reads=[SP_res[pp], const_res], writes=[bankres[lb]])
                    if r < nkb - 1:
                        S.add("pe", lambda e, h=h, cbk=cbk: e.matmul(bank(cbk), lhsT=onesb, rhs=SPb[pp][:, h, :],
                                                                     start=True, stop=True),
                              reads=[SP_res[pp], const_res], writes=[bankres[cbk]])

            def st_T(p):
                qs, r, nkb = pairs[p]
                pt = p % 3
                for h in range(2):
                    lb, cbk = 2 + h, 4 + h
                    if r == 0:
                        S.add("dve", lambda e, h=h, lb=lb: e.tensor_copy(Tbuf[pt][:, h, :], bank(lb)),
                              reads=[bankres[lb]], pwrites=[T_res[pt]])
                        if r < nkb - 1:
                            S.add("dve", lambda e, h=h, cbk=cbk: e.tensor_copy(carry[h], bank(cbk)),
                                  reads=[bankres[cbk]], writes=[carry_res[h]])
                    else:
                        S.add("dve", lambda e, h=h, lb=lb: e.tensor_tensor(out=Tbuf[pt][:, h, :], in0=bank(lb),
                                                                           in1=carry[h], op=ALU.subtract),
                              reads=[bankres[lb], carry_res[h]], pwrites=[T_res[pt]])
                        if r < nkb - 1:
                            S.add("dve", lambda e, h=h, cbk=cbk: e.tensor_tensor(out=carry[h], in0=bank(cbk),
                                                                               in1=carry[h], op=ALU.add),
                                  reads=[bankres[cbk]], writes=[carry_res[h]])

            def st_A(p):
                qs, r, nkb = pairs[p]
                kb = nkb - 1 - r
                pt = p % 3
                pp = p % 2
                if PAIR_A:
                    S.add("act", lambda e: e.activation(out=Abf[pp], in_=Tbuf[pt], func=AF.Exp),
                          reads=[T_res[pt]], writes=[A_res[pp]])
                else:
                    for h in range(2):
                        S.add("act", lambda e, h=h: e.activation(out=Abf[pp][:, h, :], in_=Tbuf[pt][:, h, :], func=AF.Exp),
                              reads=[T_res[pt]], writes=[A_res[pp]])
                if need_mask(qs, kb):
                    mask_pair(Abf[pp], A_res[pp], qs, kb)

            def st_AV(p):
                qs, r, nkb = pairs[p]
                kb = nkb - 1 - r
                pp = p % 2
                for h in range(2):
                    ob = 6 + h
                    S.add("pe", lambda e, h=h, ob=ob: e.matmul(bank(ob), lhsT=Vs[:, kb, :], rhs=Abf[pp][:, h, :],
                                                               start=(r == 0), stop=(r == nkb - 1)),
                          reads=[A_res[pp], att_in], writes=[bankres[ob]])
                    if r == nkb - 1:
                        yb = qs % 2
                        ps_ = slice(64 * h, 64 * h + 64)
                        S.add("dve", lambda e, ob=ob, ps_=ps_, yb=yb: e.tensor_copy(ybuf[yb][ps_, :], bank(ob)[ps_, :]),
                              reads=[bankres[ob]], writes=[ybuf_res[yb]])
                        if h == 1:
                            dst = qs // 4
                            c0 = (qs % 4) * 512
                            S.add("sp", lambda e, dst=dst, c0=c0, yb=yb: e.dma_start(out=s2_d[dst][:, c0:c0 + 512], in_=ybuf[yb]),
                                  reads=[ybuf_res[yb]], pwrites=[s2_res[dst]], kind="d")
                            if qs % 4 == 3:
                                allgather(s2_d[dst], g2_d[dst * 512:(dst + 1) * 512, :], s2_res[dst], g2_res)

            st_Z(0)
            for step in range(NP + 4):
                if step < NP:
                    st_E(step)
                if 0 <= step - 1 < NP:
                    st_LC(step - 1)
                if step + 1 < NP:
                    st_Z(step + 1)
                if 0 <= step - 3 < NP:
                    st_AV(step - 3)
                if 0 <= step - 2 < NP:
                    st_A(step - 2)
                if step < NP:
                    st_SP(step)
                if 0 <= step - 1 < NP:
                    st_T(step - 1)
            S.barrier()

        if stage >= 4:
            S.tag = 'outproj'
            ysb = f32v(OH, 8192).rearrange("p (c t) -> p c t", t=NTOK)
            ysb_res = [[Res() for _ in range(NS)] for _ in range(4)]
            S.add("sp", lambda e: e.dma_start(
                out=ysb, in_=g2_d[bass.ds(vals["v"] * 512, 512), :].rearrange("(i p) c -> p i c", p=128)),
                reads=[g2_res], writes=[r_ for i in range(4) for r_ in ysb_res[i]], kind="d")
            ysn = bfv(OA, 8192).rearrange("p (c t) -> p c t", t=NTOK)
            ysn_res = [Res() for _ in range(NS)]
            wo = bfv(OA + 4096, 8192).rearrange("p (k c) -> p k c", c=D)
            wo_res = Res()
            S.add("pool", lambda e: e.dma_start(out=wo, in_=wout_d.rearrange("(k p) c -> p k c", p=128)),
                  writes=[wo_res], kind="d")
            fm_norm(ysb, ysb_res, gsn, ysn, ysn_res, 6)
            ysb_readers = sorted(set(o for row in ysb_res for r_ in row for o in (r_.readers + r_.writers)))
            load_gbc(W["ffn2_norm"])
            lag4 = Lagged(ysb_readers)
            mmi = 0
            for tt in range(NT):
                for nh in range(2):
                    pb = 2 + (mmi % 4)
                    mmi += 1
                    for kc in range(8):
                        src, sres = (ysn, ysn_res) if kc < 4 else (ycn, ycn_res)
                        S.add("pe", lambda e, pb=pb, kc=kc, tt=tt, nh=nh, src=src: e.matmul(
                            bank(pb), lhsT=src[:, kc % 4, tt * 128:(tt + 1) * 128], rhs=wo[:, kc, nh * 512:(nh + 1) * 512],
                            start=(kc == 0), stop=(kc == 7)),
                            reads=[sres[tt // 4], wo_res], writes=[bankres[pb]])
                    S.add("dve", lambda e, pb=pb, tt=tt, nh=nh: e.tensor_tensor(
                        out=x_sb[:, tt, nh * 512:(nh + 1) * 512], in0=bank(pb), in1=x_sb[:, tt, nh * 512:(nh + 1) * 512],
                        op=ALU.add), reads=[bankres[pb]], writes=[x_res[tt]])
                if stage >= 5:
                    lag4(tt)
            lag4.flush()
            S.barrier()

        if stage >= 5:
            S.tag = 'ffn2'
            ffn("ffn2", do_make=False, next_gain=finn_d, post_tile=final_tile)
            S.barrier()

        S.tag = 'final'
        if stage < 5:
            if stage == 3:
                load_x(xs1_d)
            for tt in range(NT):
                S.add("sp", lambda e, tt=tt: e.dma_start(out=out_d[tt * 128:(tt + 1) * 128, :], in_=x_sb[:, tt, :]),
                      reads=[x_res[tt]], pwrites=[out_res], kind="d")
        S.barrier()
        S.emit(st)
    return nc


def _prep_inputs(inputs):
    f = lambda a: np.ascontiguousarray(np.asarray(a, dtype=np.float32))
    x = f(inputs["x"])
    shared = {}
    for k in ("ffn1_w_gate", "ffn1_w_up", "ffn1_w_down", "ffn2_w_gate", "ffn2_w_up", "ffn2_w_down", "w_in", "w_out",
              "conv_w"):
        shared[k] = f(inputs[k])[0]
    for k in ("ffn1_norm", "ffn2_norm", "mix_norm", "conv_b", "sb_out_norm", "conv_out_norm"):
        shared[k] = f(inputs[k]).reshape(1, -1)
    shared["final_norm"] = f(inputs["final_norm"]).reshape(1, -1)
    in_maps = []
    for c in range(8):
        b, j = c // 4, c % 4
        m = dict(shared)
        m["x"] = np.ascontiguousarray(x[b, j * NTOK:(j + 1) * NTOK, :])
        m["cid"] = np.array([[j]], dtype=np.int32)
        sel = np.zeros((128, 4), dtype=np.float32)
        if j > 0:
            sel[:, j - 1] = 1.0
        m["sel"] = sel
        in_maps.append(m)
    return in_maps


_NC_CACHE = {}


def kernel(**inputs):
    stage = 99
    if stage not in _NC_CACHE:
        _NC_CACHE[stage] = build_program(stage)
    nc = _NC_CACHE[stage]
    in_maps = _prep_inputs(inputs)
    res = run_bass_kernel_spmd(nc, in_maps, core_ids=list(range(8)))
    out = np.empty((2, SEQ, D), dtype=np.float32)
    for c in range(8):
        b, j = c // 4, c % 4
        out[b, j * NTOK:(j + 1) * NTOK, :] = res.results[c]["out"]
    return out
```

```python
import numpy as np
PAIR_SP = True
PAIR_A = True
KSKIP = set()
from contextlib import ExitStack
import concourse.bass as bass
import concourse.mybir as mybir
from concourse.bass_utils import run_bass_kernel_spmd

F32 = mybir.dt.float32
BF16 = mybir.dt.bfloat16
I32 = mybir.dt.int32
AF = mybir.ActivationFunctionType
ALU = mybir.AluOpType

D = 1024
DFF = 2816
NTOK = 2048
NT = 16
NS = 4
SEQ = 8192
EPS = 1e-6
GROUPS = [[0, 1, 2, 3], [4, 5, 6, 7]]


class Res:
    __slots__ = ("writers", "readers")

    def __init__(self):
        self.writers = []
        self.readers = []


class Sched:
    ENGS = ("pe", "act", "dve", "pool", "sp")

    def __init__(self, nc, n_dma_sems=12):
        self.nc = nc
        self.ops = []
        self.queues = {e: [] for e in self.ENGS}
        self.n_dma_sems = n_dma_sems
        self.epoch = 0
        self.pending_dma = []
        self.last = {e: None for e in self.ENGS}
        self.tag = ""

    def add(self, eng, fn, reads=(), writes=(), deps=(), kind="c", pwrites=()):
        i = len(self.ops)
        if self.tag in KSKIP:
            fn = lambda eng: eng.nop()
            kind = "c"
        dl = set()
        for d in deps:
            if d is not None:
                dl.add(d)
        for r in reads:
            dl.update(r.writers)
        for w in writes:
            dl.update(w.writers)
            dl.update(w.readers)
        for w in pwrites:
            dl.update(w.readers)
        self.ops.append([eng, fn, sorted(dl), kind, self.epoch])
        self.queues[eng].append(i)
        for r in reads:
            r.readers.append(i)
        for w in writes:
            w.writers = [i]
            w.readers = []
        for w in pwrites:
            if w.readers:
                w.writers = []
                w.readers = []
            w.writers.append(i)
        if kind != "c":
            self.pending_dma.append(i)
        else:
            self.last[eng] = i
        return i

    def barrier(self):
        deps = [v for v in self.last.values() if v is not None] + list(self.pending_dma)
        self.pending_dma = []
        for e in self.ENGS:
            self.add(e, lambda eng: eng.nop(), deps=deps)
        self.epoch += 1

    def check(self):
        done = set()
        ptr = {e: 0 for e in self.ENGS}
        total = len(self.ops)
        while len(done) < total:
            progressed = False
            for e in self.ENGS:
                q = self.queues[e]
                while ptr[e] < len(q):
                    i = q[ptr[e]]
                    if all(d in done for d in self.ops[i][2]):
                        done.add(i)
                        ptr[e] += 1
                        progressed = True
                    else:
                        break
            if not progressed:
                stuck = {e: self.queues[e][ptr[e]] for e in self.ENGS if ptr[e] < len(self.queues[e])}
                raise RuntimeError(f"schedule deadlock; stuck ops {stuck}")

    def emit(self, stack):
        nc = self.nc
        self.check()
        needed = set()
        for op in self.ops:
            needed.update(op[2])
        nep = self.epoch + 1
        csem = {(e, ep): stack.enter_context(nc.semaphore(f"c_{e}{ep}"))
                for e in self.ENGS for ep in range(nep)}
        dsem = {e: [stack.enter_context(nc.semaphore(f"d_{e}{k}")) for k in range(self.n_dma_sems)]
                for e in ("sp", "pool")}
        ncc = sum(1 for op in self.ops if op[3] == "cc")
        ccsems = [stack.enter_context(nc.semaphore(f"cc{k}")) for k in range(ncc)]
        sig = {}
        pre_wait = {}
        cccount = 0
        for e in self.ENGS:
            ccount = {}
            dcount = [0] * self.n_dma_sems
            rr = 0
            for i in self.queues[e]:
                _e, _f, _deps, kind, ep = self.ops[i]
                if kind == "d":
                    k = rr % self.n_dma_sems
                    rr += 1
                    s = dsem[e][k]
                    if dcount[k] > 0:
                        pre_wait[i] = (s, dcount[k])
                    dcount[k] += 16
                    sig[i] = (s, dcount[k])
                elif kind == "cc":
                    sig[i] = (ccsems[cccount], 1)
                    cccount += 1
                elif i in needed:
                    ccount[ep] = ccount.get(ep, 0) + 1
                    sig[i] = (csem[(e, ep)], ccount[ep])
        block = stack.enter_context(nc.Block())
        ops = self.ops
        queues = self.queues

        def run_queue(e, eng):
            waited = {}
            for i in queues[e]:
                _e, fn, deps, kind, _ep = ops[i]
                ws = []
                if i in pre_wait:
                    ws.append(pre_wait[i])
                for d in deps:
                    if e == "pe" and ops[d][0] == "pe" and ops[d][3] == "c":
                        continue
                    ws.append(sig[d])
                for (s, v) in ws:
                    key = id(s)
                    if waited.get(key, 0) >= v:
                        continue
                    waited[key] = v
                    eng.wait_ge(s, v)
                ins = fn(eng)
                if i in sig:
                    s, v = sig[i]
                    ins.then_inc(s, 16 if kind == "d" else 1)

        @block.tensor
        def _(eng):
            run_queue("pe", eng)

        @block.scalar
        def _(eng):
            run_queue("act", eng)

        @block.vector
        def _(eng):
            run_queue("dve", eng)

        @block.gpsimd
        def _(eng):
            run_queue("pool", eng)

        @block.sync
        def _(eng):
            run_queue("sp", eng)


def build_program(stage=99):
    nc = bass.Bass("TRN2", target_bir_lowering=False)

    def din(name, shape, dt=F32):
        return nc.dram_tensor(name, list(shape), dt, kind="ExternalInput").ap()

    def dint(name, shape, dt=F32):
        return nc.dram_tensor(name, list(shape), dt, kind="Internal").ap()

    x_d = din("x", [NTOK, D])
    out_d = nc.dram_tensor("out", [NTOK, D], F32, kind="ExternalOutput").ap()
    W = {}
    for f in ("ffn1", "ffn2"):
        W[f + "_norm"] = din(f + "_norm", [1, D])
        W[f + "_w_gate"] = din(f + "_w_gate", [D, DFF])
        W[f + "_w_up"] = din(f + "_w_up", [D, DFF])
        W[f + "_w_down"] = din(f + "_w_down", [DFF, D])
    mixn_d = din("mix_norm", [1, D])
    win_d = din("w_in", [D, 3072])
    convw_d = din("conv_w", [3, 512])
    convb_d = din("conv_b", [1, 512])
    sbn_d = din("sb_out_norm", [1, 512])
    cvn_d = din("conv_out_norm", [1, 512])
    wout_d = din("w_out", [D, D])
    finn_d = din("final_norm", [1, D])
    cid_d = din("cid", [1, 1], I32)
    sel_d = din("sel", [128, 4])

    xs1_d = dint("xs1", [NTOK, D])
    s1a_d = [dint(f"s1a{q}", [4 * 256, 512], BF16) for q in range(4)]
    g1a_d = [dint(f"g1a{q}", [4 * 1024, 512], BF16) for q in range(4)]
    s1v_d = [dint(f"s1v{h}", [4 * 1024, 128], BF16) for h in range(2)]
    g1v_d = [dint(f"g1v{h}", [4 * 4096, 128], BF16) for h in range(2)]
    tin_d = dint("t_in", [128, 8])
    tout_d = dint("t_out", [512, 8])
    s2_d = [dint(f"s2{q}", [128, 2048]) for q in range(4)]
    g2_d = dint("g2", [2048, 2048])

    st = ExitStack()
    with st:
        M = st.enter_context(nc.sbuf_tensor("arena", [128, 50700], F32))
        PS = st.enter_context(nc.psum_tensor("ps", [128, 4096], F32))
        cidreg = st.enter_context(nc.sync.register("cidreg"))
        S = Sched(nc)

        def f32v(off, n):
            return M[:, off:off + n]

        def bfv(off, n):
            return M[:, off:off + n // 2].bitcast(BF16)

        def bank(b):
            return PS[:, b * 512:(b + 1) * 512]

        bankres = [Res() for _ in range(8)]

        OX, OH, OA, OW, OY, OC = 0, 16400, 24592, 32784, 46096, 50192
        x_sb = f32v(OX, 16384).rearrange("p (t d) -> p t d", d=D)
        x_res = [Res() for _ in range(NT)]
        hT = bfv(OH, 16384).rearrange("p (k t) -> p k t", t=NTOK)
        hT_res = [Res() for _ in range(NT)]
        aT = bfv(OA, 6 * 2048).rearrange("p (f t) -> p f t", t=NTOK)
        wgu = [bfv(OW + i * 1024, 2048).rearrange("p (k c) -> p k c", c=256) for i in range(4)]
        wgu_res = [Res() for _ in range(4)]
        wdb = [bfv(OW + 4096 + i * 3072, 6144).rearrange("p (f c) -> p f c", c=D) for i in range(2)]
        wdb_res = [Res() for _ in range(2)]
        xn = [bfv(OW + 10240 + i * 512, 1024) for i in range(2)]
        xn_res = [Res() for _ in range(2)]
        gbc = f32v(OW + 11264, 1024)
        gbc_res = Res()
        sil = [f32v(OW + 12288 + i * 512, 512) for i in range(2)]
        sil_res = [Res() for _ in range(2)]
        ycn = bfv(OY, 8192).rearrange("p (c t) -> p c t", t=NTOK)
        ycn_res = [Res() for _ in range(NS)]
        ident = bfv(OC, 128)
        trineg = bfv(OC + 64, 128)
        onesb = bfv(OC + 128, 128)
        ssq = f32v(OC + 192, 16)
        lnv = f32v(OC + 208, 16)
        rstd = f32v(OC + 224, 16)
        epst = f32v(OC + 240, 1)
        onet = f32v(OC + 241, 1)
        cw = f32v(OC + 244, 12).rearrange("p (i c) -> p i c", c=4)
        cb = f32v(OC + 256, 4)
        gcn = f32v(OC + 260, 4)
        gsn = f32v(OC + 264, 4)
        sel = f32v(OC + 268, 4)
        tails = f32v(OC + 272, 32).rearrange("p (r e) -> p r e", e=8)
        halo = f32v(OC + 304, 8).rearrange("p (c m) -> p c m", m=2)
        const_res = Res()
        stat_res = Res()

        def mk_consts():
            S.add("pool", lambda e: e.memset(ident, 1.0), writes=[const_res])
            S.add("pool", lambda e: e.affine_select(out=ident, in_=ident, pattern=[[-1, 128]],
                                                    compare_op=ALU.is_equal, fill=0.0, base=0,
                                                    channel_multiplier=1), writes=[const_res])
            S.add("pool", lambda e: e.memset(trineg, -1.0), writes=[const_res])
            S.add("pool", lambda e: e.affine_select(out=trineg, in_=trineg, pattern=[[-1, 128]],
                                                    compare_op=ALU.is_ge, fill=0.0, base=0,
                                                    channel_multiplier=1), writes=[const_res])
            S.add("pool", lambda e: e.memset(onesb, 1.0), writes=[const_res])
            S.add("pool", lambda e: e.memset(epst, EPS), writes=[const_res])
            S.add("pool", lambda e: e.memset(onet, 1.0), writes=[const_res])
            for i in range(3):
                S.add("sp", lambda e, i=i: e.dma_start(
                    out=cw[:, i, :], in_=convw_d[i, :].rearrange("(c p) -> p c", p=128),
                    allow_slow_non_contiguous=True), pwrites=[const_res], kind="d")
            for (dst, src) in ((cb, convb_d), (gcn, cvn_d), (gsn, sbn_d)):
                S.add("sp", lambda e, dst=dst, src=src: e.dma_start(
                    out=dst, in_=src[0, :].rearrange("(c p) -> p c", p=128),
                    allow_slow_non_contiguous=True), pwrites=[const_res], kind="d")
            S.add("sp", lambda e: e.dma_start(out=sel, in_=sel_d), pwrites=[const_res], kind="d")

        def load_x(src, tts=range(NT)):
            for tt in tts:
                S.add("sp", lambda e, tt=tt: e.dma_start(out=x_sb[:, tt, :], in_=src[tt * 128:(tt + 1) * 128, :]),
                      writes=[x_res[tt]], kind="d")

        def tm_stats(junk_bufs, junk_res, tts=range(NT)):
            t0_, t1_ = tts[0], tts[-1] + 1
            for tt in tts:
                S.add("act", lambda e, tt=tt: e.activation(out=junk_bufs[tt % 2], in_=x_sb[:, tt, :], func=AF.Square,
                                                           accum_out=ssq[:, tt:tt + 1]),
                      reads=[x_res[tt]], writes=[junk_res[tt % 2], stat_res])
            S.add("act", lambda e: e.activation(out=lnv[:, t0_:t1_], in_=ssq[:, t0_:t1_], func=AF.Ln, bias=epst,
                                                scale=1.0 / D),
                  reads=[const_res], writes=[stat_res])
            S.add("act", lambda e: e.activation(out=rstd[:, t0_:t1_], in_=lnv[:, t0_:t1_], func=AF.Exp, scale=-0.5),
                  writes=[stat_res])

        def load_gbc(g_d):
            S.add("sp", lambda e: e.dma_start(out=gbc, in_=g_d.partition_broadcast(128)), writes=[gbc_res], kind="d")

        def make_hT(g_d, tts=range(NT)):
            load_gbc(g_d)
            tm_stats(xn, xn_res, tts)
            for tt in tts:
                b = tt % 2
                S.add("dve", lambda e, tt=tt, b=b: e.scalar_tensor_tensor(
                    out=xn[b], in0=x_sb[:, tt, :], scalar=rstd[:, tt:tt + 1], in1=gbc, op0=ALU.mult, op1=ALU.mult),
                    reads=[x_res[tt], stat_res, gbc_res], writes=[xn_res[b]])
                tp = bank(b).bitcast(BF16)
                for kc in range(8):
                    S.add("pe", lambda e, kc=kc, b=b, tp=tp: e.transpose(
                        out=tp[:, kc * 128:(kc + 1) * 128], in_=xn[b][:, kc * 128:(kc + 1) * 128], identity=ident),
                        reads=[xn_res[b], const_res], writes=[bankres[b]])
                S.add("act", lambda e, tt=tt, tp=tp: e.activation(
                    out=hT[:, :, tt * 128:(tt + 1) * 128], in_=tp.rearrange("p (k t) -> p k t", t=128), func=AF.Copy),
                    reads=[bankres[b]], writes=[hT_res[tt]])

        tstat = [Res() for _ in range(NT)]
        junkb = [bfv(OW + 12288, 1024), bfv(OW + 12288, 1024)]
        xn3 = [xn[0], xn[1], bfv(OW + 12288 + 512, 1024)]
        xn3_res = [xn_res[0], xn_res[1], sil_res[1]]

        def tile_stats(tt):
            j = 0
            S.add("act", lambda e: e.activation(out=junkb[j], in_=x_sb[:, tt, :], func=AF.Square,
                                                accum_out=ssq[:, tt:tt + 1]),
                  reads=[x_res[tt]], writes=[sil_res[j], tstat[tt]])
            S.add("act", lambda e: e.activation(out=lnv[:, tt:tt + 1], in_=ssq[:, tt:tt + 1], func=AF.Ln, bias=epst,
                                                scale=1.0 / D),
                  reads=[const_res], writes=[tstat[tt]])
            S.add("act", lambda e: e.activation(out=rstd[:, tt:tt + 1], in_=lnv[:, tt:tt + 1], func=AF.Exp, scale=-0.5),
                  writes=[tstat[tt]])

        def hT_p1(tt):
            tile_stats(tt)
            b3 = tt % 3
            S.add("dve", lambda e: e.scalar_tensor_tensor(
                out=xn3[b3], in0=x_sb[:, tt, :], scalar=rstd[:, tt:tt + 1], in1=gbc, op0=ALU.mult, op1=ALU.mult),
                reads=[x_res[tt], tstat[tt], gbc_res], writes=[xn3_res[b3]])

        def hT_p2(tt, extra_deps=()):
            b = tt % 2
            b3 = tt % 3
            tp = bank(b).bitcast(BF16)
            for kc in range(8):
                S.add("pe", lambda e, kc=kc: e.transpose(
                    out=tp[:, kc * 128:(kc + 1) * 128], in_=xn3[b3][:, kc * 128:(kc + 1) * 128], identity=ident),
                    reads=[xn3_res[b3], const_res], writes=[bankres[b]])
            S.add("act", lambda e: e.activation(
                out=hT[:, :, tt * 128:(tt + 1) * 128], in_=tp.rearrange("p (k t) -> p k t", t=128), func=AF.Copy),
                reads=[bankres[b]], writes=[hT_res[tt]], deps=extra_deps)

        class Lagged:
            def __init__(self, extra_deps=()):
                self.q = []
                self.extra = extra_deps

            def __call__(self, tt):
                if len(self.q) == 2:
                    hT_p2(self.q.pop(0), self.extra)
                hT_p1(tt)
                self.q.append(tt)

            def flush(self):
                while self.q:
                    hT_p2(self.q.pop(0), self.extra)

        ost = [f32v(OY + i * 1024, 1024) for i in range(2)]
        ost_res = [Res(), Res()]
        out_res = Res()

        def final_tile(tt):
            tile_stats(tt)
            b = tt % 2
            S.add("dve", lambda e: e.scalar_tensor_tensor(
                out=ost[b], in0=x_sb[:, tt, :], scalar=rstd[:, tt:tt + 1], in1=gbc, op0=ALU.mult, op1=ALU.mult),
                reads=[x_res[tt], tstat[tt], gbc_res], writes=[ost_res[b]])
            S.add("sp", lambda e: e.dma_start(out=out_d[tt * 128:(tt + 1) * 128, :], in_=ost[b]),
                  reads=[ost_res[b]], pwrites=[out_res], kind="d")

        def ffn(pref, tts=range(NT), hook=None, do_make=True, next_gain=None, post_tile=None):
            wg_d, wu_d, wd_d = W[pref + "_w_gate"], W[pref + "_w_up"], W[pref + "_w_down"]
            if do_make:
                make_hT(W[pref + "_norm"], tts)
            if next_gain is not None:
                load_gbc(next_gain)
            tss = range(tts[0] // 4, tts[-1] // 4 + 1)
            aT_res = [[Res() for _ in range(NS)] for _ in range(6)]
            quarters = [(0, 6), (6, 6), (12, 6), (18, 4)]
            blk = 0
            mmi = 0
            for qi, (f0, nf) in enumerate(quarters):
                wq = qi % 2
                S.add("pool", lambda e, wq=wq, f0=f0, nf=nf: e.dma_start(
                    out=wdb[wq][:, 0:nf, :], in_=wd_d[f0 * 128:(f0 + nf) * 128, :].rearrange("(f p) c -> p f c", p=128)),
                    writes=[wdb_res[wq]], kind="d")
                for pi in range(nf // 2):
                    c0 = (f0 + 2 * pi) * 128
                    bg, bu = (2 * blk) % 4, (2 * blk + 1) % 4
                    blk += 1
                    for (bb, wsrc) in ((bg, wg_d), (bu, wu_d)):
                        S.add("pool", lambda e, bb=bb, wsrc=wsrc, c0=c0: e.dma_start(
                            out=wgu[bb], in_=wsrc[:, c0:c0 + 256].rearrange("(k p) c -> p k c", p=128)),
                            writes=[wgu_res[bb]], kind="d")
                    if hook and (blk % 2 == 0) and (blk // 2 - 1) < len(hook) and blk >= 2:
                        hook[blk // 2 - 1]()
                    for fi in range(2):
                        fl = 2 * pi + fi
                        for ts in tss:
                            gb, ub = 2 + (mmi % 2), 4 + (mmi % 2)
                            sb_ = mmi % 2
                            mmi += 1
                            for (pb, wb) in ((gb, bg), (ub, bu)):
                                for kc in range(8):
                                    S.add("pe", lambda e, pb=pb, wb=wb, kc=kc, fi=fi, ts=ts: e.matmul(
                                        bank(pb), lhsT=wgu[wb][:, kc, fi * 128:(fi + 1) * 128],
                                        rhs=hT[:, kc, ts * 512:(ts + 1) * 512], start=(kc == 0), stop=(kc == 7)),
                                        reads=[wgu_res[wb]] + hT_res[4 * ts:4 * ts + 4], writes=[bankres[pb]])
                            S.add("act", lambda e, gb=gb, sb_=sb_: e.activation(out=sil[sb_], in_=bank(gb), func=AF.Silu),
                                  reads=[bankres[gb]], writes=[sil_res[sb_]])
                            S.add("dve", lambda e, ub=ub, sb_=sb_, fl=fl, ts=ts: e.tensor_tensor(
                                out=aT[:, fl, ts * 512:(ts + 1) * 512], in0=bank(ub), in1=sil[sb_], op=ALU.mult),
                                reads=[bankres[ub], sil_res[sb_]], writes=[aT_res[fl][ts]])
                for tt in tts:
                    for nh in range(2):
                        db = 6 + ((tt * 2 + nh) % 2)
                        for fl in range(nf):
                            S.add("pe", lambda e, db=db, fl=fl, tt=tt, nh=nh, wq=wq, nf=nf: e.matmul(
                                bank(db), lhsT=aT[:, fl, tt * 128:(tt + 1) * 128],
                                rhs=wdb[wq][:, fl, nh * 512:(nh + 1) * 512], start=(fl == 0), stop=(fl == nf - 1)),
                                reads=[aT_res[fl][tt // 4], wdb_res[wq]], writes=[bankres[db]])
                        S.add("dve", lambda e, db=db, tt=tt, nh=nh: e.scalar_tensor_tensor(
                            out=x_sb[:, tt, nh * 512:(nh + 1) * 512], in0=bank(db), scalar=0.5,
                            in1=x_sb[:, tt, nh * 512:(nh + 1) * 512], op0=ALU.mult, op1=ALU.add),
                            reads=[bankres[db]], writes=[x_res[tt]])
                    if post_tile is not None and qi == len(quarters) - 1:
                        post_tile(tt)
                if post_tile is not None and qi == len(quarters) - 1 and hasattr(post_tile, "flush"):
                    post_tile.flush()

        vals = {}

        def rl(e):
            ins = e.reg_load(cidreg, cid_d[0:1, 0:1])
            vals["v"] = e.snap(cidreg)
            return ins
        S.add("sp", rl)
        mk_consts()
        HALVES = [range(0, 8), range(8, 16)]
        load_x(x_d, HALVES[0])
        load_x(x_d, HALVES[1])

        Bsb = f32v(OX, 8192).rearrange("p (c t) -> p c t", t=NTOK)
        xc = f32v(OX + 8192, 8200).rearrange("p (c t) -> p c t", t=NTOK + 2)
        bx_res = Res()
        stq = [bfv(OY + i * 1024, 2048) for i in range(2)]
        stq_res = [Res(), Res()]
        stv = [bfv(OY + 2048 + i * 128, 256) for i in range(2)]
        stv_res = [Res(), Res()]
        s1a_res, s1v_res, tin_res = Res(), Res(), Res()
        g1a_res, g1v_res, tout_res = Res(), Res(), Res()
        xs1_res = Res()

        def allgather(src, dst, rres, wres):
            S.add("pool", lambda e: e.collective_compute("AllGather", ALU.bypass, replica_groups=GROUPS,
                                                         ins=[src], outs=[dst]),
                  reads=[rres], pwrites=[wres], kind="cc")
        pj = {"mmi": 0, "sqi": 0, "svi": 0, "wbi": 0}

        def proj_blocks(blks, tts, hook=None, hook_after=2, drain=None):
            tss = range(tts[0] // 4, tts[-1] // 4 + 1)
            for bi_, blkI in enumerate(blks):
                if drain and bi_ > 0:
                    for _ in range(4):
                        if drain:
                            drain.pop(0)()
                wb = pj["wbi"] % 4
                pj["wbi"] += 1
                S.add("pool", lambda e, wb=wb, blkI=blkI: e.dma_start(
                    out=wgu[wb], in_=win_d[:, blkI * 256:(blkI + 1) * 256].rearrange("(k p) c -> p k c", p=128)),
                    writes=[wgu_res[wb]], kind="d")
                if hook and bi_ == 3:
                    hook[0]()
                if hook and bi_ == len(blks) - 1:
                    for hk_ in hook[1:]:
                        hk_()
                kindb = blkI // 2
                if kindb == 2:
                    for tt in tts:
                        pb = 2 + (pj["mmi"] % 4)
                        pj["mmi"] += 1
                        for kc in range(8):
                            S.add("pe", lambda e, pb=pb, wb=wb, kc=kc, tt=tt: e.matmul(
                                bank(pb)[:, 0:256], lhsT=hT[:, kc, tt * 128:(tt + 1) * 128], rhs=wgu[wb][:, kc, :],
                                start=(kc == 0), stop=(kc == 7)),
                                reads=[wgu_res[wb], hT_res[tt]], writes=[bankres[pb]])
                        sv = pj["svi"] % 2
                        pj["svi"] += 1
                        S.add("act", lambda e, pb=pb, sv=sv: e.activation(out=stv[sv], in_=bank(pb)[:, 0:256], func=AF.Copy),
                              reads=[bankres[pb]], writes=[stv_res[sv]])
                        for hh in range(2):
                            hp = (blkI - 4) * 2 + hh
                            S.add("sp", lambda e, sv=sv, hh=hh, hp=hp, tt=tt: e.dma_start(
                                out=s1v_d[tt // 8][hp * 1024:(hp + 1) * 1024, :].rearrange("(p b) f -> p b f", b=8)[:, tt % 8, :],
                                in_=stv[sv][:, hh * 128:(hh + 1) * 128]),
                                reads=[stv_res[sv]], pwrites=[s1v_res], kind="d")
                    continue
                for ci in range(2):
                    chunk = (blkI % 2) * 2 + ci
                    sq = pj["sqi"] % 2
                    if kindb < 2:
                        pj["sqi"] += 1
                    for ts in tss:
                        pb = 2 + (pj["mmi"] % 4)
                        pj["mmi"] += 1
                        for kc in range(8):
                            S.add("pe", lambda e, pb=pb, wb=wb, kc=kc, ci=ci, ts=ts: e.matmul(
                                bank(pb), lhsT=wgu[wb][:, kc, ci * 128:(ci + 1) * 128],
                                rhs=hT[:, kc, ts * 512:(ts + 1) * 512], start=(kc == 0), stop=(kc == 7)),
                                reads=[wgu_res[wb]] + hT_res[4 * ts:4 * ts + 4], writes=[bankres[pb]])
                        tsl = slice(ts * 512, (ts + 1) * 512)
                        if kindb == 0:
                            S.add("act", lambda e, pb=pb, sq=sq, tsl=tsl: e.activation(
                                out=stq[sq][:, tsl], in_=bank(pb), func=AF.Copy, scale=0.125),
                                reads=[bankres[pb]], writes=[stq_res[sq]])
                        elif kindb == 1:
                            S.add("act", lambda e, pb=pb, sq=sq, tsl=tsl: e.activation(
                                out=stq[sq][:, tsl], in_=bank(pb), func=AF.Copy),
                                reads=[bankres[pb]], writes=[stq_res[sq]])
                        elif kindb == 3:
                            S.add("act", lambda e, pb=pb, chunk=chunk, tsl=tsl: e.activation(
                                out=Bsb[:, chunk, tsl], in_=bank(pb), func=AF.Copy),
                                reads=[bankres[pb], xs1_res], pwrites=[bx_res])
                        elif kindb == 4:
                            S.add("act", lambda e, pb=pb, chunk=chunk, ts=ts: e.activation(
                                out=xc[:, chunk, 2 + ts * 512: 2 + (ts + 1) * 512], in_=bank(pb), func=AF.Copy),
                                reads=[bankres[pb], xs1_res], pwrites=[bx_res])
                        else:
                            S.add("dve", lambda e, pb=pb, chunk=chunk, ts=ts: e.tensor_tensor(
                                out=xc[:, chunk, 2 + ts * 512: 2 + (ts + 1) * 512], in0=bank(pb),
                                in1=xc[:, chunk, 2 + ts * 512: 2 + (ts + 1) * 512], op=ALU.mult),
                                reads=[bankres[pb], bx_res], pwrites=[bx_res])
                    if kindb < 2:
                        r0 = chunk * 256 + kindb * 128
                        for q4 in tss:
                            S.add("sp", lambda e, sq=sq, r0=r0, q4=q4: e.dma_start(
                                out=s1a_d[q4][r0:r0 + 128, :], in_=stq[sq][:, q4 * 512:(q4 + 1) * 512]),
                                reads=[stq_res[sq]], pwrites=[s1a_res], kind="d")

        def collectives_for(hi):
            return [lambda: allgather(s1a_d[2 * hi], g1a_d[2 * hi], s1a_res, g1a_res),
                    lambda: allgather(s1v_d[hi], g1v_d[hi], s1v_res, g1v_res),
                    lambda: allgather(s1a_d[2 * hi + 1], g1a_d[2 * hi + 1], s1a_res, g1a_res)]

        pend = [None]
        deferred = []
        for hi, tts in enumerate(HALVES):
            hk = pend[0]
            pend[0] = None
            ffn("ffn1", tts, hook=hk, do_make=(hi == 0), next_gain=mixn_d,
                post_tile=Lagged() if stage >= 2 else None)
            if stage >= 2:
                S.tag = 'proj'
                for tt in tts:
                    S.add("sp", lambda e, tt=tt: e.dma_start(out=xs1_d[tt * 128:(tt + 1) * 128, :], in_=x_sb[:, tt, :]),
                          reads=[x_res[tt]], pwrites=[xs1_res], kind="d")
                if hi == 0:
                    load_gbc(W["ffn1_norm"])
                    tb_ = list(HALVES[1])
                    deferred.append(lambda: hT_p1(tb_[0]))
                    deferred.append(lambda: hT_p1(tb_[1]))
                    for k_ in range(2, len(tb_)):
                        deferred.append(lambda k_=k_: hT_p2(tb_[k_ - 2]))
                        deferred.append(lambda k_=k_: hT_p1(tb_[k_]))
                    deferred.append(lambda: hT_p2(tb_[-2]))
                    deferred.append(lambda: hT_p2(tb_[-1]))
                proj_blocks(range(0, 6), tts, drain=deferred)
                while deferred:
                    deferred.pop(0)()
                pend[0] = collectives_for(hi)
                S.tag = ''
            elif hi == 0:
                make_hT(W["ffn1_norm"], HALVES[1])

        if stage >= 2:
            S.tag = 'proj'
            proj_blocks(range(6, 12), range(NT), hook=pend[0], hook_after=2)
            S.add("sp", lambda e: e.dma_start(out=tin_d.rearrange("p (c m) -> p c m", m=2), in_=xc[:, :, NTOK:NTOK + 2]),
                  reads=[bx_res], writes=[tin_res], kind="d")
            allgather(tin_d, tout_d, tin_res, tout_res)

            S.tag = 'attload'
            KQ = bfv(OH, 16384).rearrange("p (w c) -> p w c", w=2)
            Qp = KQ[:, 0, :]
            KT = KQ[:, 1, :]
            Vs = bfv(OA, 8192).rearrange("p (b f) -> p b f", f=128)
            QA = bfv(OA + 4096, 8192)
            QB = bfv(OA + 8192, 8192)
            att_in = Res()
            kq_res = [Res() for _ in range(4)]
            ht_dead = sorted(set(o for r_ in hT_res for o in (r_.readers + r_.writers)))
            S.add("pool", lambda e: e.memset(QA[64:128, :], 0.0), pwrites=[att_in])
            S.add("pool", lambda e: e.memset(QB[0:64, :], 0.0), pwrites=[att_in])
            for i in range(4):
                cs = slice(i * 2048, (i + 1) * 2048)
                for q4 in range(4):
                    S.add("sp", lambda e, i=i, q4=q4: e.dma_start(
                        out=KQ[:, :, i * 2048 + q4 * 512: i * 2048 + (q4 + 1) * 512],
                        in_=g1a_d[q4][bass.ds(vals["v"] * 256 + i * 1024, 256), :].rearrange("(w p) c -> p w c", p=128)),
                        reads=[g1a_res], pwrites=[kq_res[i]], deps=ht_dead, kind="d")
                for h2 in range(2):
                    S.add("sp", lambda e, i=i, h2=h2: e.dma_start(
                        out=Vs[:, i * 16 + h2 * 8: i * 16 + h2 * 8 + 8, :],
                        in_=g1v_d[h2][bass.ds(vals["v"] * 1024 + i * 4096, 1024), :].rearrange("(p b) f -> p b f", b=8)),
                        reads=[g1v_res], pwrites=[att_in], kind="d")
                S.add("sp", lambda e, cs=cs: e.dma_start(out=QA[0:64, cs], in_=Qp[0:64, cs]),
                      reads=[kq_res[i]], pwrites=[att_in], kind="d")
                S.add("sp", lambda e, cs=cs: e.dma_start(out=QB[64:128, cs], in_=Qp[64:128, cs]),
                      reads=[kq_res[i]], pwrites=[att_in], kind="d")
            tails_res = Res()
            S.add("sp", lambda e: e.dma_start(out=tails, in_=tout_d.rearrange("(r p) e -> p r e", p=128)),
                  reads=[tout_res], writes=[tails_res], kind="d")

            S.tag = 'conv'
            accb = [f32v(OW + 4096 + i * 512, 512) for i in range(2)]
            acc_res = [Res(), Res()]
            sqb = [bfv(OW + 4096 + 1024 + i * 256, 512) for i in range(2)]
            sqb_res = [Res(), Res()]
            lnb = f32v(OW + 4096 + 1536, 512)
            rsb = f32v(OW + 4096 + 2048, 512)
            lnb_res, rsb_res = Res(), Res()
            stage_dmas = sorted(set(s1a_res.writers + s1v_res.writers))
            haloflat = f32v(OC + 304, 8)
            halo_res = Res()

            def halo_ops():
                S.add("dve", lambda e: e.tensor_scalar(out=haloflat, in0=tails[:, 0, :], scalar1=sel[:, 0:1], scalar2=None,
                                                       op0=ALU.mult), reads=[const_res, tails_res], writes=[stat_res])
                for r in range(1, 4):
                    S.add("dve", lambda e, r=r: e.scalar_tensor_tensor(
                        out=haloflat, in0=tails[:, r, :], scalar=sel[:, r:r + 1], in1=haloflat, op0=ALU.mult, op1=ALU.add),
                        reads=[const_res, tails_res], writes=[stat_res])
                S.add("dve", lambda e: e.tensor_copy(xc[:, :, 0:2], halo), reads=[stat_res], writes=[halo_res])
            yc_res = [[Res() for _ in range(NS)] for _ in range(4)]
            ci_ = 0
            for ts in (1, 2, 3, 0):
                if ts == 0:
                    halo_ops()
                for cc in range(4):
                    a = ci_ % 2
                    ci_ += 1
                    t0 = ts * 512
                    S.add("dve", lambda e, a=a, cc=cc, t0=t0: e.tensor_scalar(
                        out=accb[a], in0=xc[:, cc, t0 + 2:t0 + 514], scalar1=cw[:, 2, cc:cc + 1],
                        scalar2=cb[:, cc:cc + 1], op0=ALU.mult, op1=ALU.add),
                        reads=[bx_res, const_res], writes=[acc_res[a]])
                    S.add("dve", lambda e, a=a, cc=cc, t0=t0: e.scalar_tensor_tensor(
                        out=accb[a], in0=xc[:, cc, t0 + 1:t0 + 513], scalar=cw[:, 1, cc:cc + 1], in1=accb[a],
                        op0=ALU.mult, op1=ALU.add), reads=[bx_res, halo_res], writes=[acc_res[a]])
                    S.add("dve", lambda e, a=a, cc=cc, t0=t0: e.scalar_tensor_tensor(
                        out=accb[a], in0=xc[:, cc, t0:t0 + 512], scalar=cw[:, 0, cc:cc + 1], in1=accb[a],
                        op0=ALU.mult, op1=ALU.add), reads=[bx_res, halo_res], writes=[acc_res[a]])
                    S.add("pool", lambda e, a=a, cc=cc, t0=t0: e.tensor_tensor(
                        out=Bsb[:, cc, t0:t0 + 512], in0=Bsb[:, cc, t0:t0 + 512], in1=accb[a], op=ALU.mult),
                        reads=[acc_res[a], bx_res], writes=[yc_res[cc][ts]])

            def fm_norm(src, src_res, gain, dst, dst_res, nb):
                k = 0
                for ts in range(NS):
                    tsl = slice(ts * 512, (ts + 1) * 512)
                    for cc in range(4):
                        q = k % 2
                        k += 1
                        S.add("act", lambda e, q=q, cc=cc, tsl=tsl: e.activation(out=sqb[q], in_=src[:, cc, tsl], func=AF.Square),
                              reads=[src_res[cc][ts]], writes=[sqb_res[q]])
                        S.add("pe", lambda e, q=q, cc=cc: e.matmul(bank(nb), lhsT=onesb, rhs=sqb[q],
                                                                   start=(cc == 0), stop=(cc == 3)),
                              reads=[sqb_res[q], const_res], writes=[bankres[nb]])
                    S.add("act", lambda e: e.activation(out=lnb, in_=bank(nb), func=AF.Ln, bias=epst, scale=1.0 / 512),
                          reads=[bankres[nb], const_res], writes=[lnb_res])
                    S.add("act", lambda e: e.activation(out=rsb, in_=lnb, func=AF.Exp, scale=-0.5),
                          reads=[lnb_res], writes=[rsb_res])
                    for cc in range(4):
                        S.add("dve", lambda e, cc=cc, tsl=tsl: e.scalar_tensor_tensor(
                            out=dst[:, cc, tsl], in0=src[:, cc, tsl], scalar=gain[:, cc:cc + 1], in1=rsb,
                            op0=ALU.mult, op1=ALU.mult),
                            reads=[src_res[cc][ts], rsb_res, const_res], writes=[dst_res[ts]])

            S.add("dve", lambda e: e.nop(), deps=stage_dmas)
            fm_norm(Bsb, yc_res, gcn, ycn, ycn_res, 6)
            S.barrier()

        if stage >= 3:
            S.tag = 'att'
            load_x(xs1_d)
            OB = OW + 4096
            Ebuf = [f32v(OB + i * 1024, 1024).rearrange("p (h t) -> p h t", h=2) for i in range(2)]
            Tbuf = [f32v(OB + 2048 + i * 1024, 1024).rearrange("p (h t) -> p h t", h=2) for i in range(3)]
            SPb = [bfv(OB + 5120 + i * 512, 1024).rearrange("p (h t) -> p h t", h=2) for i in range(2)]
            Abf = [bfv(OB + 6144 + i * 512, 1024).rearrange("p (h t) -> p h t", h=2) for i in range(2)]
            carry = [f32v(OB + 7168 + i * 512, 512) for i in range(2)]
            ybuf = [f32v(OB + 8192 + i * 512, 512) for i in range(2)]
            E_res = [Res(), Res()]
            T_res = [Res(), Res(), Res()]
            SP_res = [Res(), Res()]
            A_res = [Res(), Res()]
            carry_res = [Res(), Res()]
            ybuf_res = [Res(), Res()]
            s2_res = [Res() for _ in range(4)]
            g2_res = Res()
            Qh = [QA, QB]
            NQS = SEQ // 512
            pairs = []
            for qs in range(NQS):
                nkb = 4 * qs + 4
                for r in range(nkb):
                    pairs.append((qs, r, nkb))
            NP = len(pairs)

            def need_mask(qs, kb):
                return kb >= 4 * qs

            def mask_pair(buf, res, qs, kb):
                base = 512 * qs - 128 * kb
                for h in range(2):
                    S.add("pool", lambda e, h=h: e.affine_select(out=buf[:, h, :], in_=buf[:, h, :], pattern=[[1, 512]],
                                                                 compare_op=ALU.is_gt, fill=0.0, base=base,
                                                                 channel_multiplier=-1),
                          writes=[res])

            def st_Z(p):
                qs, r, nkb = pairs[p]
                kb = nkb - 1 - r
                for h in range(2):
                    S.add("pe", lambda e, h=h: e.matmul(bank(h), lhsT=KT[:, kb * 128:(kb + 1) * 128],
                                                        rhs=Qh[h][:, qs * 512:(qs + 1) * 512], start=True, stop=True),
                          reads=[att_in], writes=[bankres[h]])

            def st_E(p):
                pp = p % 2
                for h in range(2):
                    S.add("act", lambda e, h=h: e.activation(out=Ebuf[pp][:, h, :], in_=bank(h), func=AF.Exp, bias=0.0, scale=1.0),
                          reads=[bankres[h]], pwrites=[E_res[pp]])

            def st_SP(p):
                qs, r, nkb = pairs[p]
                kb = nkb - 1 - r
                pp = p % 2
                if PAIR_SP:
                    S.add("act", lambda e: e.activation(out=SPb[pp], in_=Ebuf[pp], func=AF.Ln, bias=1.0),
                          reads=[E_res[pp], const_res], writes=[SP_res[pp]])
                else:
                    for h in range(2):
                        S.add("act", lambda e, h=h: e.activation(out=SPb[pp][:, h, :], in_=Ebuf[pp][:, h, :], func=AF.Ln, bias=onet),
                              reads=[E_res[pp], const_res], writes=[SP_res[pp]])
                if need_mask(qs, kb):
                    mask_pair(SPb[pp], SP_res[pp], qs, kb)

            def st_LC(p):
                qs, r, nkb = pairs[p]
                kb = nkb - 1 - r
                pp = p % 2
                for h in range(2):
                    lb, cbk = 2 + h, 4 + h
                    S.add("pe", lambda e, h=h, lb=lb: e.matmul(bank(lb), lhsT=KT[:, kb * 128:(kb + 1) * 128],
                                                               rhs=Qh[h][:, qs * 512:(qs + 1) * 512], start=True, stop=False),
                          reads=[att_in], writes=[bankres[lb]])
                    S.add("pe", lambda e, h=h, lb=lb: e.matmul(bank(lb), lhsT=trineg, rhs=SPb[pp][:, h, :],
                                                               start=False, stop=True),
                          reads=[SP_res[pp], const_res], writes=[bankres[lb]])
                    if r < nkb - 1:
                        S.add("pe", lambda e, h=h, cbk=cbk: e.matmul(bank(cbk), lhsT=onesb, rhs=SPb[pp][:, h, :],
                                                                     start=True, stop=True),
                              reads=[SP_res[pp], const_res], writes=[bankres[cbk]])

            def st_T(p):
                qs, r, nkb = pairs[p]
                pt = p % 3
                for h in range(2):
                    lb, cbk = 2 + h, 4 + h
                    if r == 0:
                        S.add("dve", lambda e, h=h, lb=lb: e.tensor_copy(Tbuf[pt][:, h, :], bank(lb)),
                              reads=[bankres[lb]], pwrites=[T_res[pt]])
                        if r < nkb - 1:
                            S.add("dve", lambda e, h=h, cbk=cbk: e.tensor_copy(carry[h], bank(cbk)),
                                  reads=[bankres[cbk]], writes=[carry_res[h]])
                    else:
                        S.add("dve", lambda e, h=h, lb=lb: e.tensor_tensor(out=Tbuf[pt][:, h, :], in0=bank(lb),
                                                                           in1=carry[h], op=ALU.subtract),
                              reads=[bankres[lb], carry_res[h]], pwrites=[T_res[pt]])
                        if r < nkb - 1:
                            S.add("dve", lambda e, h=h, cbk=cbk: e.tensor_tensor(out=carry[h], in0=bank(cbk),
                                                                               in1=carry[h], op=ALU.add),
                                  reads=[bankres[cbk]], writes=[carry_res[h]])

            def st_A(p):
                qs, r, nkb = pairs[p]
                kb = nkb - 1 - r
                pt = p % 3
                pp = p % 2
                if PAIR_A:
                    S.add("act", lambda e: e.activation(out=Abf[pp], in_=Tbuf[pt], func=AF.Exp, bias=0.0, scale=1.0),
                          reads=[T_res[pt]], writes=[A_res[pp]])
                else:
                    for h in range(2):
                        S.add("act", lambda e, h=h: e.activation(out=Abf[pp][:, h, :], in_=Tbuf[pt][:, h, :], func=AF.Exp),
                              reads=[T_res[pt]], writes=[A_res[pp]])
                if need_mask(qs, kb):
                    mask_pair(Abf[pp], A_res[pp], qs, kb)

            def st_AV(p):
                qs, r, nkb = pairs[p]
                kb = nkb - 1 - r
                pp = p % 2
                for h in range(2):
                    ob = 6 + h
                    S.add("pe", lambda e, h=h, ob=ob: e.matmul(bank(ob), lhsT=Vs[:, kb, :], rhs=Abf[pp][:, h, :],
                                                               start=(r == 0), stop=(r == nkb - 1)),
                          reads=[A_res[pp], att_in], writes=[bankres[ob]])
                    if r == nkb - 1:
                        yb = qs % 2
                        ps_ = slice(64 * h, 64 * h + 64)
                        S.add("dve", lambda e, ob=ob, ps_=ps_, yb=yb: e.tensor_copy(ybuf[yb][ps_, :], bank(ob)[ps_, :]),
                              reads=[bankres[ob]], writes=[ybuf_res[yb]])
                        if h == 1:
                            dst = qs // 4
                            c0 = (qs % 4) * 512
                            S.add("sp", lambda e, dst=dst, c0=c0, yb=yb: e.dma_start(out=s2_d[dst][:, c0:c0 + 512], in_=ybuf[yb]),
                                  reads=[ybuf_res[yb]], pwrites=[s2_res[dst]], kind="d")
                            if qs % 4 == 3:
                                allgather(s2_d[dst], g2_d[dst * 512:(dst + 1) * 512, :], s2_res[dst], g2_res)

            st_Z(0)
            for step in range(NP + 4):
                if step < NP:
                    st_E(step)
                if 0 <= step - 1 < NP:
                    st_LC(step - 1)
                if step + 1 < NP:
                    st_Z(step + 1)
                if 0 <= step - 3 < NP:
                    st_AV(step - 3)
                if 0 <= step - 2 < NP:
                    st_A(step - 2)
                if step < NP:
                    st_SP(step)
                if 0 <= step - 1 < NP:
                    st_T(step - 1)
            S.barrier()

        if stage >= 4:
            S.tag = 'outproj'
            ysb = f32v(OH, 8192).rearrange("p (c t) -> p c t", t=NTOK)
            ysb_res = [[Res() for _ in range(NS)] for _ in range(4)]
            S.add("sp", lambda e: e.dma_start(
                out=ysb, in_=g2_d[bass.ds(vals["v"] * 512, 512), :].rearrange("(i p) c -> p i c", p=128)),
                reads=[g2_res], writes=[r_ for i in range(4) for r_ in ysb_res[i]], kind="d")
            ysn = bfv(OA, 8192).rearrange("p (c t) -> p c t", t=NTOK)
            ysn_res = [Res() for _ in range(NS)]
            wo = bfv(OA + 4096, 8192).rearrange("p (k c) -> p k c", c=D)
            wo_res = Res()
            S.add("pool", lambda e: e.dma_start(out=wo, in_=wout_d.rearrange("(k p) c -> p k c", p=128)),
                  writes=[wo_res], kind="d")
            fm_norm(ysb, ysb_res, gsn, ysn, ysn_res, 6)
            ysb_readers = sorted(set(o for row in ysb_res for r_ in row for o in (r_.readers + r_.writers)))
            load_gbc(W["ffn2_norm"])
            lag4 = Lagged(ysb_readers)
            mmi = 0
            for tt in range(NT):
                for nh in range(2):
                    pb = 2 + (mmi % 4)
                    mmi += 1
                    for kc in range(8):
                        src, sres = (ysn, ysn_res) if kc < 4 else (ycn, ycn_res)
                        S.add("pe", lambda e, pb=pb, kc=kc, tt=tt, nh=nh, src=src: e.matmul(
                            bank(pb), lhsT=src[:, kc % 4, tt * 128:(tt + 1) * 128], rhs=wo[:, kc, nh * 512:(nh + 1) * 512],
                            start=(kc == 0), stop=(kc == 7)),
                            reads=[sres[tt // 4], wo_res], writes=[bankres[pb]])
                    S.add("dve", lambda e, pb=pb, tt=tt, nh=nh: e.tensor_tensor(
                        out=x_sb[:, tt, nh * 512:(nh + 1) * 512], in0=bank(pb), in1=x_sb[:, tt, nh * 512:(nh + 1) * 512],
                        op=ALU.add), reads=[bankres[pb]], writes=[x_res[tt]])
                if stage >= 5:
                    lag4(tt)
            lag4.flush()
            S.barrier()

        if stage >= 5:
            S.tag = 'ffn2'
            ffn("ffn2", do_make=False, next_gain=finn_d, post_tile=final_tile)
            S.barrier()

        S.tag = 'final'
        if stage < 5:
            if stage == 3:
                load_x(xs1_d)
            for tt in range(NT):
                S.add("sp", lambda e, tt=tt: e.dma_start(out=out_d[tt * 128:(tt + 1) * 128, :], in_=x_sb[:, tt, :]),
                      reads=[x_res[tt]], pwrites=[out_res], kind="d")
        S.barrier()
        S.emit(st)
    return nc


def _prep_inputs(inputs):
    f = lambda a: np.ascontiguousarray(np.asarray(a, dtype=np.float32))
    x = f(inputs["x"])
    shared = {}
    for k in ("ffn1_w_gate", "ffn1_w_up", "ffn1_w_down", "ffn2_w_gate", "ffn2_w_up", "ffn2_w_down", "w_in", "w_out",
              "conv_w"):
        shared[k] = f(inputs[k])[0]
    for k in ("ffn1_norm", "ffn2_norm", "mix_norm", "conv_b", "sb_out_norm", "conv_out_norm"):
        shared[k] = f(inputs[k]).reshape(1, -1)
    shared["final_norm"] = f(inputs["final_norm"]).reshape(1, -1)
    in_maps = []
    for c in range(8):
        b, j = c // 4, c % 4
        m = dict(shared)
        m["x"] = np.ascontiguousarray(x[b, j * NTOK:(j + 1) * NTOK, :])
        m["cid"] = np.array([[j]], dtype=np.int32)
        sel = np.zeros((128, 4), dtype=np.float32)
        if j > 0:
            sel[:, j - 1] = 1.0
        m["sel"] = sel
        in_maps.append(m)
    return in_maps


_NC_CACHE = {}


def kernel(**inputs):
    stage = 99
    if stage not in _NC_CACHE:
        _NC_CACHE[stage] = build_program(stage)
    nc = _NC_CACHE[stage]
    in_maps = _prep_inputs(inputs)
    res = run_bass_kernel_spmd(nc, in_maps, core_ids=list(range(8)))
    out = np.empty((2, SEQ, D), dtype=np.float32)
    for c in range(8):
        b, j = c // 4, c % 4
        out[b, j * NTOK:(j + 1) * NTOK, :] = res.results[c]["out"]
    return out
```
